# Optimizing a Trainium2 kernel written in Bass

```python
import math
import jax
import jax.numpy as jnp
from jax import lax
import numpy as np

D_MODEL = 2048
BATCH = 16
SEQ = 256
DEPTH = 2
DEC_BATCH = 2
DEC_SEQ = 2048
PAST_LEN = 512

GRID_W = 64
ROPE_BASE = 10000.0
EPS = 1e-6
Q_BLOCK = 128
N_EVEN = (DEPTH + 1) // 2
N_ODD = DEPTH // 2
D_FF = 5632
N_MOD = 9
A_HEADS = 8
A_DH = 64
A_DV = 128
B_HEADS = 4
B_DK = 128
B_DV = 256
B_RANK = 16
B_TAU = 16.0
B_CHUNK = 32
A_QK_W = A_HEADS * 2 * A_DH
A_V_W = A_HEADS * A_DV
B_QK_W = B_HEADS * B_DK
B_V_W = B_HEADS * B_DV
AB_IN = 2 * A_QK_W + A_V_W + 2 * B_QK_W + 2 * B_V_W + 2 * B_RANK
AB_OUT = A_V_W + B_V_W
AB_SPLITS = (A_QK_W,
             2 * A_QK_W,
             2 * A_QK_W + A_V_W,
             2 * A_QK_W + A_V_W + B_QK_W,
             2 * A_QK_W + A_V_W + 2 * B_QK_W,
             2 * A_QK_W + A_V_W + 2 * B_QK_W + B_V_W,
             2 * A_QK_W + A_V_W + 2 * B_QK_W + 2 * B_V_W)
C_HEADS = 16
C_KV_HEADS = 4
C_DH = 128
C_WINDOW = 128
C_BLOCK = 128
C_IN = (C_HEADS + 2 * C_KV_HEADS) * C_DH
C_OUT = C_HEADS * C_DH
C_SPLITS = (C_HEADS * C_DH, (C_HEADS + C_KV_HEADS) * C_DH)

kernel_name = 'hybrid_prefix_diffusion_trunk_step'


def rms_norm(x, g):
    xf = x.astype(jnp.float32)
    y = xf * lax.rsqrt(jnp.mean(xf * xf, axis=-1, keepdims=True) + EPS)
    return (y * g.astype(jnp.float32)).astype(x.dtype)


def adaln(cond, w, b):
    return (jax.nn.silu(cond) @ w + b).reshape(cond.shape[0], N_MOD, D_MODEL)


def modulate(h, shift, scale):
    return h * (1.0 + scale[:, None, :]) + shift[:, None, :]


def half_ffn(x, mod, i, g, w_in, w_out):
    h = modulate(rms_norm(x, g), mod[:, i], mod[:, i + 1])
    gate, up = jnp.split(h @ w_in, 2, axis=-1)
    return x + 0.5 * mod[:, i + 2][:, None, :] * ((jax.nn.silu(gate) * up) @ w_out)


def heads(t, n_heads):
    b, n, _ = t.shape
    return t.reshape(b, n, n_heads, -1).transpose(0, 2, 1, 3)


def merge_heads(t):
    b, h, n, d = t.shape
    return t.transpose(0, 2, 1, 3).reshape(b, n, h * d)


def axial_angles(n, head_dim):
    rows = n // GRID_W
    row = jnp.repeat(jnp.arange(rows, dtype=jnp.float32), GRID_W)
    col = jnp.tile(jnp.arange(GRID_W, dtype=jnp.float32), rows)
    d_axis = head_dim // 2
    inv = ROPE_BASE ** (-jnp.arange(0, d_axis, 2, dtype=jnp.float32) / d_axis)
    return row[:, None] * inv[None, :], col[:, None] * inv[None, :]


def rotate(x, ang):
    x1, x2 = jnp.split(x, 2, axis=-1)
    cos, sin = jnp.cos(ang), jnp.sin(ang)
    return jnp.concatenate([x1 * cos - x2 * sin, x1 * sin + x2 * cos], axis=-1)


def axial_rope(x, angles):
    ang_r, ang_c = angles
    xr, xc = jnp.split(x.astype(jnp.float32), 2, axis=-1)
    return jnp.concatenate([rotate(xr, ang_r), rotate(xc, ang_c)], axis=-1).astype(x.dtype)


def sweep_query_blocks(fn, *qs):
    b, h, n, _ = qs[0].shape
    nb = n // Q_BLOCK
    qb = tuple(jnp.moveaxis(q.reshape(b, h, nb, Q_BLOCK, q.shape[-1]), 2, 0) for q in qs)
    out = lax.map(lambda blk: fn(*blk), qb)
    return jnp.moveaxis(out, 0, 2).reshape(b, h, n, out.shape[-1])


def diff_lambda(lam_p, layer):
    lam_init = 0.8 - 0.6 * math.exp(-0.3 * layer)
    lp = lam_p.astype(jnp.float32)
    lam = jnp.exp(jnp.sum(lp[0] * lp[1])) - jnp.exp(jnp.sum(lp[2] * lp[3])) + lam_init
    return lam, lam_init


def diff_attention(q1, q2, k1, k2, v, lam):
    scale = A_DH ** -0.5

    def blk(a, b):
        s1 = jnp.einsum('bhqd,bhkd->bhqk', a, k1).astype(jnp.float32) * scale
        s2 = jnp.einsum('bhqd,bhkd->bhqk', b, k2).astype(jnp.float32) * scale
        p = jax.nn.softmax(s1, axis=-1) - lam * jax.nn.softmax(s2, axis=-1)
        return jnp.einsum('bhqk,bhkd->bhqd', p.astype(v.dtype), v)

    return sweep_query_blocks(blk, q1, q2)


def gla_chunk_scan(q, k, v, log_a, s0):
    b, h, n, _ = q.shape
    dv = v.shape[-1]
    nc = n // B_CHUNK

    def chunks(t):
        return jnp.moveaxis(t.astype(jnp.float32).reshape(b, h, nc, B_CHUNK, t.shape[-1]), 2, 0)

    causal = jnp.tril(jnp.ones((B_CHUNK, B_CHUNK), dtype=bool))[:, :, None]

    def step(state, inp):
        qc, kc, vc, gc = inp
        cum = jnp.cumsum(gc, axis=2)
        cum_last = cum[:, :, -1:, :]
        inter = jnp.einsum('bhtd,bhde->bhte', qc * jnp.exp(cum), state)
        rel = jnp.where(causal, cum[:, :, :, None, :] - cum[:, :, None, :, :], -jnp.inf)
        scores = jnp.einsum('bhtd,bhsd,bhtsd->bhts', qc, kc, jnp.exp(rel))
        intra = jnp.einsum('bhts,bhse->bhte', scores, vc)
        new_state = (jnp.exp(cum_last[:, :, 0, :])[..., None] * state
                     + jnp.einsum('bhsd,bhse->bhde', kc * jnp.exp(cum_last - cum), vc))
        return new_state, inter + intra

    s_fin, o = lax.scan(step, s0.astype(jnp.float32),
                        (chunks(q), chunks(k), chunks(v), chunks(log_a)))
    o = jnp.moveaxis(o, 0, 2).reshape(b, h, n, dv)
    return o.astype(v.dtype), s_fin


def gla_bidirectional(q, k, v, la_f, la_b, s_f, s_b):
    o_f, s_f_new = gla_chunk_scan(q, k, v, la_f, s_f)
    flip = lambda t: jnp.flip(t, axis=2)
    o_b, s_b_new = gla_chunk_scan(flip(q), flip(k), flip(v), flip(la_b), s_b)
    return o_f + flip(o_b), s_f_new, s_b_new


def ab_project(h, w_in, alpha_w, alpha_b):
    a_q, a_k, a_v, b_q, b_k, b_v, b_r, b_low = jnp.split(h @ w_in, AB_SPLITS, axis=-1)
    a_q = heads(a_q, A_HEADS)
    a_k = heads(a_k, A_HEADS)
    attn = (a_q[..., :A_DH], a_q[..., A_DH:], a_k[..., :A_DH], a_k[..., A_DH:], heads(a_v, A_HEADS))
    low_f, low_b = jnp.split(b_low, 2, axis=-1)
    la_f = jax.nn.log_sigmoid((low_f @ alpha_w[0] + alpha_b[0]).astype(jnp.float32)) / B_TAU
    la_b = jax.nn.log_sigmoid((low_b @ alpha_w[1] + alpha_b[1]).astype(jnp.float32)) / B_TAU
    rec = (heads(b_q, B_HEADS) * (B_DK ** -0.5), heads(b_k, B_HEADS), heads(b_v, B_HEADS), b_r,
           heads(la_f, B_HEADS), heads(la_b, B_HEADS))
    return attn, rec


def ab_merge(a_o, b_o, b_r, lam_init, subln_g, bnorm_g, w_out):
    a_out = merge_heads(rms_norm(a_o, subln_g) * (1.0 - lam_init))
    b_out = merge_heads(rms_norm(b_o, bnorm_g)) * jax.nn.silu(b_r)
    return jnp.concatenate([a_out, b_out], axis=-1) @ w_out


def ab_mixer_context(h, lam, lam_init, w_in, w_out, subln_g, alpha_w, alpha_b, bnorm_g):
    (q1, q2, k1, k2, v), (bq, bk, bv, br, la_f, la_b) = ab_project(h, w_in, alpha_w, alpha_b)
    a_o = diff_attention(q1, q2, k1, k2, v, lam)
    zero = jnp.zeros((h.shape[0], B_HEADS, B_DK, B_DV), jnp.float32)
    b_o, s_f, s_b = gla_bidirectional(bq, bk, bv, la_f, la_b, zero, zero)
    out = ab_merge(a_o, b_o, br, lam_init, subln_g, bnorm_g, w_out)
    return out, (jnp.concatenate([k1, k2], axis=-1), v, s_f, s_b)


def ab_mixer_latent(h, ctx_k, ctx_v, s_f, s_b, ang, lam, lam_init, w_in, w_out, subln_g,
                    alpha_w, alpha_b, bnorm_g):
    (q1, q2, k1, k2, v), (bq, bk, bv, br, la_f, la_b) = ab_project(h, w_in, alpha_w, alpha_b)
    q1, q2, k1, k2 = (axial_rope(q1, ang), axial_rope(q2, ang), axial_rope(k1, ang), axial_rope(k2, ang))
    ck1, ck2 = jnp.split(ctx_k, 2, axis=-1)
    a_o = diff_attention(q1, q2,
                         jnp.concatenate([ck1, k1], axis=2),
                         jnp.concatenate([ck2, k2], axis=2),
                         jnp.concatenate([ctx_v, v], axis=2), lam)
    b_o, _, _ = gla_bidirectional(bq, bk, bv, la_f, la_b, s_f, s_b)
    return ab_merge(a_o, b_o, br, lam_init, subln_g, bnorm_g, w_out)


def c_project(h, w_in):
    q, k, v = jnp.split(h @ w_in, C_SPLITS, axis=-1)
    return heads(q, C_HEADS), heads(k, C_KV_HEADS), heads(v, C_KV_HEADS)


def sink_dense_attention(q, k, v, sink):
    b, hq, _, dh = q.shape
    hkv = k.shape[1]
    g = hq // hkv
    scale = dh ** -0.5
    sink_g = sink.astype(jnp.float32).reshape(hkv, g)

    def blk(qb):
        qg = qb.reshape(b, hkv, g, Q_BLOCK, dh)
        s = jnp.einsum('bhgqd,bhkd->bhgqk', qg, k).astype(jnp.float32) * scale
        sk = jnp.broadcast_to(sink_g[None, :, :, None, None], s.shape[:-1] + (1,))
        p = jax.nn.softmax(jnp.concatenate([sk, s], axis=-1), axis=-1)[..., 1:]
        o = jnp.einsum('bhgqk,bhkd->bhgqd', p.astype(v.dtype), v)
        return o.reshape(b, hq, Q_BLOCK, dh)

    return sweep_query_blocks(blk, q)


def window_sink_attention(q, k, v, kc, vc, sink):
    b, hq, n, dh = q.shape
    hkv = k.shape[1]
    g = hq // hkv
    nb = n // C_BLOCK
    nc = kc.shape[2]
    scale = dh ** -0.5
    qg = q.reshape(b, hkv, g, nb, C_BLOCK, dh)
    pad = ((0, 0), (0, 0), (C_BLOCK, C_BLOCK), (0, 0))

    def band(t):
        return jnp.concatenate(
            [t[:, :, j * C_BLOCK: j * C_BLOCK + n].reshape(b, hkv, nb, C_BLOCK, dh) for j in range(3)],
            axis=3)

    kb, vb = band(jnp.pad(k, pad)), band(jnp.pad(v, pad))
    blk_idx = jnp.arange(nb)[:, None] * C_BLOCK
    qpos = blk_idx + jnp.arange(C_BLOCK)[None, :]
    kpos = blk_idx - C_BLOCK + jnp.arange(3 * C_BLOCK)[None, :]
    valid = ((kpos[:, None, :] >= 0) & (kpos[:, None, :] < n)
             & (jnp.abs(qpos[:, :, None] - kpos[:, None, :]) <= C_WINDOW))
    s_band = jnp.einsum('bhgnqd,bhnkd->bhgnqk', qg, kb).astype(jnp.float32) * scale
    s_band = jnp.where(valid, s_band, -jnp.inf)
    s_ctx = jnp.einsum('bhgnqd,bhkd->bhgnqk', qg, kc).astype(jnp.float32) * scale
    sink_g = sink.astype(jnp.float32).reshape(hkv, g)
    sk = jnp.broadcast_to(sink_g[None, :, :, None, None, None], s_ctx.shape[:-1] + (1,))
    p = jax.nn.softmax(jnp.concatenate([sk, s_ctx, s_band], axis=-1), axis=-1)
    p_ctx = p[..., 1:1 + nc].astype(v.dtype)
    p_band = p[..., 1 + nc:].astype(v.dtype)
    o = (jnp.einsum('bhgnqk,bhkd->bhgnqd', p_ctx, vc)
         + jnp.einsum('bhgnqk,bhnkd->bhgnqd', p_band, vb))
    return o.reshape(b, hq, n, dh)


def c_mixer_context(h, w_in, w_out, sink):
    q, k, v = c_project(h, w_in)
    o = sink_dense_attention(q, k, v, sink)
    return merge_heads(o) @ w_out, (k, v)


def c_mixer_latent(h, ctx_k, ctx_v, ang, w_in, w_out, sink):
    q, k, v = c_project(h, w_in)
    q, k = axial_rope(q, ang), axial_rope(k, ang)
    o = window_sink_attention(q, k, v, ctx_k, ctx_v, sink)
    return merge_heads(o) @ w_out


def setup_inputs(seed: int = 0) -> dict:
    key = jax.random.key(seed)
    ks = jax.random.split(key, 26)

    def nrm(k, shape, s=1.0):
        return s * jax.random.normal(k, shape, dtype=jnp.float32)

    return {
        'x_prompt': nrm(ks[0], (BATCH, SEQ, D_MODEL)),
        'x_sample': nrm(ks[1], (DEC_BATCH, DEC_SEQ, D_MODEL)),
        'cache_a_k': nrm(ks[2], (DEC_BATCH, N_EVEN, A_HEADS, PAST_LEN, 2 * A_DH)),
        'cache_a_v': nrm(ks[3], (DEC_BATCH, N_EVEN, A_HEADS, PAST_LEN, A_DV)),
        'state_b_fwd': nrm(ks[4], (DEC_BATCH, N_EVEN, B_HEADS, B_DK, B_DV), 0.5),
        'state_b_bwd': nrm(ks[5], (DEC_BATCH, N_EVEN, B_HEADS, B_DK, B_DV), 0.5),
        'cache_c_k': nrm(ks[6], (DEC_BATCH, N_ODD, C_KV_HEADS, PAST_LEN, C_DH)),
        'cache_c_v': nrm(ks[7], (DEC_BATCH, N_ODD, C_KV_HEADS, PAST_LEN, C_DH)),
        'c': nrm(ks[8], (DEC_BATCH, D_MODEL)),
        'c_ctx': nrm(ks[9], (D_MODEL,)),
        'ada_w': nrm(ks[10], (DEPTH, D_MODEL, N_MOD * D_MODEL), D_MODEL ** -0.5),
        'ada_b': nrm(ks[11], (DEPTH, N_MOD * D_MODEL), 0.02),
        'norm_g': 1.0 + nrm(ks[12], (DEPTH, 3, D_MODEL), 0.02),
        'ffn_w_in': nrm(ks[13], (DEPTH, 2, D_MODEL, 2 * D_FF), D_MODEL ** -0.5),
        'ffn_w_out': nrm(ks[14], (DEPTH, 2, D_FF, D_MODEL), D_FF ** -0.5),
        'ab_w_in': nrm(ks[15], (N_EVEN, D_MODEL, AB_IN), D_MODEL ** -0.5),
        'ab_w_out': nrm(ks[16], (N_EVEN, AB_OUT, D_MODEL), AB_OUT ** -0.5),
        'a_lambda': nrm(ks[17], (N_EVEN, 4, A_DH), 0.1),
        'a_subln_g': 1.0 + nrm(ks[18], (N_EVEN, A_DV), 0.02),
        'b_alpha_w': nrm(ks[19], (N_EVEN, 2, B_RANK, B_QK_W), B_RANK ** -0.5),
        'b_alpha_b': nrm(ks[20], (N_EVEN, 2, B_QK_W), 0.02),
        'b_norm_g': 1.0 + nrm(ks[21], (N_EVEN, B_DV), 0.02),
        'c_w_in': nrm(ks[22], (N_ODD, D_MODEL, C_IN), D_MODEL ** -0.5),
        'c_w_out': nrm(ks[23], (N_ODD, C_OUT, D_MODEL), C_OUT ** -0.5),
        'c_sink': nrm(ks[24], (N_ODD, C_HEADS), 0.5),
        'final_g': 1.0 + nrm(ks[25], (D_MODEL,), 0.02),
    }


def reference(x_prompt, x_sample, cache_a_k, cache_a_v, state_b_fwd, state_b_bwd, cache_c_k,
              cache_c_v, c, c_ctx, ada_w, ada_b, norm_g, ffn_w_in, ffn_w_out, ab_w_in, ab_w_out,
              a_lambda, a_subln_g, b_alpha_w, b_alpha_b, b_norm_g, c_w_in, c_w_out, c_sink, final_g):
    n_lat = x_sample.shape[1]
    ang_a = axial_angles(n_lat, A_DH)
    ang_c = axial_angles(n_lat, C_DH)
    xp, xs = x_prompt, x_sample
    a_k_list, a_v_list, b_f_list, b_b_list, c_k_list, c_v_list = [], [], [], [], [], []
    for l in range(DEPTH):
        mod_p = adaln(c_ctx[None, :], ada_w[l], ada_b[l])
        mod_s = adaln(c, ada_w[l], ada_b[l])
        xp = half_ffn(xp, mod_p, 0, norm_g[l, 0], ffn_w_in[l, 0], ffn_w_out[l, 0])
        xs = half_ffn(xs, mod_s, 0, norm_g[l, 0], ffn_w_in[l, 0], ffn_w_out[l, 0])
        hp = modulate(rms_norm(xp, norm_g[l, 1]), mod_p[:, 3], mod_p[:, 4])
        hs = modulate(rms_norm(xs, norm_g[l, 1]), mod_s[:, 3], mod_s[:, 4])
        if l % 2 == 0:
            e = l // 2
            lam, lam_init = diff_lambda(a_lambda[e], l)
            mix_p, (ak, av, sf, sb) = ab_mixer_context(
                hp, lam, lam_init, ab_w_in[e], ab_w_out[e], a_subln_g[e], b_alpha_w[e],
                b_alpha_b[e], b_norm_g[e])
            mix_s = ab_mixer_latent(
                hs, cache_a_k[:, e], cache_a_v[:, e], state_b_fwd[:, e], state_b_bwd[:, e], ang_a,
                lam, lam_init, ab_w_in[e], ab_w_out[e], a_subln_g[e], b_alpha_w[e], b_alpha_b[e],
                b_norm_g[e])
            a_k_list.append(ak)
            a_v_list.append(av)
            b_f_list.append(sf)
            b_b_list.append(sb)
        else:
            o = l // 2
            mix_p, (ck, cv) = c_mixer_context(hp, c_w_in[o], c_w_out[o], c_sink[o])
            mix_s = c_mixer_latent(hs, cache_c_k[:, o], cache_c_v[:, o], ang_c, c_w_in[o],
                                   c_w_out[o], c_sink[o])
            c_k_list.append(ck)
            c_v_list.append(cv)
        xp = xp + mod_p[:, 5][:, None, :] * mix_p
        xs = xs + mod_s[:, 5][:, None, :] * mix_s
        xp = half_ffn(xp, mod_p, 6, norm_g[l, 2], ffn_w_in[l, 1], ffn_w_out[l, 1])
        xs = half_ffn(xs, mod_s, 6, norm_g[l, 2], ffn_w_in[l, 1], ffn_w_out[l, 1])
    y_prompt = rms_norm(xp, final_g)
    y_sample = rms_norm(xs, final_g)
    new_a_k = jnp.stack(a_k_list, axis=1)
    new_a_v = jnp.stack(a_v_list, axis=1)
    new_b_fwd = jnp.stack(b_f_list, axis=1)
    new_b_bwd = jnp.stack(b_b_list, axis=1)
    new_c_k = jnp.stack(c_k_list, axis=1)
    new_c_v = jnp.stack(c_v_list, axis=1)
    return (y_prompt, y_sample, new_a_k, new_a_v, new_b_fwd, new_b_bwd, new_c_k, new_c_v)
```

```python
import os
import contextlib
import numpy as np
import concourse.bass as bass
import concourse.mybir as mybir
from concourse.bass_utils import run_bass_kernel_spmd

F32 = mybir.dt.float32
BF16 = mybir.dt.bfloat16
AF = mybir.ActivationFunctionType
ALU = mybir.AluOpType
AX = mybir.AxisListType

ENGS = ("pe", "act", "dve", "pool", "sp")
N_DMA_SEMS = int(os.environ.get("MK_NSEM", "32"))

D = 2048
KC = 16
NTOK = 2048
TT = 1024
NT = NTOK // TT
SUB = 512
NSUB = TT // SUB
DFF = 5632
FC = 44
FH = 22
EPS = 1e-6
NEG = -30000.0

STAGE = int(os.environ.get("MK_STAGE", "99"))


class Buf:
    __slots__ = ("w", "r", "name")

    def __init__(self, name=""):
        self.w = None
        self.r = []
        self.name = name


class Op:
    __slots__ = ("eng", "fn", "deps", "is_dma", "need_inc", "sem", "val")

    def __init__(self, eng, fn, is_dma):
        self.eng = eng
        self.fn = fn
        self.deps = []
        self.is_dma = is_dma
        self.need_inc = False
        self.sem = None
        self.val = None


class Prog:
    def __init__(self, nc):
        self.nc = nc
        self.streams = {e: [] for e in ENGS}
        self.all_bufs = []

    def buf(self, name=""):
        b = Buf(name)
        self.all_bufs.append(b)
        return b

    def op(self, eng, fn, reads=(), writes=(), dma=False):
        o = Op(eng, fn, dma)
        deps = {}
        for b in reads:
            if b.w is not None:
                deps[id(b.w)] = b.w
        for b in writes:
            if b.w is not None:
                deps[id(b.w)] = b.w
            lastr = {}
            for r in b.r:
                if r.is_dma:
                    deps[id(r)] = r
                else:
                    lastr[r.eng] = r
            for r in lastr.values():
                deps[id(r)] = r
        for d in deps.values():
            if d.eng == eng and not d.is_dma and eng == "pe":
                continue
            o.deps.append(d)
            d.need_inc = True
        for b in reads:
            b.r.append(o)
        for b in writes:
            b.w = o
            b.r = []
        self.streams[eng].append(o)
        return o

    def barrier(self):
        lasts = []
        for e in ENGS:
            s = self.streams[e]
            for o in reversed(s):
                if not o.is_dma and o.fn is not None:
                    lasts.append(o)
                    break
        pend = {}
        for b in self.all_bufs:
            if b.w is not None and b.w.is_dma:
                pend[id(b.w)] = b.w
            for r in b.r:
                if r.is_dma:
                    pend[id(r)] = r
        for o in lasts:
            pend[id(o)] = o
        for e in ENGS:
            o = Op(e, None, False)
            for d in pend.values():
                if d.eng == e and not d.is_dma:
                    continue
                o.deps.append(d)
                d.need_inc = True
            self.streams[e].append(o)
        for b in self.all_bufs:
            b.w = None
            b.r = []

    def emit(self, final_wait_ops=()):
        nc = self.nc
        with contextlib.ExitStack() as st:
            esem = {e: st.enter_context(nc.semaphore("s_" + e)) for e in ENGS}
            dsem = {e: [st.enter_context(nc.semaphore("d_%s%d" % (e, i))) for i in range(N_DMA_SEMS)]
                    for e in ("sp", "pool", "act")}
            for e in ENGS:
                cnt = 0
                dcnt = 0
                for o in self.streams[e]:
                    if o.is_dma:
                        o.sem = dsem[e][dcnt % N_DMA_SEMS]
                        o.val = 16 * (dcnt // N_DMA_SEMS + 1)
                        dcnt += 1
                    elif o.need_inc:
                        if o.fn is None:
                            raise RuntimeError("barrier op referenced")
                        cnt += 1
                        o.sem = esem[e]
                        o.val = cnt
            block = st.enter_context(nc.Block())
            engobj = {"pe": "tensor", "act": "scalar", "dve": "vector", "pool": "gpsimd", "sp": "sync"}

            def make(e):
                ops = self.streams[e]

                def body(eng):
                    waited = {}

                    def wait(sem, val):
                        k = id(sem)
                        if waited.get(k, 0) < val:
                            eng.wait_ge(sem, val)
                            waited[k] = val

                    for o in ops:
                        if o.is_dma and o.val > 16:
                            wait(o.sem, o.val - 16)
                        for d in o.deps:
                            wait(d.sem, d.val)
                        if o.fn is None:
                            continue
                        inst = o.fn(eng)
                        if o.is_dma:
                            inst.then_inc(o.sem, 16)
                        elif o.need_inc:
                            inst.then_inc(o.sem, 1)
                    if e == "sp":
                        for d in final_wait_ops:
                            wait(d.sem, d.val)
                return body

            for e in ENGS:
                getattr(block, engobj[e])(make(e))


class Arena:
    def __init__(self, nc, lo, hi):
        self.nc = nc
        self.lo = lo
        self.hi = hi
        self.cur = lo
        self.n = 0

    def alloc(self, shape, dtype, name="t"):
        nbytes = int(np.prod(shape[1:])) * (2 if dtype == BF16 else 4)
        off = (self.cur + 63) // 64 * 64
        if off + nbytes > self.hi:
            raise RuntimeError("SBUF arena overflow %s %d+%d > %d" % (name, off, nbytes, self.hi))
        self.cur = off + nbytes
        self.n += 1
        return self.nc.alloc_sbuf_tensor_at("%s_%d" % (name, self.n), list(shape), dtype, offset=off)

    def mark(self):
        return self.cur

    def reset(self, m):
        self.cur = m


LAM_INIT0 = 0.8 - 0.6 * float(np.exp(-0.3 * 0))


def build_program():
    nc = bass.Bass("TRN2", target_bir_lowering=False)
    P = Prog(nc)
    used_inputs = []

    def din(name, shape, dt=F32):
        used_inputs.append(name)
        return nc.dram_tensor(name, list(shape), dt, kind="ExternalInput").ap()

    def dout(name, shape, dt=F32):
        return nc.dram_tensor(name, list(shape), dt, kind="ExternalOutput").ap()

    def dscr(name, shape, dt=F32):
        return nc.dram_tensor(name, list(shape), dt, kind="Internal").ap()

    x_in = din("x", [NTOK, D])
    cond_T = din("cond_T", [128, KC])
    ada_w = [din("ada_w%d" % l, [36, 128, KC * 512]) for l in range(2)]
    ada_b = din("ada_b", [2, 1, 9 * D])
    normg_T = din("normg_T", [128, 2 * 3 * KC])
    ident_in = din("ident", [128, 128])
    xs = dscr("xs", [NT, KC, 128, TT])

    A = Arena(nc, 16512, 229344 - 64)
    PS = [nc.alloc_psum_tensor("ps%d" % i, [128, 512], F32) for i in range(8)]
    PSB = [P.buf("ps%d" % i) for i in range(8)]

    ident = A.alloc([128, 128], F32, "ident")
    ones_bf = A.alloc([128, 128], BF16, "ones")
    ones_f = A.alloc([1, 128], F32, "onesf")
    condT = A.alloc([128, KC], F32, "condT")
    sc_bf = A.alloc([128, KC], BF16, "scbf")
    normgT = A.alloc([128, 6 * KC], F32, "normgT")
    modT = A.alloc([128, 2 * 144], F32, "modT")
    gsT = A.alloc([128, 2 * 3 * KC], F32, "gsT")
    hgT = A.alloc([128, 2 * 3 * KC], F32, "hgT")
    B_const = P.buf("const")
    B_mod = P.buf("mod")

    P.op("sp", lambda e: e.dma_start(out=ident[:], in_=ident_in), writes=[B_const], dma=True)
    P.op("sp", lambda e: e.dma_start(out=condT[:], in_=cond_T), writes=[B_const], dma=True)
    P.op("sp", lambda e: e.dma_start(out=normgT[:], in_=normg_T), writes=[B_const], dma=True)
    P.op("dve", lambda e: e.memset(ones_bf[:], 1.0), writes=[B_const])
    P.op("dve", lambda e: e.memset(ones_f[:], 1.0), writes=[B_const])
    P.op("act", lambda e: e.activation(out=sc_bf[:], in_=condT[:], func=AF.Silu), reads=[B_const], writes=[B_const])

    def cp(eng, out, in_, reads, writes):
        if eng == "act":
            return P.op("act", lambda e: e.copy(out=out, in_=in_), reads=reads, writes=writes)
        return P.op(eng, lambda e: e.tensor_copy(out=out, in_=in_), reads=reads, writes=writes)

    m0 = A.mark()
    row = A.alloc([1, 9 * D], F32, "modrow")
    brow = A.alloc([1, 9 * D], F32, "biasrow")
    wg = [A.alloc([128, KC, 512], BF16, "adaw") for _ in range(2)]
    B_row = P.buf("row")
    B_brow = P.buf("brow")
    B_wg = [P.buf("wg0"), P.buf("wg1")]
    for l in range(2):
        P.op("sp", lambda e, l=l: e.dma_start(out=brow[:], in_=ada_b[l]), writes=[B_brow], dma=True)
        for g in range(36):
            wb = wg[g % 2]
            P.op("pool", lambda e, wb=wb, l=l, g=g: e.dma_start(
                out=wb[:].rearrange("p k n -> p (k n)"), in_=ada_w[l][g]), writes=[B_wg[g % 2]], dma=True)
            pb = g % 2
            for kc in range(KC):
                P.op("pe", lambda e, wb=wb, kc=kc, pb=pb: e.matmul(
                    PS[pb][0:1, :], lhsT=sc_bf[:, kc:kc + 1], rhs=wb[:, kc, :], start=(kc == 0), stop=(kc == KC - 1)),
                    reads=[B_wg[g % 2], B_const], writes=[PSB[pb]])
            P.op("dve", lambda e, g=g, pb=pb: e.tensor_tensor(
                out=row[0:1, g * 512:(g + 1) * 512], in0=PS[pb][0:1, :], in1=brow[0:1, g * 512:(g + 1) * 512], op=ALU.add),
                reads=[PSB[pb], B_brow], writes=[B_row])
        for j in range(144):
            P.op("pe", lambda e, j=j: e.matmul(PS[2][:, j:j + 1], lhsT=row[0:1, j * 128:(j + 1) * 128],
                                                rhs=ones_f[0:1, 0:1], start=True, stop=True),
                 reads=[B_row, B_const], writes=[PSB[2]])
        P.op("dve", lambda e, l=l: e.tensor_copy(out=modT[:, l * 144:(l + 1) * 144], in_=PS[2][:, 0:144]),
             reads=[PSB[2]], writes=[B_mod])
        for k in range(3):
            o = (l * 3 + k) * KC
            sh = l * 144 + (3 * k) * KC
            P.op("dve", lambda e, o=o, sh=sh: e.scalar_tensor_tensor(
                out=gsT[:, o:o + KC], in0=modT[:, sh + KC:sh + 2 * KC], scalar=1.0, in1=normgT[:, o:o + KC],
                op0=ALU.add, op1=ALU.mult), reads=[B_mod, B_const], writes=[B_mod])
            P.op("dve", lambda e, o=o, sh=sh, k=k: e.tensor_scalar(
                out=hgT[:, o:o + KC], in0=modT[:, sh + 2 * KC:sh + 3 * KC], scalar1=(1.0 if k == 1 else 0.5), scalar2=None,
                op0=ALU.mult), reads=[B_mod], writes=[B_mod])
    P.barrier()
    A.reset(m0)

    def shiftT(l, k):
        o = l * 144 + 3 * k * KC
        return modT[:, o:o + KC]

    def gs(l, k):
        o = (l * 3 + k) * KC
        return gsT[:, o:o + KC]

    def hg(l, k):
        o = (l * 3 + k) * KC
        return hgT[:, o:o + KC]

    seg0 = A.mark()
    xT = [A.alloc([128, TT], F32, "xT") for _ in range(KC)]
    hT = [A.alloc([128, TT], BF16, "hT") for _ in range(KC)]
    B_x = [P.buf("x%d" % i) for i in range(KC)]
    B_h = [P.buf("h%d" % i) for i in range(KC)]
    rstd = A.alloc([128, TT], F32, "rstd")
    B_rstd = P.buf("rstd")
    n_sq = [A.alloc([128, TT], BF16, "sq") for _ in range(2)]
    n_tmp = [A.alloc([128, TT], F32, "nt") for _ in range(2)]
    B_nsq = [P.buf(), P.buf()]
    B_ntmp = [P.buf(), P.buf()]
    seg_top = A.mark()
    NWI = 3
    NWO = 3
    f_actT = [A.alloc([128, TT], BF16, "actT") for _ in range(FH)]
    f_Bact = [P.buf() for _ in range(FH)]
    f_wi = [A.alloc([128, KC, 256], BF16, "wi") for _ in range(NWI)]
    f_Bwi = [P.buf() for _ in range(NWI)]
    f_wo = [A.alloc([128, FH, 128], BF16, "wo") for _ in range(NWO)]
    f_Bwo = [P.buf() for _ in range(NWO)]
    f_sg = [A.alloc([128, SUB], F32, "sg") for _ in range(2)]
    f_Bsg = [P.buf(), P.buf()]
    A.reset(seg_top)
    phase = {"last": "other", "fcnt": 0, "ocnt": 0}

    def begin_other():
        P.barrier()
        phase["last"] = "other"

    def load_x_from_input(t):
        begin_other()
        m = A.mark()
        xin = [A.alloc([128, 4, D], F32, "xin") for _ in range(2)]
        B_xin = [P.buf(), P.buf()]
        for g in range(TT // 512):
            xb = xin[g % 2]
            src = x_in[t * TT + g * 512: t * TT + (g + 1) * 512, :].rearrange("(j p) d -> p j d", p=128)
            P.op("sp", lambda e, xb=xb, src=src: e.dma_start(out=xb[:], in_=src), writes=[B_xin[g % 2]], dma=True)
            for kc in range(KC):
                pb = kc % 2
                for j in range(4):
                    P.op("pe", lambda e, xb=xb, kc=kc, j=j, pb=pb: e.transpose(
                        out=PS[pb][:, j * 128:(j + 1) * 128], in_=xb[:, j, kc * 128:(kc + 1) * 128], identity=ident[:]),
                        reads=[B_xin[g % 2], B_const], writes=[PSB[pb]])
                cp("act" if kc % 2 == 0 else "dve", xT[kc][:, g * 512:(g + 1) * 512], PS[pb][:], [PSB[pb]], [B_x[kc]])
        P.barrier()
        A.reset(m)

    def store_x(t, dst):
        for kc in range(KC):
            P.op("sp", lambda e, kc=kc: e.dma_start(out=dst[t, kc], in_=xT[kc][:]), reads=[B_x[kc]], dma=True)

    def load_x(t, src):
        for kc in range(KC):
            P.op("sp", lambda e, kc=kc: e.dma_start(out=xT[kc][:], in_=src[t, kc]), writes=[B_x[kc]], dma=True)

    def norm_mod(gsv, shv, out_f32=None):
        sq, tmp, B_sq, B_tmp = n_sq, n_tmp, B_nsq, B_ntmp
        for kc in range(KC):
            s = kc % 2
            P.op("act", lambda e, kc=kc, s=s: e.activation(out=sq[s][:], in_=xT[kc][:], func=AF.Square),
                 reads=[B_x[kc]], writes=[B_sq[s]])
            for sub in range(NSUB):
                P.op("pe", lambda e, kc=kc, s=s, sub=sub: e.matmul(
                    PS[6 + sub][:], lhsT=ones_bf[:], rhs=sq[s][:, sub * SUB:(sub + 1) * SUB], start=(kc == 0), stop=(kc == KC - 1)),
                    reads=[B_sq[s], B_const], writes=[PSB[6 + sub]])
        for sub in range(NSUB):
            P.op("act", lambda e, sub=sub: e.activation(out=tmp[0][:, sub * SUB:(sub + 1) * SUB], in_=PS[6 + sub][:], func=AF.Sqrt,
                                                         scale=1.0 / D, bias=EPS), reads=[PSB[6 + sub]], writes=[B_tmp[0]])
        P.op("dve", lambda e: e.reciprocal(out=rstd[:], in_=tmp[0][:]), reads=[B_tmp[0]], writes=[B_rstd])
        for kc in range(KC):
            s = kc % 2
            P.op("dve", lambda e, kc=kc, s=s: e.tensor_tensor(out=tmp[s][:], in0=xT[kc][:], in1=rstd[:], op=ALU.mult),
                 reads=[B_x[kc], B_rstd], writes=[B_tmp[s]])
            if out_f32 is None:
                P.op("act", lambda e, kc=kc, s=s: e.activation(out=hT[kc][:], in_=tmp[s][:], func=AF.Identity,
                                                                scale=gsv[:, kc:kc + 1], bias=shv[:, kc:kc + 1]),
                     reads=[B_tmp[s], B_mod, B_const], writes=[B_h[kc]])
            else:
                P.op("act", lambda e, kc=kc, s=s: e.activation(out=out_f32[kc][:], in_=tmp[s][:], func=AF.Copy,
                                                                scale=gsv[:, kc:kc + 1]),
                     reads=[B_tmp[s], B_mod, B_const], writes=[B_x[kc]])

    w_in_d = {}
    w_out_d = {}

    def ffn(l, i):
        if (l, i) not in w_in_d:
            w_in_d[(l, i)] = din("w_in_%d_%d" % (l, i), [FC, 128, KC * 256])
            w_out_d[(l, i)] = din("w_out_%d_%d" % (l, i), [2, KC, 128, FH * 128])
        w_in = w_in_d[(l, i)]
        w_out = w_out_d[(l, i)]
        k = 0 if i == 0 else 2
        if phase["last"] != "ffn":
            P.barrier()
        phase["last"] = "ffn"
        norm_mod(gs(l, k), shiftT(l, k))
        actT, B_act, wi, B_wi, wo, B_wo, sg, B_sg = f_actT, f_Bact, f_wi, f_Bwi, f_wo, f_Bwo, f_sg, f_Bsg
        hgv = hg(l, k)
        cnt = phase["fcnt"]
        for hf in range(2):
            for fi in range(FH):
                f = hf * FH + fi
                w = wi[f % NWI]
                bw = B_wi[f % NWI]
                P.op("pool", lambda e, w=w, f=f: e.dma_start(out=w[:].rearrange("p k n -> p (k n)"), in_=w_in[f]),
                     writes=[bw], dma=True)
                for sub in range(NSUB):
                    pg = (cnt % 2) * 2
                    pu = pg + 1
                    for kc in range(KC):
                        P.op("pe", lambda e, w=w, kc=kc, sub=sub, pg=pg: e.matmul(
                            PS[pg][:], lhsT=w[:, kc, 0:128], rhs=hT[kc][:, sub * SUB:(sub + 1) * SUB],
                            start=(kc == 0), stop=(kc == KC - 1)), reads=[bw, B_h[kc]], writes=[PSB[pg]])
                    for kc in range(KC):
                        P.op("pe", lambda e, w=w, kc=kc, sub=sub, pu=pu: e.matmul(
                            PS[pu][:], lhsT=w[:, kc, 128:256], rhs=hT[kc][:, sub * SUB:(sub + 1) * SUB],
                            start=(kc == 0), stop=(kc == KC - 1)), reads=[bw, B_h[kc]], writes=[PSB[pu]])
                    s = cnt % 2
                    P.op("act", lambda e, s=s, pg=pg: e.activation(out=sg[s][:], in_=PS[pg][:], func=AF.Silu),
                         reads=[PSB[pg]], writes=[B_sg[s]])
                    P.op("dve", lambda e, s=s, pu=pu, fi=fi, sub=sub: e.tensor_tensor(
                        out=actT[fi][:, sub * SUB:(sub + 1) * SUB], in0=sg[s][:], in1=PS[pu][:], op=ALU.mult),
                        reads=[B_sg[s], PSB[pu]], writes=[B_act[fi]])
                    cnt += 1
            for d in range(KC):
                idx = phase["ocnt"]
                phase["ocnt"] += 1
                w = wo[idx % NWO]
                bw = B_wo[idx % NWO]
                P.op("pool", lambda e, w=w, hf=hf, d=d: e.dma_start(out=w[:].rearrange("p k n -> p (k n)"), in_=w_out[hf, d]),
                     writes=[bw], dma=True)
                for sub in range(NSUB):
                    po = 4 + (idx * NSUB + sub) % 2
                    for fi in range(FH):
                        P.op("pe", lambda e, w=w, fi=fi, sub=sub, po=po: e.matmul(
                            PS[po][:], lhsT=w[:, fi, :], rhs=actT[fi][:, sub * SUB:(sub + 1) * SUB],
                            start=(fi == 0), stop=(fi == FH - 1)), reads=[bw, B_act[fi]], writes=[PSB[po]])
                    P.op("dve", lambda e, d=d, sub=sub, po=po: e.scalar_tensor_tensor(
                        out=xT[d][:, sub * SUB:(sub + 1) * SUB], in0=PS[po][:], scalar=hgv[:, d:d + 1],
                        in1=xT[d][:, sub * SUB:(sub + 1) * SUB], op0=ALU.mult, op1=ALU.add),
                        reads=[PSB[po], B_x[d], B_mod], writes=[B_x[d]])
        phase["fcnt"] = cnt

    def linear_fmaj(wsrc, nchunks, ncol, consumer, banks=(0, 1), nbuf=3, msplit=None):
        m = A.mark()
        wt = [A.alloc([128, KC, ncol], BF16, "wf") for _ in range(nbuf)]
        B_wt = [P.buf() for _ in range(nbuf)]
        cnt = 0
        for c in range(nchunks):
            w = wt[c % nbuf]
            bw = B_wt[c % nbuf]
            P.op("pool", lambda e, w=w, c=c: e.dma_start(out=w[:].rearrange("p k n -> p (k n)"), in_=wsrc(c)),
                 writes=[bw], dma=True)
            for sub in range(NSUB):
                parts = [(0, ncol)] if msplit is None else msplit
                for (m0_, m1_) in parts:
                    pb = banks[cnt % len(banks)]
                    cnt += 1
                    for kc in range(KC):
                        P.op("pe", lambda e, w=w, kc=kc, sub=sub, pb=pb, m0_=m0_, m1_=m1_: e.matmul(
                            PS[pb][0:m1_ - m0_, :], lhsT=w[:, kc, m0_:m1_], rhs=hT[kc][:, sub * SUB:(sub + 1) * SUB],
                            start=(kc == 0), stop=(kc == KC - 1)), reads=[bw, B_h[kc]], writes=[PSB[pb]])
                    consumer(c, sub, pb, m0_)
        return m

    def linear_tmaj(wsrc, ngroups, consumer, banks=(4, 5)):
        m = A.mark()
        wt = [A.alloc([128, KC, 512], BF16, "wtm") for _ in range(2)]
        B_wt = [P.buf() for _ in range(2)]
        cnt = 0
        for g in range(ngroups):
            w = wt[g % 2]
            bw = B_wt[g % 2]
            P.op("pool", lambda e, w=w, g=g: e.dma_start(out=w[:].rearrange("p k n -> p (k n)"), in_=wsrc(g)),
                 writes=[bw], dma=True)
            for tb in range(TT // 128):
                pb = banks[cnt % len(banks)]
                cnt += 1
                for kc in range(KC):
                    P.op("pe", lambda e, w=w, kc=kc, tb=tb, pb=pb: e.matmul(
                        PS[pb][:], lhsT=hT[kc][:, tb * 128:(tb + 1) * 128], rhs=w[:, kc, :],
                        start=(kc == 0), stop=(kc == KC - 1)), reads=[bw, B_h[kc]], writes=[PSB[pb]])
                consumer(g, tb, pb)
        return m

    class Rot:
        def __init__(self, shape, dt, n, name):
            self.t = [A.alloc(shape, dt, name) for _ in range(n)]
            self.b = [P.buf(name) for _ in range(n)]
            self.i = 0

        def next(self):
            k = self.i % len(self.t)
            self.i += 1
            return self.t[k], self.b[k]

    def rope_consumer(pb, tok0, cosT, sinT, permT, B_tab, dst_ap_fn, R):
        qs, bqs = R["qs"].next()
        cp("act", qs[:], PS[pb][:], [PSB[pb]], [bqs])
        pr = R["pbank"][R["pcnt"][0] % 2]
        R["pcnt"][0] += 1
        P.op("pe", lambda e: e.matmul(PS[pr][:], lhsT=permT[:], rhs=qs[:], start=True, stop=True),
             reads=[bqs, B_tab], writes=[PSB[pr]])
        t1, bt1 = R["t1"].next()
        P.op("dve", lambda e: e.tensor_tensor(out=t1[:], in0=qs[:], in1=cosT[:, tok0:tok0 + SUB], op=ALU.mult),
             reads=[bqs, B_tab], writes=[bt1])
        t2, bt2 = R["t2"].next()
        P.op("dve", lambda e: e.tensor_tensor(out=t2[:], in0=PS[pr][:], in1=sinT[:, tok0:tok0 + SUB], op=ALU.mult),
             reads=[PSB[pr], B_tab], writes=[bt2])
        ob, bob = R["ob"].next()
        P.op("dve", lambda e: e.tensor_tensor(out=ob[:], in0=t1[:], in1=t2[:], op=ALU.add),
             reads=[bt1, bt2], writes=[bob])
        dst, src = dst_ap_fn(ob)
        P.op("sp", lambda e: e.dma_start(out=dst, in_=src), reads=[bob], dma=True)

    ab_wF = din("ab_wF", [32, 128, KC * 128])
    ab_wlow = din("ab_wlow", [128, KC * 32])
    ab_wT = din("ab_wT", [7, 128, KC * 512])
    cosA_in = din("cosA", [128, NTOK])
    sinA_in = din("sinA", [128, NTOK])
    permA_in = din("permA", [128, 128])
    QA = dscr("QA", [8, 128, NTOK], BF16)
    KA = dscr("KA", [8, 128, NTOK], BF16)
    VA2 = dscr("VA2", [8, 128, 16, 128])
    BQ = dscr("BQ", [4, 128, NTOK])
    BKF = dscr("BKF", [4, 128, NTOK])
    BKT = dscr("BKT", [16, 128, 512])
    BVT = dscr("BVT", [16, 128, 1024], BF16)
    BR = dscr("BR", [8, 128, NTOK])
    LOW = dscr("LOW", [2, 16, NTOK])
    MO = dscr("MO", [16, 128, NTOK], BF16)
    o_ak = dout("o_ak", [NTOK, 1024])
    o_av = dout("o_av", [NTOK, 1024])
    out_ops = []

    def proj_L0(t):
        begin_other()
        m = A.mark()
        cosT = A.alloc([128, TT], F32, "cos")
        sinT = A.alloc([128, TT], F32, "sin")
        permT = A.alloc([128, 128], F32, "perm")
        B_tab = P.buf("tab")
        P.op("sp", lambda e: e.dma_start(out=cosT[:], in_=cosA_in[:, t * TT:(t + 1) * TT]), writes=[B_tab], dma=True)
        P.op("sp", lambda e: e.dma_start(out=sinT[:], in_=sinA_in[:, t * TT:(t + 1) * TT]), writes=[B_tab], dma=True)
        P.op("sp", lambda e: e.dma_start(out=permT[:], in_=permA_in), writes=[B_tab], dma=True)
        R = {"qs": Rot([128, SUB], F32, 2, "qs"), "t1": Rot([128, SUB], F32, 2, "t1"), "t2": Rot([128, SUB], F32, 2, "t2"),
             "ob": Rot([128, SUB], BF16, 3, "ob"), "pbank": (2, 3), "pcnt": [0]}
        stF = Rot([128, SUB], F32, 3, "stF")

        def consF(c, sub, pb, m0_):
            tok0 = sub * SUB
            g0 = t * TT + tok0
            if c < 16:
                dst = (QA if c < 8 else KA)[c % 8][:, g0:g0 + SUB]
                rope_consumer(pb, tok0, cosT, sinT, permT, B_tab, lambda ob: (dst, ob[:]), R)
            else:
                if c < 20:
                    dst = BQ[c - 16][:, g0:g0 + SUB]
                elif c < 24:
                    dst = BKF[c - 20][:, g0:g0 + SUB]
                else:
                    dst = BR[c - 24][:, g0:g0 + SUB]
                s_, bs_ = stF.next()
                if c >= 24:
                    P.op("act", lambda e: e.activation(out=s_[:], in_=PS[pb][:], func=AF.Silu), reads=[PSB[pb]], writes=[bs_])
                else:
                    cp("act", s_[:], PS[pb][:], [PSB[pb]], [bs_])
                P.op("sp", lambda e: e.dma_start(out=dst, in_=s_[:]), reads=[bs_], dma=True)

        SUBS = int(os.environ.get("MK_SUB", "9"))
        if SUBS == 5:
            stT = Rot([128, 512], F32, 3, "stT")
            MKT = 1

            def consT0(g, tb, pb):
                tg = t * (TT // 128) + tb
                r0 = tg * 128
                s_, bs_ = stT.next()
                cp("act", s_[:], PS[pb][:], [PSB[pb]], [bs_])
                dst = o_ak[r0:r0 + 128, 0:512]
                out_ops.append(P.op("sp", lambda e: e.dma_start(out=dst, in_=s_[:]), reads=[bs_], dma=True))
            linear_tmaj(lambda g: ab_wT[g], 1, consT0)
            P.barrier()
            A.reset(m)
            return
        if SUBS >= 1:
            linear_fmaj(lambda c: ab_wF[c], 16 if SUBS == 1 else 32, 128, consF)
        stL = Rot([16, SUB], F32, 2, "stL")

        def consL(c, sub, pb, m0_):
            g0 = t * TT + sub * SUB
            s_, bs_ = stL.next()
            cp("dve", s_[:], PS[pb][0:16, :], [PSB[pb]], [bs_])
            P.op("sp", lambda e: e.dma_start(out=LOW[m0_ // 16][:, g0:g0 + SUB], in_=s_[:]), reads=[bs_], dma=True)

        if SUBS >= 3:
            linear_fmaj(lambda c: ab_wlow, 1, 32, consL, msplit=[(0, 16), (16, 32)], nbuf=1)
        stT = Rot([128, 512], F32, 3, "stT")
        stTb = Rot([128, 512], BF16, 3, "stTb")

        MKT = int(os.environ.get("MK_T", "15"))

        def consT(g, tb, pb):
            tg = t * (TT // 128) + tb
            r0 = tg * 128
            if MKT == 0:
                s_, bs_ = stT.next()
                cp("act", s_[:], PS[pb][:], [PSB[pb]], [bs_])
                return
            if g < 4:
                s_, bs_ = stT.next()
                cp("act", s_[:], PS[pb][:], [PSB[pb]], [bs_])
                dst = (o_ak if g < 2 else o_av)[r0:r0 + 128, (g % 2) * 512:(g % 2 + 1) * 512]
                if MKT & 1:
                    out_ops.append(P.op("sp", lambda e: e.dma_start(out=dst, in_=s_[:]), reads=[bs_], dma=True))
                if g >= 2 and (MKT & 2):
                    P.op("sp", lambda e: e.dma_start(out=VA2[(g - 2) * 4:(g - 1) * 4, :, tg, :].rearrange("h p d -> p h d"),
                                                     in_=s_[:].rearrange("p (h d) -> p h d", d=128)), reads=[bs_], dma=True)
            elif g == 4 and not (MKT & 4):
                return
            elif g > 4 and not (MKT & 8):
                return
            elif g == 4:
                s_, bs_ = stT.next()
                cp("act", s_[:], PS[pb][:], [PSB[pb]], [bs_])
                P.op("sp", lambda e: e.dma_start(out=BKT[tg], in_=s_[:]), reads=[bs_], dma=True)
            else:
                sb_, bsb_ = stTb.next()
                cp("dve", sb_[:], PS[pb][:], [PSB[pb]], [bsb_])
                P.op("sp", lambda e: e.dma_start(out=BVT[tg][:, (g - 5) * 512:(g - 4) * 512], in_=sb_[:]), reads=[bsb_], dma=True)

        if os.environ.get("MK_BAR"):
            P.barrier()
        if SUBS >= 4:
            linear_tmaj(lambda g: ab_wT[g], int(os.environ.get("MK_NG", "7")), consT)
        P.barrier()
        A.reset(m)

    def attn_L0():
        alpha_in = din("alpha17", [2, 17, 512])
        Utri_in = din("Utri", [2, 128, 128])
        gmask_in = din("gmask", [2, 128, 128])
        keep_in = din("keep", [128, 1])
        st_in = [din("st_f", [4, 128, 256]), din("st_b", [4, 128, 256])]
        o_st = [dout("o_bf", [8, 4, 128, 256]), dout("o_bb", [8, 4, 128, 256])]
        bnormg_in = din("bnormg_T", [128, 2])
        m = A.mark()
        oB = [A.alloc([128, 2, NTOK], F32, "oB") for _ in range(4)]
        B_oB = [P.buf("oB%d" % h) for h in range(4)]
        S = [[A.alloc([128, 256], F32, "S") for _ in range(2)] for _ in range(4)]
        Sb = [[A.alloc([128, 256], BF16, "Sb") for _ in range(2)] for _ in range(4)]
        B_S = [[P.buf() for _ in range(2)] for _ in range(4)]
        B_Sb = [[P.buf() for _ in range(2)] for _ in range(4)]
        aw = A.alloc([17, 2, 512], F32, "aw")
        Ut = A.alloc([128, 2, 128], F32, "Ut")
        gm = A.alloc([128, 2, 128], F32, "gm")
        keep = A.alloc([128, 1], F32, "keep")
        B_gc = P.buf("gconst")
        for d in range(2):
            P.op("sp", lambda e, d=d: e.dma_start(out=aw[:, d, :], in_=alpha_in[d]), writes=[B_gc], dma=True)
            P.op("sp", lambda e, d=d: e.dma_start(out=Ut[:, d, :], in_=Utri_in[d]), writes=[B_gc], dma=True)
            P.op("sp", lambda e, d=d: e.dma_start(out=gm[:, d, :], in_=gmask_in[d]), writes=[B_gc], dma=True)
        P.op("sp", lambda e: e.dma_start(out=keep[:], in_=keep_in), writes=[B_gc], dma=True)
        for h in range(4):
            for d in range(2):
                P.op("sp", lambda e, h=h, d=d: e.dma_start(out=S[h][d][:], in_=st_in[d][h]), writes=[B_S[h][d]], dma=True)
                cp("dve", Sb[h][d][:], S[h][d][:], [B_S[h][d]], [B_Sb[h][d]])
        lowd = Rot([17, 128], F32, 2, "lowd")
        for k in range(2):
            P.op("dve", lambda e, k=k: e.memset(lowd.t[k][:], 1.0), writes=[lowd.b[k]])
        e1 = Rot([128, 512], F32, 2, "e1")
        lnv = Rot([128, 512], F32, 2, "lnv")
        Ep = Rot([128, 512], F32, 2, "Ep")
        Em = Rot([128, 512], F32, 2, "Em")
        EmT = Rot([128, 512], F32, 2, "EmT")
        qF = Rot([128, 4, 128], F32, 2, "qF")
        kF = Rot([128, 4, 128], F32, 2, "kF")
        kT = Rot([128, 512], F32, 2, "kT")
        vT = Rot([128, 1024], BF16, 2, "vT")
        qe = Rot([128, 512], BF16, 2, "qe")
        ke = Rot([128, 512], BF16, 2, "ke")
        keT = Rot([128, 512], BF16, 2, "keT")
        STm = Rot([128, 128], BF16, 3, "STm")
        tmpS = Rot([128, 256], F32, 3, "tmpS")
        Sout = Rot([128, 256], F32, 4, "Sout")
        pcnt = {"a": 0, "st": 0, "o": 0}
        MKA = int(os.environ.get("MK_A", "7"))
        MKG = int(os.environ.get("MK_G", "99"))
        for i in range(16 if (MKA & 1) else 0):
            for d in range(2):
                b = i if d == 0 else 15 - i
                tok = slice(b * 128, (b + 1) * 128)
                lw, blw = lowd.next()
                P.op("sp", lambda e, lw=lw, d=d, tok=tok: e.dma_start(out=lw[0:16, :], in_=LOW[d][:, tok]), writes=[blw], dma=True)
                qf, bqf = qF.next()
                P.op("sp", lambda e, qf=qf, tok=tok: e.dma_start(out=qf[:], in_=BQ[:, :, tok].rearrange("h p t -> p h t")),
                     writes=[bqf], dma=True)
                kf, bkf = kF.next()
                P.op("sp", lambda e, kf=kf, tok=tok: e.dma_start(out=kf[:], in_=BKF[:, :, tok].rearrange("h p t -> p h t")),
                     writes=[bkf], dma=True)
                kt, bkt = kT.next()
                P.op("sp", lambda e, kt=kt, b=b: e.dma_start(out=kt[:], in_=BKT[b]), writes=[bkt], dma=True)
                vt, bvt = vT.next()
                P.op("sp", lambda e, vt=vt, b=b: e.dma_start(out=vt[:], in_=BVT[b]), writes=[bvt], dma=True)
                pa = pcnt["a"] % 2
                pcnt["a"] += 1
                P.op("pe", lambda e, lw=lw, d=d, pa=pa: e.matmul(PS[pa][:], lhsT=lw[0:17, :], rhs=aw[0:17, d, :], start=True, stop=True),
                     reads=[blw, B_gc], writes=[PSB[pa]])
                x1, bx1 = e1.next()
                P.op("act", lambda e, x1=x1, pa=pa: e.activation(out=x1[:], in_=PS[pa][:], func=AF.Exp, scale=-1.0),
                     reads=[PSB[pa]], writes=[bx1])
                lv, blv = lnv.next()
                P.op("act", lambda e, x1=x1, lv=lv: e.activation(out=lv[:], in_=x1[:], func=AF.Ln, bias=1.0),
                     reads=[bx1], writes=[blv])
                P.op("pe", lambda e, lv=lv, d=d, pa=pa: e.matmul(PS[pa][:], lhsT=Ut[:, d, :], rhs=lv[:], start=True, stop=True),
                     reads=[blv, B_gc], writes=[PSB[pa]])
                emt, bemt = EmT.next()
                P.op("act", lambda e, emt=emt, pa=pa: e.activation(out=emt[:], in_=PS[pa][:], func=AF.Exp, scale=-1.0),
                     reads=[PSB[pa]], writes=[bemt])
                for h in range(4):
                    P.op("pe", lambda e, lv=lv, d=d, h=h: e.matmul(PS[2][:, h * 128:(h + 1) * 128], lhsT=lv[:, h * 128:(h + 1) * 128],
                                                                  rhs=Ut[:, d, :], start=True, stop=True),
                         reads=[blv, B_gc], writes=[PSB[2]])
                ep, bep = Ep.next()
                P.op("act", lambda e, ep=ep: e.activation(out=ep[:], in_=PS[2][:], func=AF.Exp), reads=[PSB[2]], writes=[bep])
                em, bem = Em.next()
                P.op("act", lambda e, em=em: e.activation(out=em[:], in_=PS[2][:], func=AF.Exp, scale=-1.0), reads=[PSB[2]], writes=[bem])
                q_, bq_ = qe.next()
                P.op("dve", lambda e, q_=q_, qf=qf, ep=ep: e.scalar_tensor_tensor(
                    out=q_[:], in0=qf[:].rearrange("p h t -> p (h t)"), scalar=float(128 ** -0.5), in1=ep[:], op0=ALU.mult, op1=ALU.mult),
                    reads=[bqf, bep], writes=[bq_])
                k_, bk_ = ke.next()
                P.op("dve", lambda e, k_=k_, kf=kf, em=em: e.tensor_tensor(
                    out=k_[:], in0=kf[:].rearrange("p h t -> p (h t)"), in1=em[:], op=ALU.mult), reads=[bkf, bem], writes=[bk_])
                kt_, bkt_ = keT.next()
                P.op("dve", lambda e, kt_=kt_, kt=kt, emt=emt: e.tensor_tensor(out=kt_[:], in0=kt[:], in1=emt[:], op=ALU.mult),
                     reads=[bkt, bemt], writes=[bkt_])
                for h in range(4):
                    hs = slice(h * 128, (h + 1) * 128)
                    pst = 3 + pcnt["st"] % 2
                    pcnt["st"] += 1
                    P.op("pe", lambda e, k_=k_, q_=q_, hs=hs, pst=pst: e.matmul(PS[pst][:, 0:128], lhsT=k_[:, hs], rhs=q_[:, hs], start=True, stop=True),
                         reads=[bk_, bq_], writes=[PSB[pst]])
                    sm, bsm = STm.next()
                    P.op("dve", lambda e, sm=sm, pst=pst, d=d: e.tensor_tensor(out=sm[:], in0=PS[pst][:, 0:128], in1=gm[:, d, :], op=ALU.mult),
                         reads=[PSB[pst], B_gc], writes=[bsm])
                    po = 5 + pcnt["o"] % 2
                    pcnt["o"] += 1
                    for c in range(2):
                        P.op("pe", lambda e, vt=vt, sm=sm, h=h, c=c, po=po: e.matmul(
                            PS[po][:, c * 128:(c + 1) * 128], lhsT=vt[:, h * 256 + c * 128:h * 256 + (c + 1) * 128], rhs=sm[:],
                            start=True, stop=False), reads=[bvt, bsm], writes=[PSB[po]])
                        P.op("pe", lambda e, q_=q_, h=h, d=d, c=c, hs=hs, po=po: e.matmul(
                            PS[po][:, c * 128:(c + 1) * 128], lhsT=Sb[h][d][:, c * 128:(c + 1) * 128], rhs=q_[:, hs],
                            start=False, stop=True), reads=[B_Sb[h][d], bq_], writes=[PSB[po]])
                    first = (d == 0 and b <= 7) or (d == 1 and b >= 8)
                    dsto = oB[h][:, :, tok]
                    srco = PS[po][:, 0:256].rearrange("p (c t) -> p c t", c=2)
                    if first:
                        cp("act", dsto, srco, [PSB[po]], [B_oB[h]])
                    else:
                        P.op("dve", lambda e, dsto=dsto, srco=srco: e.tensor_tensor(out=dsto, in0=srco, in1=dsto, op=ALU.add),
                             reads=[PSB[po], B_oB[h]], writes=[B_oB[h]])
                    P.op("pe", lambda e, kt_=kt_, vt=vt, h=h, hs=hs: e.matmul(PS[7][:, 0:256], lhsT=kt_[:, hs], rhs=vt[:, h * 256:(h + 1) * 256],
                                                                            start=True, stop=True), reads=[bkt_, bvt], writes=[PSB[7]])
                    ts_, bts_ = tmpS.next()
                    P.op("dve", lambda e, ts_=ts_, h=h, d=d: e.tensor_tensor(out=ts_[:], in0=PS[7][:, 0:256], in1=S[h][d][:], op=ALU.add),
                         reads=[PSB[7], B_S[h][d]], writes=[bts_])
                    col = h * 128 + (127 if d == 0 else 0)
                    al = ep[:, col:col + 1]
                    seq_end = (b % 2 == 1) if d == 0 else (b % 2 == 0)
                    if not seq_end:
                        P.op("dve", lambda e, ts_=ts_, h=h, d=d, al=al: e.tensor_scalar(out=S[h][d][:], in0=ts_[:], scalar1=al, scalar2=None, op0=ALU.mult),
                             reads=[bts_, bep], writes=[B_S[h][d]])
                        P.op("act", lambda e, ts_=ts_, h=h, d=d, al=al: e.activation(out=Sb[h][d][:], in_=ts_[:], func=AF.Copy, scale=al),
                             reads=[bts_, bep], writes=[B_Sb[h][d]])
                    else:
                        so, bso = Sout.next()
                        P.op("dve", lambda e, ts_=ts_, so=so, al=al: e.tensor_scalar(out=so[:], in0=ts_[:], scalar1=al, scalar2=None, op0=ALU.mult),
                             reads=[bts_, bep], writes=[bso])
                        out_ops.append(P.op("sp", lambda e, so=so, d=d, b=b, h=h: e.dma_start(out=o_st[d][b // 2, h], in_=so[:]), reads=[bso], dma=True))
                        P.op("dve", lambda e, so=so, h=h, d=d: e.tensor_scalar(out=S[h][d][:], in0=so[:], scalar1=keep[:, 0:1], scalar2=None, op0=ALU.mult),
                             reads=[bso, B_gc], writes=[B_S[h][d]])
                        P.op("act", lambda e, so=so, h=h, d=d: e.activation(out=Sb[h][d][:], in_=so[:], func=AF.Copy, scale=keep[:, 0:1]),
                             reads=[bso, B_gc], writes=[B_Sb[h][d]])
        bng = A.alloc([128, 2], F32, "bng")
        P.op("sp", lambda e: e.dma_start(out=bng[:], in_=bnormg_in), writes=[B_gc], dma=True)
        sqb = Rot([128, SUB], BF16, 2, "sqb")
        rt = Rot([128, SUB], F32, 2, "rt")
        rr = Rot([128, SUB], F32, 2, "rr")
        brs = Rot([128, SUB], F32, 3, "brs")
        tn = Rot([128, SUB], F32, 2, "tn")
        mob = Rot([128, SUB], BF16, 3, "mob")
        for h in range(4 if (MKA & 2) else 0):
            for sub in range(NTOK // SUB):
                ts = slice(sub * SUB, (sub + 1) * SUB)
                pb = sub % 2
                for c in range(2):
                    s_, bs_ = sqb.next()
                    P.op("act", lambda e, s_=s_, h=h, c=c, ts=ts: e.activation(out=s_[:], in_=oB[h][:, c, ts], func=AF.Square),
                         reads=[B_oB[h]], writes=[bs_])
                    P.op("pe", lambda e, s_=s_, c=c, pb=pb: e.matmul(PS[pb][:], lhsT=ones_bf[:], rhs=s_[:], start=(c == 0), stop=(c == 1)),
                         reads=[bs_, B_const], writes=[PSB[pb]])
                r1, br1 = rt.next()
                P.op("act", lambda e, r1=r1, pb=pb: e.activation(out=r1[:], in_=PS[pb][:], func=AF.Sqrt, scale=1.0 / 256, bias=EPS),
                     reads=[PSB[pb]], writes=[br1])
                r2, br2 = rr.next()
                P.op("dve", lambda e, r1=r1, r2=r2: e.reciprocal(out=r2[:], in_=r1[:]), reads=[br1], writes=[br2])
                for c in range(2):
                    br_, bbr_ = brs.next()
                    P.op("sp", lambda e, br_=br_, h=h, c=c, ts=ts: e.dma_start(out=br_[:], in_=BR[h * 2 + c][:, ts]), writes=[bbr_], dma=True)
                    t_, bt_ = tn.next()
                    P.op("dve", lambda e, t_=t_, h=h, c=c, ts=ts, r2=r2: e.tensor_tensor(out=t_[:], in0=oB[h][:, c, ts], in1=r2[:], op=ALU.mult),
                         reads=[B_oB[h], br2], writes=[bt_])
                    mo_, bmo_ = mob.next()
                    P.op("dve", lambda e, mo_=mo_, t_=t_, c=c, br_=br_: e.scalar_tensor_tensor(
                        out=mo_[:], in0=t_[:], scalar=bng[:, c:c + 1], in1=br_[:], op0=ALU.mult, op1=ALU.mult),
                        reads=[bt_, bbr_, B_gc], writes=[bmo_])
                    P.op("sp", lambda e, mo_=mo_, h=h, c=c, ts=ts: e.dma_start(out=MO[8 + h * 2 + c][:, ts], in_=mo_[:]), reads=[bmo_], dma=True)
        P.barrier()
        A.reset(m)

        ctx_ak = din("ctx_ak", [8, 512, 128])
        ctx_av = din("ctx_av", [8, 512, 128])
        biasA_in = din("biasA", [128, 20 * 8])
        lam_in = din("a_lambda", [1, 256])
        subg_in = din("subg_T", [128, 1])
        m = A.mark()
        biasA = A.alloc([128, 20, 8], F32, "biasA")
        lamr = A.alloc([1, 256], F32, "lamr")
        lamp = A.alloc([1, 128], F32, "lamp")
        lams = A.alloc([1, 8], F32, "lams")
        neglam = A.alloc([128, 1], F32, "neglam")
        sgl = A.alloc([128, 1], F32, "sgl")
        B_ac = P.buf("aconst")
        P.op("sp", lambda e: e.dma_start(out=biasA[:].rearrange("p a b -> p (a b)"), in_=biasA_in), writes=[B_ac], dma=True)
        P.op("sp", lambda e: e.dma_start(out=lamr[:], in_=lam_in), writes=[B_ac], dma=True)
        P.op("sp", lambda e: e.dma_start(out=sgl[:], in_=subg_in), writes=[B_ac], dma=True)
        P.op("dve", lambda e: e.tensor_tensor(out=lamp[0:1, 0:64], in0=lamr[0:1, 0:64], in1=lamr[0:1, 64:128], op=ALU.mult), reads=[B_ac], writes=[B_ac])
        P.op("dve", lambda e: e.tensor_tensor(out=lamp[0:1, 64:128], in0=lamr[0:1, 128:192], in1=lamr[0:1, 192:256], op=ALU.mult), reads=[B_ac], writes=[B_ac])
        P.op("dve", lambda e: e.reduce_sum(out=lams[0:1, 0:2], in_=lamp[0:1, :].rearrange("p (a b) -> p a b", a=2), axis=AX.X), reads=[B_ac], writes=[B_ac])
        P.op("act", lambda e: e.activation(out=lams[0:1, 2:4], in_=lams[0:1, 0:2], func=AF.Exp), reads=[B_ac], writes=[B_ac])
        P.op("dve", lambda e: e.scalar_tensor_tensor(out=lams[0:1, 4:5], in0=lams[0:1, 3:4], scalar=-LAM_INIT0, in1=lams[0:1, 2:3],
                                                     op0=ALU.add, op1=ALU.subtract), reads=[B_ac], writes=[B_ac])
        P.op("pe", lambda e: e.matmul(PS[7][:, 0:1], lhsT=ones_f[0:1, :], rhs=lams[0:1, 4:5], start=True, stop=True),
             reads=[B_ac, B_const], writes=[PSB[7]])
        cp("dve", neglam[:], PS[7][:, 0:1], [PSB[7]], [B_ac])
        P.op("dve", lambda e: e.tensor_scalar(out=sgl[:], in0=sgl[:], scalar1=1.0 - LAM_INIT0, scalar2=None, op0=ALU.mult), reads=[B_ac], writes=[B_ac])

        QAh = Rot([128, 8, 2, 256], BF16, 2, "QAh")
        for k_ in range(2):
            P.op("dve", lambda e, k_=k_: e.memset(QAh.t[k_][:], 0.0), writes=[QAh.b[k_]])
        KAh = Rot([128, 512 + NTOK], BF16, 2, "KAh")
        VAh = Rot([128, 20, 128], BF16, 2, "VAh")
        ckt = Rot([128, 4, 128], F32, 2, "ckt")
        Pb = Rot([128, 512], BF16, 5, "Pb")
        rden = Rot([128, 512], F32, 2, "rden")
        on = Rot([128, 512], F32, 2, "on")
        ao = Rot([128, 256], F32, 2, "ao")
        sqa = Rot([128, 256], BF16, 2, "sqa")
        r1a = Rot([128, 256], F32, 2, "r1a")
        r2a = Rot([128, 256], F32, 2, "r2a")
        ta = Rot([128, 256], F32, 2, "ta")
        oa = Rot([128, 256], BF16, 3, "oa")
        scnt = 0
        for h in range(8 if (MKA & 4) else 0):
            qh, bqh = QAh.next()
            kh, bkh = KAh.next()
            vh, bvh = VAh.next()
            ck, bck = ckt.next()
            MKL = int(os.environ.get("MK_L", "31"))
            if MKL & 1:
                P.op("sp", lambda e, qh=qh, h=h: e.dma_start(out=qh[0:64, :, 0, :], in_=QA[h][0:64, :].rearrange("p (a b) -> p a b", b=256)), writes=[bqh], dma=True)
                P.op("sp", lambda e, qh=qh, h=h: e.dma_start(out=qh[64:128, :, 1, :], in_=QA[h][64:128, :].rearrange("p (a b) -> p a b", b=256)), writes=[bqh], dma=True)
            if MKL & 2:
                P.op("sp", lambda e, kh=kh, h=h: e.dma_start(out=kh[:, 512:], in_=KA[h]), writes=[bkh], dma=True)
            if MKL & 4:
                P.op("sp", lambda e, ck=ck, h=h: e.dma_start(out=ck[:], in_=ctx_ak[h].rearrange("(c p) d -> p c d", p=128)), writes=[bck], dma=True)
                for c in range(4):
                    P.op("pe", lambda e, ck=ck, c=c: e.transpose(out=PS[7][:, c * 128:(c + 1) * 128], in_=ck[:, c, :], identity=ident[:]),
                         reads=[bck, B_const], writes=[PSB[7]])
                cp("dve", kh[:, 0:512], PS[7][:], [PSB[7]], [bkh])
            if MKL & 8:
                P.op("pool", lambda e, vh=vh, h=h: e.dma_start(out=vh[:, 0:4, :], in_=ctx_av[h].rearrange("(c p) d -> p c d", p=128)), writes=[bvh], dma=True)
            if MKL & 16:
                P.op("pool", lambda e, vh=vh, h=h: e.dma_start(out=vh[:, 4:20, :], in_=VA2[h]), writes=[bvh], dma=True)
            MKD = int(os.environ.get("MK_D", "9"))
            SB = (0, 1, 6)
            steps = [(qt, kc) for qt in range(8) for kc in range(20)]
            pend_pb = {}

            def emit_S(j, kh=kh, qh=qh, bkh=bkh, bqh=bqh):
                qt, kc = steps[j]
                ps_ = SB[j % 3]
                ks_ = slice(kc * 128, (kc + 1) * 128)
                P.op("pe", lambda e: e.matmul(PS[ps_][:], lhsT=kh[:, ks_], rhs=qh[:, qt].rearrange("p a b -> p (a b)"), start=True, stop=True),
                     reads=[bkh, bqh], writes=[PSB[ps_]])

            def emit_exp(j):
                qt, kc = steps[j]
                ps_ = SB[j % 3]
                pb_, bpb_ = Pb.next()
                P.op("act", lambda e: e.activation(out=pb_[:], in_=PS[ps_][:], func=AF.Exp, scale=0.125, bias=biasA[:, kc, qt:qt + 1]),
                     reads=[PSB[ps_], B_ac], writes=[bpb_])
                pend_pb[j] = (pb_, bpb_)

            def emit_PV(j, vh=vh, bvh=bvh):
                qt, kc = steps[j]
                po = 2 + qt % 2
                pd = 4 + qt % 2
                pb_, bpb_ = pend_pb.pop(j)
                P.op("pe", lambda e: e.matmul(PS[po][:], lhsT=vh[:, kc, :], rhs=pb_[:], start=(kc == 0), stop=(kc == 19)),
                     reads=[bvh, bpb_], writes=[PSB[po]])
                P.op("pe", lambda e: e.matmul(PS[pd][:], lhsT=ones_bf[:], rhs=pb_[:], start=(kc == 0), stop=(kc == 19)),
                     reads=[bpb_, B_const], writes=[PSB[pd]])

            nst = len(steps)
            emit_S(0)
            emit_S(1)
            for j in range(nst):
                if j + 2 < nst:
                    emit_S(j + 2)
                emit_exp(j)
                emit_PV(j)
                qt, kc = steps[j]
                if kc != 19:
                    continue
                qs_ = slice(qt * 256, (qt + 1) * 256)
                po = 2 + qt % 2
                pd = 4 + qt % 2
                if MKD < 4:
                    continue
                rd, brd = rden.next()
                P.op("dve", lambda e, rd=rd, pd=pd: e.reciprocal(out=rd[:], in_=PS[pd][:]), reads=[PSB[pd]], writes=[brd])
                on_, bon_ = on.next()
                P.op("dve", lambda e, on_=on_, po=po, rd=rd: e.tensor_tensor(out=on_[:], in0=PS[po][:], in1=rd[:], op=ALU.mult), reads=[PSB[po], brd], writes=[bon_])
                ao_, bao_ = ao.next()
                P.op("dve", lambda e, ao_=ao_, on_=on_: e.scalar_tensor_tensor(out=ao_[:], in0=on_[:, 256:512], scalar=neglam[:, 0:1], in1=on_[:, 0:256],
                                                                              op0=ALU.mult, op1=ALU.add), reads=[bon_, B_ac], writes=[bao_])
                sq_, bsq_ = sqa.next()
                P.op("act", lambda e, sq_=sq_, ao_=ao_: e.activation(out=sq_[:], in_=ao_[:], func=AF.Square), reads=[bao_], writes=[bsq_])
                P.op("pe", lambda e, sq_=sq_: e.matmul(PS[7][:, 0:256], lhsT=ones_bf[:], rhs=sq_[:], start=True, stop=True), reads=[bsq_, B_const], writes=[PSB[7]])
                r1_, br1_ = r1a.next()
                P.op("act", lambda e, r1_=r1_: e.activation(out=r1_[:], in_=PS[7][:, 0:256], func=AF.Sqrt, scale=1.0 / 128, bias=EPS), reads=[PSB[7]], writes=[br1_])
                r2_, br2_ = r2a.next()
                P.op("dve", lambda e, r1_=r1_, r2_=r2_: e.reciprocal(out=r2_[:], in_=r1_[:]), reads=[br1_], writes=[br2_])
                oa_, boa_ = oa.next()
                P.op("dve", lambda e, oa_=oa_, ao_=ao_, r2_=r2_: e.scalar_tensor_tensor(out=oa_[:], in0=ao_[:], scalar=sgl[:, 0:1], in1=r2_[:],
                                                                                       op0=ALU.mult, op1=ALU.mult), reads=[bao_, br2_, B_ac], writes=[boa_])
                P.op("sp", lambda e, oa_=oa_, h=h, qs_=qs_: e.dma_start(out=MO[h][:, qs_], in_=oa_[:]), reads=[boa_], dma=True)
        P.barrier()
        A.reset(m)


    L1d = {}

    def decl_L1():
        if L1d:
            return
        L1d["c_wF"] = din("c_wF", [20, 128, KC * 128])
        L1d["c_wT"] = din("c_wT", [2, 128, KC * 512])
        L1d["cosC"] = din("cosC", [128, NTOK])
        L1d["sinC"] = din("sinC", [128, NTOK])
        L1d["permC"] = din("permC", [128, 128])
        L1d["QC"] = dscr("QC", [4, 128, 4, NTOK], BF16)
        L1d["KCs"] = dscr("KCs", [4, 128, NTOK], BF16)
        L1d["VC"] = dscr("VC2", [4, 128, 16, 128])
        L1d["MO1"] = dscr("MO1", [16, 128, NTOK])
        L1d["o_ck"] = dout("o_ck", [NTOK, 512])
        L1d["o_cv"] = dout("o_cv", [NTOK, 512])

    def proj_L1(t):
        decl_L1()
        QC, KCs, VC = L1d["QC"], L1d["KCs"], L1d["VC"]
        begin_other()
        m = A.mark()
        cosT = A.alloc([128, TT], F32, "cos")
        sinT = A.alloc([128, TT], F32, "sin")
        permT = A.alloc([128, 128], F32, "perm")
        B_tab = P.buf("tab")
        P.op("sp", lambda e: e.dma_start(out=cosT[:], in_=L1d["cosC"][:, t * TT:(t + 1) * TT]), writes=[B_tab], dma=True)
        P.op("sp", lambda e: e.dma_start(out=sinT[:], in_=L1d["sinC"][:, t * TT:(t + 1) * TT]), writes=[B_tab], dma=True)
        P.op("sp", lambda e: e.dma_start(out=permT[:], in_=L1d["permC"]), writes=[B_tab], dma=True)
        R = {"qs": Rot([128, SUB], F32, 2, "qs"), "t1": Rot([128, SUB], F32, 2, "t1"), "t2": Rot([128, SUB], F32, 2, "t2"),
             "ob": Rot([128, SUB], BF16, 3, "ob"), "pbank": (2, 3), "pcnt": [0]}

        def consF(c, sub, pb, m0_):
            tok0 = sub * SUB
            g0 = t * TT + tok0
            if c < 16:
                dstf = lambda ob: (QC[c // 4][:, c % 4, g0:g0 + SUB], ob[:])
            else:
                dstf = lambda ob: (KCs[c - 16][:, g0:g0 + SUB], ob[:])
            rope_consumer(pb, tok0, cosT, sinT, permT, B_tab, dstf, R)

        linear_fmaj(lambda c: L1d["c_wF"][c], 20, 128, consF)
        stT = Rot([128, 512], F32, 3, "stT")
        stTb = Rot([128, 512], BF16, 3, "stTb")

        def consT(g, tb, pb):
            tg = t * (TT // 128) + tb
            r0 = tg * 128
            s_, bs_ = stT.next()
            cp("act", s_[:], PS[pb][:], [PSB[pb]], [bs_])
            dst = (L1d["o_ck"] if g == 0 else L1d["o_cv"])[r0:r0 + 128, :]
            out_ops.append(P.op("sp", lambda e: e.dma_start(out=dst, in_=s_[:]), reads=[bs_], dma=True))
            if g == 1:
                P.op("sp", lambda e: e.dma_start(out=VC[:, :, tg, :].rearrange("h p d -> p h d"),
                                                 in_=s_[:].rearrange("p (h d) -> p h d", d=128)), reads=[bs_], dma=True)

        linear_tmaj(lambda g: L1d["c_wT"][g], 2, consT)
        P.barrier()
        A.reset(m)

    def attn_L1():
        decl_L1()
        QC, KCs, VC, MO1 = L1d["QC"], L1d["KCs"], L1d["VC"], L1d["MO1"]
        ctx_ck = din("ctx_ck", [4, 512, 128])
        ctx_cv = din("ctx_cv", [4, 512, 128])
        maskC_in = din("maskC", [128, 16 * 3 * 128])
        ctxb_in = din("ctxbiasC", [128, 1])
        sink_in = din("c_sink", [1, 16])
        m = A.mark()
        maskE = A.alloc([128, 16, 3, 4, 128], BF16, "maskE")
        ctxb = A.alloc([128, 1], F32, "ctxb")
        sk1 = A.alloc([1, 32], F32, "sk1")
        sk = A.alloc([128, 16], F32, "sk")
        sinkT = A.alloc([128, 16, 128], F32, "sinkT")
        B_cc = P.buf("cconst")
        maskK = A.alloc([128, 16, 3, 128], BF16, "maskK")
        B_mk = P.buf("maskK")
        P.op("pool", lambda e: e.dma_start(out=maskK[:].rearrange("p a b c -> p (a b c)"), in_=maskC_in), writes=[B_mk], dma=True)
        for j in range(4):
            P.op("dve", lambda e, j=j: e.tensor_copy(out=maskE[:, :, :, j, :], in_=maskK[:]), reads=[B_mk], writes=[B_cc])
        P.op("sp", lambda e: e.dma_start(out=ctxb[:], in_=ctxb_in), writes=[B_cc], dma=True)
        P.op("sp", lambda e: e.dma_start(out=sk1[0:1, 0:16], in_=sink_in), writes=[B_cc], dma=True)
        P.op("act", lambda e: e.activation(out=sk1[0:1, 16:32], in_=sk1[0:1, 0:16], func=AF.Exp), reads=[B_cc], writes=[B_cc])
        P.op("pe", lambda e: e.matmul(PS[7][:, 0:16], lhsT=ones_f[0:1, :], rhs=sk1[0:1, 16:32], start=True, stop=True),
             reads=[B_cc, B_const], writes=[PSB[7]])
        cp("dve", sk[:], PS[7][:, 0:16], [PSB[7]], [B_cc])
        P.op("dve", lambda e: e.tensor_copy(out=sinkT[:], in_=sk[:].unsqueeze(2).broadcast_to([128, 16, 128])), reads=[B_cc], writes=[B_cc])
        QCg = Rot([128, 4, NTOK], BF16, 2, "QCg")
        KCg = Rot([128, 512 + NTOK], BF16, 2, "KCg")
        VCg = Rot([128, 20, 128], BF16, 2, "VCg")
        ckt = Rot([128, 4, 128], F32, 2, "ckt")
        Pb = Rot([128, 512], BF16, 5, "Pb")
        den = Rot([128, 512], F32, 2, "den")
        rd = Rot([128, 512], F32, 2, "rd")
        ob = Rot([128, 4, 128], F32, 3, "obC")
        scale = float(128 ** -0.5)
        scnt = 0
        for g in range(4):
            qg, bqg = QCg.next()
            kg, bkg = KCg.next()
            vg, bvg = VCg.next()
            ck, bck = ckt.next()
            P.op("sp", lambda e, qg=qg, g=g: e.dma_start(out=qg[:], in_=QC[g]), writes=[bqg], dma=True)
            P.op("sp", lambda e, kg=kg, g=g: e.dma_start(out=kg[:, 512:], in_=KCs[g]), writes=[bkg], dma=True)
            P.op("sp", lambda e, ck=ck, g=g: e.dma_start(out=ck[:], in_=ctx_ck[g].rearrange("(c p) d -> p c d", p=128)), writes=[bck], dma=True)
            P.op("pool", lambda e, vg=vg, g=g: e.dma_start(out=vg[:, 0:4, :], in_=ctx_cv[g].rearrange("(c p) d -> p c d", p=128)), writes=[bvg], dma=True)
            P.op("pool", lambda e, vg=vg, g=g: e.dma_start(out=vg[:, 4:20, :], in_=VC[g]), writes=[bvg], dma=True)
            for c in range(4):
                P.op("pe", lambda e, ck=ck, c=c: e.transpose(out=PS[7][:, c * 128:(c + 1) * 128], in_=ck[:, c, :], identity=ident[:]),
                     reads=[bck, B_const], writes=[PSB[7]])
            cp("dve", kg[:, 0:512], PS[7][:], [PSB[7]], [bkg])
            SB = (0, 1, 6)
            steps = []
            for qb in range(16):
                chunks = [(c, None) for c in range(4)] + [(4 + kb, kb - qb + 1) for kb in (qb - 1, qb, qb + 1) if 0 <= kb < 16]
                for idx, (kc, mi) in enumerate(chunks):
                    steps.append((qb, kc, mi, idx, len(chunks)))
            pend_pb = {}

            def emit_S(j, kg=kg, qg=qg, bkg=bkg, bqg=bqg):
                qb, kc, mi, idx, n = steps[j]
                ps_ = SB[j % 3]
                P.op("pe", lambda e: e.matmul(PS[ps_][:], lhsT=kg[:, kc * 128:(kc + 1) * 128], rhs=qg[:, :, qb * 128:(qb + 1) * 128], start=True, stop=True),
                     reads=[bkg, bqg], writes=[PSB[ps_]])

            def emit_exp(j):
                qb, kc, mi, idx, n = steps[j]
                ps_ = SB[j % 3]
                pb_, bpb_ = Pb.next()
                if kc < 4:
                    P.op("act", lambda e: e.activation(out=pb_[:], in_=PS[ps_][:], func=AF.Exp, scale=scale, bias=ctxb[:, 0:1]),
                         reads=[PSB[ps_], B_cc], writes=[bpb_])
                else:
                    P.op("act", lambda e: e.activation(out=pb_[:], in_=PS[ps_][:], func=AF.Exp, scale=scale),
                         reads=[PSB[ps_]], writes=[bpb_])
                    P.op("dve", lambda e: e.tensor_tensor(out=pb_[:], in0=pb_[:], in1=maskE[:, qb, mi].rearrange("p j q -> p (j q)"), op=ALU.mult),
                         reads=[bpb_, B_cc], writes=[bpb_])
                pend_pb[j] = (pb_, bpb_)

            def emit_PV(j, vg=vg, bvg=bvg):
                qb, kc, mi, idx, n = steps[j]
                po = 2 + qb % 2
                pd = 4 + qb % 2
                pb_, bpb_ = pend_pb.pop(j)
                P.op("pe", lambda e: e.matmul(PS[po][:], lhsT=vg[:, kc, :], rhs=pb_[:], start=(idx == 0), stop=(idx == n - 1)),
                     reads=[bvg, bpb_], writes=[PSB[po]])
                P.op("pe", lambda e: e.matmul(PS[pd][:], lhsT=ones_bf[:], rhs=pb_[:], start=(idx == 0), stop=(idx == n - 1)),
                     reads=[bpb_, B_const], writes=[PSB[pd]])

            nst = len(steps)
            emit_S(0)
            emit_S(1)
            for j in range(nst):
                if j + 2 < nst:
                    emit_S(j + 2)
                emit_exp(j)
                emit_PV(j)
                qb, kc, mi, idx, n = steps[j]
                if idx != n - 1:
                    continue
                po = 2 + qb % 2
                pd = 4 + qb % 2
                dn, bdn = den.next()
                P.op("dve", lambda e, dn=dn, pd=pd, g=g: e.tensor_tensor(out=dn[:], in0=PS[pd][:], in1=sinkT[:, g * 4:(g + 1) * 4, :].rearrange("p j q -> p (j q)"), op=ALU.add),
                     reads=[PSB[pd], B_cc], writes=[bdn])
                r_, br_ = rd.next()
                P.op("dve", lambda e, r_=r_, dn=dn: e.reciprocal(out=r_[:], in_=dn[:]), reads=[bdn], writes=[br_])
                o_, bo_ = ob.next()
                P.op("dve", lambda e, o_=o_, po=po, r_=r_: e.tensor_tensor(out=o_[:].rearrange("p j q -> p (j q)"), in0=PS[po][:], in1=r_[:], op=ALU.mult),
                     reads=[PSB[po], br_], writes=[bo_])
                P.op("sp", lambda e, o_=o_, g=g, qb=qb: e.dma_start(
                    out=MO1[g * 4:(g + 1) * 4, :, qb * 128:(qb + 1) * 128].rearrange("j p q -> p j q"), in_=o_[:]), reads=[bo_], dma=True)
        P.barrier()
        A.reset(m)

    def final_out(t):
        if "finalg_in" not in L1d:
            L1d["finalg_in"] = din("finalg_T", [128, KC])
            L1d["y_out"] = dout("y", [NTOK, D])
        finalg_T = L1d["finalg_in"]
        y_out = L1d["y_out"]
        begin_other()
        m = A.mark()
        fg = A.alloc([128, KC], F32, "fg")
        B_fg = P.buf("fg")
        P.op("sp", lambda e: e.dma_start(out=fg[:], in_=finalg_T), writes=[B_mod], dma=True)
        norm_mod(fg, None, out_f32=xT)
        yst = Rot([128, D], F32, 2, "yst")
        cnt = 0
        for tb in range(TT // 128):
            ys, bys = yst.next()
            for k4 in range(4):
                pb = cnt % 2
                cnt += 1
                for kk in range(4):
                    kc = k4 * 4 + kk
                    P.op("pe", lambda e, kc=kc, kk=kk, tb=tb, pb=pb: e.transpose(
                        out=PS[pb][:, kk * 128:(kk + 1) * 128], in_=xT[kc][:, tb * 128:(tb + 1) * 128], identity=ident[:]),
                        reads=[B_x[kc], B_const], writes=[PSB[pb]])
                cp("act" if k4 % 2 == 0 else "dve", ys[:, k4 * 512:(k4 + 1) * 512], PS[pb][:], [PSB[pb]], [bys])
            r0 = t * TT + tb * 128
            out_ops.append(P.op("sp", lambda e, ys=ys, r0=r0: e.dma_start(out=y_out[r0:r0 + 128, :], in_=ys[:]), reads=[bys], dma=True))
        P.barrier()
        A.reset(m)

    def mixer_out(t, MOsrc, wsrc, l, cast=False):
        begin_other()
        for kc in range(KC):
            P.op("pool" if cast else "sp", lambda e, kc=kc: e.dma_start(out=hT[kc][:], in_=MOsrc[kc][:, t * TT:(t + 1) * TT]), writes=[B_h[kc]], dma=True)
        hgv = hg(l, 1)

        def cons(c, sub, pb, m0_):
            P.op("dve", lambda e: e.scalar_tensor_tensor(
                out=xT[c][:, sub * SUB:(sub + 1) * SUB], in0=PS[pb][:], scalar=hgv[:, c:c + 1],
                in1=xT[c][:, sub * SUB:(sub + 1) * SUB], op0=ALU.mult, op1=ALU.add), reads=[PSB[pb], B_x[c], B_mod], writes=[B_x[c]])

        mm = linear_fmaj(wsrc, KC, 128, cons)
        P.barrier()
        A.reset(mm)

    def dump_dbg(t, dbg):
        ops = []
        for kc in range(KC):
            ops.append(P.op("sp", lambda e, kc=kc: e.dma_start(out=dbg[t, kc], in_=xT[kc][:]), reads=[B_x[kc]], dma=True))
        return ops

    last = []
    dbg = dout("dbg", [NT, KC, 128, TT]) if STAGE < 99 else None
    for t in range(NT):
        load_x_from_input(t)
        ffn(0, 0)
        if STAGE <= 1:
            last += dump_dbg(t, dbg)
            P.barrier()
            continue
        norm_mod(gs(0, 1), shiftT(0, 1))
        proj_L0(t)
        store_x(t, xs)
        P.barrier()
    if STAGE >= 3:
        A.reset(seg0)
        attn_L0()
        A.reset(seg_top)
        phase["last"] = "other"
        ab_wout = din("ab_wout", [KC, 128, KC * 128])
        for t in range(NT):
            load_x(t, xs)
            mixer_out(t, MO, lambda c: ab_wout[c], 0)
            if STAGE <= 3:
                last += dump_dbg(t, dbg)
                P.barrier()
                continue
            ffn(0, 1)
            if STAGE <= 4:
                last += dump_dbg(t, dbg)
                P.barrier()
                continue
            ffn(1, 0)
            if STAGE <= 5:
                last += dump_dbg(t, dbg)
                P.barrier()
                continue
            norm_mod(gs(1, 1), shiftT(1, 1))
            proj_L1(t)
            store_x(t, xs)
            P.barrier()
    if STAGE >= 6:
        A.reset(seg0)
        attn_L1()
        A.reset(seg_top)
        phase["last"] = "other"
        c_wout = din("c_wout", [KC, 128, KC * 128])
        for t in range(NT):
            load_x(t, xs)
            mixer_out(t, L1d["MO1"], lambda c: c_wout[c], 1, cast=True)
            if STAGE <= 6:
                last += dump_dbg(t, dbg)
                P.barrier()
                continue
            ffn(1, 1)
            if STAGE <= 7:
                last += dump_dbg(t, dbg)
                P.barrier()
                continue
            final_out(t)

    P.emit(final_wait_ops=last + out_ops)
    return nc, used_inputs


def _colchunks(W, cols, ncol):
    return np.ascontiguousarray(W[:, cols].reshape(KC, 128, ncol).transpose(1, 0, 2).reshape(128, KC * ncol))


def _rope_tables(head_dim, sample):
    n = NTOK
    if not sample:
        return np.ones((128, n), np.float32), np.zeros((128, n), np.float32)
    grid_w = 64
    t = np.arange(n)
    row = (t // grid_w).astype(np.float32)
    col = (t % grid_w).astype(np.float32)
    d_axis = head_dim // 2
    inv = (10000.0 ** (-np.arange(0, d_axis, 2, dtype=np.float32) / d_axis)).astype(np.float32)
    ang_r = row[:, None] * inv[None, :]
    ang_c = col[:, None] * inv[None, :]
    nf = d_axis // 2
    cos = np.zeros((128, n), np.float32)
    sin = np.zeros((128, n), np.float32)
    for r in range(128):
        loc = r % head_dim
        ang = ang_r if loc < d_axis else ang_c
        f = (loc % d_axis) % nf
        cos[r] = np.cos(ang[:, f])
        sin[r] = np.sin(ang[:, f])
    return cos, sin


def _perm_T(head_dim):
    d_axis = head_dim // 2
    nf = d_axis // 2
    Pm = np.zeros((128, 128), np.float32)
    for r in range(128):
        base = r - (r % d_axis)
        loc = r % d_axis
        if loc < nf:
            Pm[r, base + loc + nf] = -1.0
        else:
            Pm[r, base + loc - nf] = 1.0
    return np.ascontiguousarray(Pm.T)


def _prep_shared(inp):
    f32 = np.float32
    sh = {}
    for l in range(2):
        sh["ada_w%d" % l] = np.ascontiguousarray(
            inp["ada_w"][l].reshape(KC, 128, 36, 512).transpose(2, 1, 0, 3).reshape(36, 128, KC * 512))
    sh["ada_b"] = np.ascontiguousarray(inp["ada_b"].reshape(2, 1, 9 * D))
    sh["normg_T"] = np.ascontiguousarray(inp["norm_g"].reshape(6, KC, 128).transpose(2, 0, 1).reshape(128, 6 * KC))
    sh["finalg_T"] = np.ascontiguousarray(inp["final_g"].reshape(KC, 128).T)
    for l in range(2):
        for i in range(2):
            wi = inp["ffn_w_in"][l, i].reshape(KC, 128, 2, FC, 128)
            sh["w_in_%d_%d" % (l, i)] = np.ascontiguousarray(wi.transpose(3, 1, 0, 2, 4).reshape(FC, 128, KC * 256))
            wo = inp["ffn_w_out"][l, i].reshape(2, FH, 128, KC, 128)
            sh["w_out_%d_%d" % (l, i)] = np.ascontiguousarray(wo.transpose(0, 3, 2, 1, 4).reshape(2, KC, 128, FH * 128))
    sh["ident"] = np.eye(128, dtype=f32)
    W = inp["ab_w_in"][0]
    fch = ([np.arange(c * 128, (c + 1) * 128) for c in range(16)]
           + [np.arange(3072 + c * 128, 3072 + (c + 1) * 128) for c in range(8)]
           + [np.arange(5120 + c * 128, 5120 + (c + 1) * 128) for c in range(8)])
    sh["ab_wF"] = np.stack([_colchunks(W, c, 128) for c in fch])
    sh["ab_wlow"] = _colchunks(W, np.arange(6144, 6176), 32)
    tg = [np.arange(1024, 1536), np.arange(1536, 2048), np.arange(2048, 2560), np.arange(2560, 3072),
          np.arange(3584, 4096), np.arange(4096, 4608), np.arange(4608, 5120)]
    sh["ab_wT"] = np.stack([_colchunks(W, c, 512) for c in tg])
    sh["permA"] = _perm_T(64)
    sh["alpha17"] = np.ascontiguousarray(np.concatenate([inp["b_alpha_w"][0], inp["b_alpha_b"][0][:, None, :]], axis=1))
    tt = np.arange(128)
    Uf = np.where(tt[:, None] <= tt[None, :], -1.0 / 16, 0.0).astype(f32)
    Ub = np.where(tt[:, None] >= tt[None, :], -1.0 / 16, 0.0).astype(f32)
    sh["Utri"] = np.stack([Uf, Ub])
    mf = (tt[:, None] <= tt[None, :]).astype(f32)
    mb = (tt[:, None] >= tt[None, :]).astype(f32)
    sh["gmask"] = np.stack([mf, mb])
    sh["bnormg_T"] = np.ascontiguousarray(inp["b_norm_g"][0].reshape(2, 128).T)
    sh["a_lambda"] = np.ascontiguousarray(inp["a_lambda"][0].reshape(1, 256))
    sh["subg_T"] = np.ascontiguousarray(inp["a_subln_g"][0].reshape(128, 1))
    sh["ab_wout"] = np.stack([_colchunks(inp["ab_w_out"][0], np.arange(c * 128, (c + 1) * 128), 128) for c in range(KC)])
    Wc = inp["c_w_in"][0]
    sh["c_wF"] = np.stack([_colchunks(Wc, np.arange(c * 128, (c + 1) * 128), 128) for c in range(20)])
    sh["c_wT"] = np.stack([_colchunks(Wc, np.arange(2048, 2560), 512), _colchunks(Wc, np.arange(2560, 3072), 512)])
    sh["permC"] = _perm_T(128)
    sh["c_sink"] = np.ascontiguousarray(inp["c_sink"][0].reshape(1, 16))
    sh["c_wout"] = np.stack([_colchunks(inp["c_w_out"][0], np.arange(c * 128, (c + 1) * 128), 128) for c in range(KC)])
    return sh


def _per_core(inp, g):
    f32 = np.float32
    m = {}
    sample = g >= 2
    if not sample:
        m["x"] = np.ascontiguousarray(inp["x_prompt"][8 * g:8 * g + 8].reshape(NTOK, D))
        cond = inp["c_ctx"]
        m["ctx_ak"] = np.zeros((8, 512, 128), f32)
        m["ctx_av"] = np.zeros((8, 512, 128), f32)
        m["st_f"] = np.zeros((4, 128, 256), f32)
        m["st_b"] = np.zeros((4, 128, 256), f32)
        m["keep"] = np.zeros((128, 1), f32)
        bias = np.full((128, 20, 8), NEG, f32)
        for kc in range(4, 20):
            bias[:, kc, (kc - 4) // 2] = 0.0
        m["biasA"] = bias.reshape(128, 160)
    else:
        b = g - 2
        m["x"] = np.ascontiguousarray(inp["x_sample"][b])
        cond = inp["c"][b]
        m["ctx_ak"] = np.ascontiguousarray(inp["cache_a_k"][b, 0])
        m["ctx_av"] = np.ascontiguousarray(inp["cache_a_v"][b, 0])
        m["st_f"] = np.ascontiguousarray(inp["state_b_fwd"][b, 0])
        m["st_b"] = np.ascontiguousarray(inp["state_b_bwd"][b, 0])
        m["keep"] = np.ones((128, 1), f32)
        m["biasA"] = np.zeros((128, 160), f32)
    m["cond_T"] = np.ascontiguousarray(cond.reshape(KC, 128).T)
    m["cosA"], m["sinA"] = _rope_tables(64, sample)
    m["cosC"], m["sinC"] = _rope_tables(128, sample)
    k = np.arange(128)[:, None]
    q = np.arange(128)[None, :]
    mk = np.zeros((128, 16, 3, 128), f32)
    for qb in range(16):
        for mi in range(3):
            kb = qb - 1 + mi
            if not (0 <= kb < 16):
                continue
            if sample:
                mk[:, qb, mi, :] = (np.abs((qb * 128 + q) - (kb * 128 + k)) <= 128).astype(f32)
            else:
                mk[:, qb, mi, :] = 1.0 if (kb // 2 == qb // 2) else 0.0
    m["maskC"] = mk.reshape(128, 16 * 3 * 128)
    if sample:
        b = g - 2
        m["ctx_ck"] = np.ascontiguousarray(inp["cache_c_k"][b, 0])
        m["ctx_cv"] = np.ascontiguousarray(inp["cache_c_v"][b, 0])
        m["ctxbiasC"] = np.zeros((128, 1), f32)
    else:
        m["ctx_ck"] = np.zeros((4, 512, 128), f32)
        m["ctx_cv"] = np.zeros((4, 512, 128), f32)
        m["ctxbiasC"] = np.full((128, 1), NEG, f32)
    return m


def kernel(**inp):
    inp = {k: np.asarray(v) for k, v in inp.items()}
    groups = [int(s) for s in os.environ.get("MK_GROUPS", "0,1,2,3,0,1,2,3").split(",")]
    sh = _prep_shared(inp)
    nc, used = build_program()
    pcs = {}
    in_maps = []
    for g in groups:
        if g not in pcs:
            pcs[g] = _per_core(inp, g)
        full = dict(sh)
        full.update(pcs[g])
        in_maps.append({k: full[k] for k in used})
    res = run_bass_kernel_spmd(nc, in_maps, core_ids=list(range(len(groups))))
    R = res.results
    if STAGE < 99:
        return R
    gi = {g: groups.index(g) for g in range(4)}
    pr = [R[gi[0]], R[gi[1]]]
    y_prompt = np.concatenate([r["y"].reshape(8, 256, D) for r in pr], axis=0)
    y_sample = np.stack([R[gi[2]]["y"], R[gi[3]]["y"]], axis=0)

    def heads_out(name, nh, dh):
        a = [r[name].reshape(8, 256, nh, dh).transpose(0, 2, 1, 3) for r in pr]
        return np.ascontiguousarray(np.concatenate(a, axis=0)[:, None])

    new_a_k = heads_out("o_ak", 8, 128)
    new_a_v = heads_out("o_av", 8, 128)
    new_b_fwd = np.ascontiguousarray(np.concatenate([r["o_bf"] for r in pr], axis=0)[:, None])
    new_b_bwd = np.ascontiguousarray(np.concatenate([r["o_bb"] for r in pr], axis=0)[:, None])
    new_c_k = heads_out("o_ck", 4, 128)
    new_c_v = heads_out("o_cv", 4, 128)
    f32 = np.float32
    return tuple(np.asarray(a, dtype=f32) for a in (y_prompt, y_sample, new_a_k, new_a_v, new_b_fwd, new_b_bwd, new_c_k, new_c_v))
```

```python
import os
import contextlib
import numpy as np
import concourse.bass as bass
import concourse.mybir as mybir
from concourse.bass_utils import run_bass_kernel_spmd

F32 = mybir.dt.float32
BF16 = mybir.dt.bfloat16
AF = mybir.ActivationFunctionType
ALU = mybir.AluOpType
AX = mybir.AxisListType

ENGS = ("pe", "act", "dve", "pool", "sp")
N_DMA_SEMS = int(os.environ.get("MK_NSEM", "32"))

D = 2048
KC = 16
NTOK = 2048
TT = 1024
NT = NTOK // TT
SUB = 512
NSUB = TT // SUB
DFF = 5632
FC = 44
FH = 22
EPS = 1e-6
NEG = -30000.0

STAGE = int(os.environ.get("MK_STAGE", "99"))


class Buf:
    __slots__ = ("w", "r", "name")

    def __init__(self, name=""):
        self.w = None
        self.r = []
        self.name = name


class Op:
    __slots__ = ("eng", "fn", "deps", "is_dma", "need_inc", "sem", "val")

    def __init__(self, eng, fn, is_dma):
        self.eng = eng
        self.fn = fn
        self.deps = []
        self.is_dma = is_dma
        self.need_inc = False
        self.sem = None
        self.val = None


class Prog:
    def __init__(self, nc):
        self.nc = nc
        self.streams = {e: [] for e in ENGS}
        self.all_bufs = []

    def buf(self, name=""):
        b = Buf(name)
        self.all_bufs.append(b)
        return b

    def op(self, eng, fn, reads=(), writes=(), dma=False):
        o = Op(eng, fn, dma)
        deps = {}
        for b in reads:
            if b.w is not None:
                deps[id(b.w)] = b.w
        for b in writes:
            if b.w is not None:
                deps[id(b.w)] = b.w
            lastr = {}
            for r in b.r:
                if r.is_dma:
                    deps[id(r)] = r
                else:
                    lastr[r.eng] = r
            for r in lastr.values():
                deps[id(r)] = r
        for d in deps.values():
            if d.eng == eng and not d.is_dma and eng == "pe":
                continue
            o.deps.append(d)
            d.need_inc = True
        for b in reads:
            b.r.append(o)
        for b in writes:
            b.w = o
            b.r = []
        self.streams[eng].append(o)
        return o

    def barrier(self):
        lasts = []
        for e in ENGS:
            s = self.streams[e]
            for o in reversed(s):
                if not o.is_dma and o.fn is not None:
                    lasts.append(o)
                    break
        pend = {}
        for b in self.all_bufs:
            if b.w is not None and b.w.is_dma:
                pend[id(b.w)] = b.w
            for r in b.r:
                if r.is_dma:
                    pend[id(r)] = r
        for o in lasts:
            pend[id(o)] = o
        for e in ENGS:
            o = Op(e, None, False)
            for d in pend.values():
                if d.eng == e and not d.is_dma:
                    continue
                o.deps.append(d)
                d.need_inc = True
            self.streams[e].append(o)
        for b in self.all_bufs:
            b.w = None
            b.r = []

    def emit(self, final_wait_ops=()):
        nc = self.nc
        with contextlib.ExitStack() as st:
            esem = {e: st.enter_context(nc.semaphore("s_" + e)) for e in ENGS}
            dsem = {e: [st.enter_context(nc.semaphore("d_%s%d" % (e, i))) for i in range(N_DMA_SEMS)]
                    for e in ("sp", "pool", "act")}
            for e in ENGS:
                cnt = 0
                dcnt = 0
                for o in self.streams[e]:
                    if o.is_dma:
                        o.sem = dsem[e][dcnt % N_DMA_SEMS]
                        o.val = 16 * (dcnt // N_DMA_SEMS + 1)
                        dcnt += 1
                    elif o.need_inc:
                        if o.fn is None:
                            raise RuntimeError("barrier op referenced")
                        cnt += 1
                        o.sem = esem[e]
                        o.val = cnt
            block = st.enter_context(nc.Block())
            engobj = {"pe": "tensor", "act": "scalar", "dve": "vector", "pool": "gpsimd", "sp": "sync"}

            def make(e):
                ops = self.streams[e]

                def body(eng):
                    waited = {}

                    def wait(sem, val):
                        k = id(sem)
                        if waited.get(k, 0) < val:
                            eng.wait_ge(sem, val)
                            waited[k] = val

                    for o in ops:
                        if o.is_dma and o.val > 16:
                            wait(o.sem, o.val - 16)
                        for d in o.deps:
                            wait(d.sem, d.val)
                        if o.fn is None:
                            continue
                        inst = o.fn(eng)
                        if o.is_dma:
                            inst.then_inc(o.sem, 16)
                        elif o.need_inc:
                            inst.then_inc(o.sem, 1)
                    if e == "sp":
                        for d in final_wait_ops:
                            wait(d.sem, d.val)
                return body

            for e in ENGS:
                getattr(block, engobj[e])(make(e))


class Arena:
    def __init__(self, nc, lo, hi):
        self.nc = nc
        self.lo = lo
        self.hi = hi
        self.cur = lo
        self.n = 0

    def alloc(self, shape, dtype, name="t"):
        nbytes = int(np.prod(shape[1:])) * (2 if dtype == BF16 else 4)
        off = (self.cur + 63) // 64 * 64
        if off + nbytes > self.hi:
            raise RuntimeError("SBUF arena overflow %s %d+%d > %d" % (name, off, nbytes, self.hi))
        self.cur = off + nbytes
        self.n += 1
        return self.nc.alloc_sbuf_tensor_at("%s_%d" % (name, self.n), list(shape), dtype, offset=off)

    def mark(self):
        return self.cur

    def reset(self, m):
        self.cur = m


LAM_INIT0 = 0.8 - 0.6 * float(np.exp(-0.3 * 0))


def build_program():
    nc = bass.Bass("TRN2", target_bir_lowering=False)
    P = Prog(nc)
    used_inputs = []

    def din(name, shape, dt=F32):
        used_inputs.append(name)
        return nc.dram_tensor(name, list(shape), dt, kind="ExternalInput").ap()

    def dout(name, shape, dt=F32):
        return nc.dram_tensor(name, list(shape), dt, kind="ExternalOutput").ap()

    def dscr(name, shape, dt=F32):
        return nc.dram_tensor(name, list(shape), dt, kind="Internal").ap()

    x_in = din("x", [NTOK, D])
    cond_T = din("cond_T", [128, KC])
    ada_w = [din("ada_w%d" % l, [36, 128, KC * 512]) for l in range(2)]
    ada_b = din("ada_b", [2, 1, 9 * D])
    normg_T = din("normg_T", [128, 2 * 3 * KC])
    ident_in = din("ident", [128, 128])
    xs = dscr("xs", [NT, KC, 128, TT])

    A = Arena(nc, 16512, 229344 - 64)
    PS = [nc.alloc_psum_tensor("ps%d" % i, [128, 512], F32) for i in range(8)]
    PSB = [P.buf("ps%d" % i) for i in range(8)]

    ident = A.alloc([128, 128], F32, "ident")
    ones_bf = A.alloc([128, 128], BF16, "ones")
    ones_f = A.alloc([1, 128], F32, "onesf")
    condT = A.alloc([128, KC], F32, "condT")
    sc_bf = A.alloc([128, KC], BF16, "scbf")
    normgT = A.alloc([128, 6 * KC], F32, "normgT")
    modT = A.alloc([128, 2 * 144], F32, "modT")
    gsT = A.alloc([128, 2 * 3 * KC], F32, "gsT")
    hgT = A.alloc([128, 2 * 3 * KC], F32, "hgT")
    B_const = P.buf("const")
    B_mod = P.buf("mod")

    P.op("sp", lambda e: e.dma_start(out=ident[:], in_=ident_in), writes=[B_const], dma=True)
    P.op("sp", lambda e: e.dma_start(out=condT[:], in_=cond_T), writes=[B_const], dma=True)
    P.op("sp", lambda e: e.dma_start(out=normgT[:], in_=normg_T), writes=[B_const], dma=True)
    P.op("dve", lambda e: e.memset(ones_bf[:], 1.0), writes=[B_const])
    P.op("dve", lambda e: e.memset(ones_f[:], 1.0), writes=[B_const])
    P.op("act", lambda e: e.activation(out=sc_bf[:], in_=condT[:], func=AF.Silu), reads=[B_const], writes=[B_const])

    def cp(eng, out, in_, reads, writes):
        if eng == "act":
            return P.op("act", lambda e: e.copy(out=out, in_=in_), reads=reads, writes=writes)
        return P.op(eng, lambda e: e.tensor_copy(out=out, in_=in_), reads=reads, writes=writes)

    m0 = A.mark()
    row = A.alloc([1, 9 * D], F32, "modrow")
    brow = A.alloc([1, 9 * D], F32, "biasrow")
    wg = [A.alloc([128, KC, 512], BF16, "adaw") for _ in range(2)]
    B_row = P.buf("row")
    B_brow = P.buf("brow")
    B_wg = [P.buf("wg0"), P.buf("wg1")]
    for l in range(2):
        P.op("sp", lambda e, l=l: e.dma_start(out=brow[:], in_=ada_b[l]), writes=[B_brow], dma=True)
        for g in range(36):
            wb = wg[g % 2]
            P.op("pool", lambda e, wb=wb, l=l, g=g: e.dma_start(
                out=wb[:].rearrange("p k n -> p (k n)"), in_=ada_w[l][g]), writes=[B_wg[g % 2]], dma=True)
            pb = g % 2
            for kc in range(KC):
                P.op("pe", lambda e, wb=wb, kc=kc, pb=pb: e.matmul(
                    PS[pb][0:1, :], lhsT=sc_bf[:, kc:kc + 1], rhs=wb[:, kc, :], start=(kc == 0), stop=(kc == KC - 1)),
                    reads=[B_wg[g % 2], B_const], writes=[PSB[pb]])
            P.op("dve", lambda e, g=g, pb=pb: e.tensor_tensor(
                out=row[0:1, g * 512:(g + 1) * 512], in0=PS[pb][0:1, :], in1=brow[0:1, g * 512:(g + 1) * 512], op=ALU.add),
                reads=[PSB[pb], B_brow], writes=[B_row])
        for j in range(144):
            P.op("pe", lambda e, j=j: e.matmul(PS[2][:, j:j + 1], lhsT=row[0:1, j * 128:(j + 1) * 128],
                                                rhs=ones_f[0:1, 0:1], start=True, stop=True),
                 reads=[B_row, B_const], writes=[PSB[2]])
        P.op("dve", lambda e, l=l: e.tensor_copy(out=modT[:, l * 144:(l + 1) * 144], in_=PS[2][:, 0:144]),
             reads=[PSB[2]], writes=[B_mod])
        for k in range(3):
            o = (l * 3 + k) * KC
            sh = l * 144 + (3 * k) * KC
            P.op("dve", lambda e, o=o, sh=sh: e.scalar_tensor_tensor(
                out=gsT[:, o:o + KC], in0=modT[:, sh + KC:sh + 2 * KC], scalar=1.0, in1=normgT[:, o:o + KC],
                op0=ALU.add, op1=ALU.mult), reads=[B_mod, B_const], writes=[B_mod])
            P.op("dve", lambda e, o=o, sh=sh, k=k: e.tensor_scalar(
                out=hgT[:, o:o + KC], in0=modT[:, sh + 2 * KC:sh + 3 * KC], scalar1=(1.0 if k == 1 else 0.5), scalar2=None,
                op0=ALU.mult), reads=[B_mod], writes=[B_mod])
    P.barrier()
    A.reset(m0)

    def shiftT(l, k):
        o = l * 144 + 3 * k * KC
        return modT[:, o:o + KC]

    def gs(l, k):
        o = (l * 3 + k) * KC
        return gsT[:, o:o + KC]

    def hg(l, k):
        o = (l * 3 + k) * KC
        return hgT[:, o:o + KC]

    seg0 = A.mark()
    xT = [A.alloc([128, TT], F32, "xT") for _ in range(KC)]
    hT = [A.alloc([128, TT], BF16, "hT") for _ in range(KC)]
    B_x = [P.buf("x%d" % i) for i in range(KC)]
    B_h = [P.buf("h%d" % i) for i in range(KC)]
    rstd = A.alloc([128, TT], F32, "rstd")
    B_rstd = P.buf("rstd")
    n_sq = [A.alloc([128, TT], BF16, "sq") for _ in range(2)]
    n_tmp = [A.alloc([128, TT], F32, "nt") for _ in range(2)]
    B_nsq = [P.buf(), P.buf()]
    B_ntmp = [P.buf(), P.buf()]
    seg_top = A.mark()
    NWI = 3
    NWO = 3
    f_actT = [A.alloc([128, TT], BF16, "actT") for _ in range(FH)]
    f_Bact = [P.buf() for _ in range(FH)]
    f_wi = [A.alloc([128, KC, 256], BF16, "wi") for _ in range(NWI)]
    f_Bwi = [P.buf() for _ in range(NWI)]
    f_wo = [A.alloc([128, FH, 128], BF16, "wo") for _ in range(NWO)]
    f_Bwo = [P.buf() for _ in range(NWO)]
    f_sg = [A.alloc([128, SUB], F32, "sg") for _ in range(2)]
    f_Bsg = [P.buf(), P.buf()]
    A.reset(seg_top)
    phase = {"last": "other", "fcnt": 0, "ocnt": 0}

    def begin_other():
        P.barrier()
        phase["last"] = "other"

    def load_x_from_input(t):
        begin_other()
        m = A.mark()
        xin = [A.alloc([128, 4, D], F32, "xin") for _ in range(2)]
        B_xin = [P.buf(), P.buf()]
        for g in range(TT // 512):
            xb = xin[g % 2]
            src = x_in[t * TT + g * 512: t * TT + (g + 1) * 512, :].rearrange("(j p) d -> p j d", p=128)
            P.op("sp", lambda e, xb=xb, src=src: e.dma_start(out=xb[:], in_=src), writes=[B_xin[g % 2]], dma=True)
            for kc in range(KC):
                pb = kc % 2
                for j in range(4):
                    P.op("pe", lambda e, xb=xb, kc=kc, j=j, pb=pb: e.transpose(
                        out=PS[pb][:, j * 128:(j + 1) * 128], in_=xb[:, j, kc * 128:(kc + 1) * 128], identity=ident[:]),
                        reads=[B_xin[g % 2], B_const], writes=[PSB[pb]])
                cp("act" if kc % 2 == 0 else "dve", xT[kc][:, g * 512:(g + 1) * 512], PS[pb][:], [PSB[pb]], [B_x[kc]])
        P.barrier()
        A.reset(m)

    def store_x(t, dst):
        for kc in range(KC):
            P.op("sp", lambda e, kc=kc: e.dma_start(out=dst[t, kc], in_=xT[kc][:]), reads=[B_x[kc]], dma=True)

    def load_x(t, src):
        for kc in range(KC):
            P.op("sp", lambda e, kc=kc: e.dma_start(out=xT[kc][:], in_=src[t, kc]), writes=[B_x[kc]], dma=True)

    def norm_mod(gsv, shv, out_f32=None):
        sq, tmp, B_sq, B_tmp = n_sq, n_tmp, B_nsq, B_ntmp
        for kc in range(KC):
            s = kc % 2
            P.op("act", lambda e, kc=kc, s=s: e.activation(out=sq[s][:], in_=xT[kc][:], func=AF.Square),
                 reads=[B_x[kc]], writes=[B_sq[s]])
            for sub in range(NSUB):
                P.op("pe", lambda e, kc=kc, s=s, sub=sub: e.matmul(
                    PS[6 + sub][:], lhsT=ones_bf[:], rhs=sq[s][:, sub * SUB:(sub + 1) * SUB], start=(kc == 0), stop=(kc == KC - 1)),
                    reads=[B_sq[s], B_const], writes=[PSB[6 + sub]])
        for sub in range(NSUB):
            P.op("act", lambda e, sub=sub: e.activation(out=tmp[0][:, sub * SUB:(sub + 1) * SUB], in_=PS[6 + sub][:], func=AF.Sqrt,
                                                         scale=1.0 / D, bias=EPS), reads=[PSB[6 + sub]], writes=[B_tmp[0]])
        P.op("dve", lambda e: e.reciprocal(out=rstd[:], in_=tmp[0][:]), reads=[B_tmp[0]], writes=[B_rstd])
        for kc in range(KC):
            s = kc % 2
            P.op("dve", lambda e, kc=kc, s=s: e.tensor_tensor(out=tmp[s][:], in0=xT[kc][:], in1=rstd[:], op=ALU.mult),
                 reads=[B_x[kc], B_rstd], writes=[B_tmp[s]])
            if out_f32 is None:
                P.op("act", lambda e, kc=kc, s=s: e.activation(out=hT[kc][:], in_=tmp[s][:], func=AF.Identity,
                                                                scale=gsv[:, kc:kc + 1], bias=shv[:, kc:kc + 1]),
                     reads=[B_tmp[s], B_mod, B_const], writes=[B_h[kc]])
            else:
                P.op("act", lambda e, kc=kc, s=s: e.activation(out=out_f32[kc][:], in_=tmp[s][:], func=AF.Copy,
                                                                scale=gsv[:, kc:kc + 1]),
                     reads=[B_tmp[s], B_mod, B_const], writes=[B_x[kc]])

    w_in_d = {}
    w_out_d = {}

    def ffn(l, i):
        if (l, i) not in w_in_d:
            w_in_d[(l, i)] = din("w_in_%d_%d" % (l, i), [FC, 128, KC * 256])
            w_out_d[(l, i)] = din("w_out_%d_%d" % (l, i), [2, KC, 128, FH * 128])
        w_in = w_in_d[(l, i)]
        w_out = w_out_d[(l, i)]
        k = 0 if i == 0 else 2
        if phase["last"] != "ffn":
            P.barrier()
        phase["last"] = "ffn"
        norm_mod(gs(l, k), shiftT(l, k))
        actT, B_act, wi, B_wi, wo, B_wo, sg, B_sg = f_actT, f_Bact, f_wi, f_Bwi, f_wo, f_Bwo, f_sg, f_Bsg
        hgv = hg(l, k)
        cnt = phase["fcnt"]
        for hf in range(2):
            for fi in range(FH):
                f = hf * FH + fi
                w = wi[f % NWI]
                bw = B_wi[f % NWI]
                P.op("pool", lambda e, w=w, f=f: e.dma_start(out=w[:].rearrange("p k n -> p (k n)"), in_=w_in[f]),
                     writes=[bw], dma=True)
                for sub in range(NSUB):
                    pg = (cnt % 2) * 2
                    pu = pg + 1
                    for kc in range(KC):
                        P.op("pe", lambda e, w=w, kc=kc, sub=sub, pg=pg: e.matmul(
                            PS[pg][:], lhsT=w[:, kc, 0:128], rhs=hT[kc][:, sub * SUB:(sub + 1) * SUB],
                            start=(kc == 0), stop=(kc == KC - 1)), reads=[bw, B_h[kc]], writes=[PSB[pg]])
                    for kc in range(KC):
                        P.op("pe", lambda e, w=w, kc=kc, sub=sub, pu=pu: e.matmul(
                            PS[pu][:], lhsT=w[:, kc, 128:256], rhs=hT[kc][:, sub * SUB:(sub + 1) * SUB],
                            start=(kc == 0), stop=(kc == KC - 1)), reads=[bw, B_h[kc]], writes=[PSB[pu]])
                    s = cnt % 2
                    P.op("act", lambda e, s=s, pg=pg: e.activation(out=sg[s][:], in_=PS[pg][:], func=AF.Silu),
                         reads=[PSB[pg]], writes=[B_sg[s]])
                    P.op("dve", lambda e, s=s, pu=pu, fi=fi, sub=sub: e.tensor_tensor(
                        out=actT[fi][:, sub * SUB:(sub + 1) * SUB], in0=sg[s][:], in1=PS[pu][:], op=ALU.mult),
                        reads=[B_sg[s], PSB[pu]], writes=[B_act[fi]])
                    cnt += 1
            for d in range(KC):
                idx = phase["ocnt"]
                phase["ocnt"] += 1
                w = wo[idx % NWO]
                bw = B_wo[idx % NWO]
                P.op("pool", lambda e, w=w, hf=hf, d=d: e.dma_start(out=w[:].rearrange("p k n -> p (k n)"), in_=w_out[hf, d]),
                     writes=[bw], dma=True)
                for sub in range(NSUB):
                    po = 4 + (idx * NSUB + sub) % 2
                    for fi in range(FH):
                        P.op("pe", lambda e, w=w, fi=fi, sub=sub, po=po: e.matmul(
                            PS[po][:], lhsT=w[:, fi, :], rhs=actT[fi][:, sub * SUB:(sub + 1) * SUB],
                            start=(fi == 0), stop=(fi == FH - 1)), reads=[bw, B_act[fi]], writes=[PSB[po]])
                    P.op("dve", lambda e, d=d, sub=sub, po=po: e.scalar_tensor_tensor(
                        out=xT[d][:, sub * SUB:(sub + 1) * SUB], in0=PS[po][:], scalar=hgv[:, d:d + 1],
                        in1=xT[d][:, sub * SUB:(sub + 1) * SUB], op0=ALU.mult, op1=ALU.add),
                        reads=[PSB[po], B_x[d], B_mod], writes=[B_x[d]])
        phase["fcnt"] = cnt

    def linear_fmaj(wsrc, nchunks, ncol, consumer, banks=(0, 1), nbuf=3, msplit=None):
        m = A.mark()
        wt = [A.alloc([128, KC, ncol], BF16, "wf") for _ in range(nbuf)]
        B_wt = [P.buf() for _ in range(nbuf)]
        cnt = 0
        for c in range(nchunks):
            w = wt[c % nbuf]
            bw = B_wt[c % nbuf]
            P.op("pool", lambda e, w=w, c=c: e.dma_start(out=w[:].rearrange("p k n -> p (k n)"), in_=wsrc(c)),
                 writes=[bw], dma=True)
            for sub in range(NSUB):
                parts = [(0, ncol)] if msplit is None else msplit
                for (m0_, m1_) in parts:
                    pb = banks[cnt % len(banks)]
                    cnt += 1
                    for kc in range(KC):
                        P.op("pe", lambda e, w=w, kc=kc, sub=sub, pb=pb, m0_=m0_, m1_=m1_: e.matmul(
                            PS[pb][0:m1_ - m0_, :], lhsT=w[:, kc, m0_:m1_], rhs=hT[kc][:, sub * SUB:(sub + 1) * SUB],
                            start=(kc == 0), stop=(kc == KC - 1)), reads=[bw, B_h[kc]], writes=[PSB[pb]])
                    consumer(c, sub, pb, m0_)
        return m

    def linear_tmaj(wsrc, ngroups, consumer, banks=(4, 5)):
        m = A.mark()
        wt = [A.alloc([128, KC, 512], BF16, "wtm") for _ in range(2)]
        B_wt = [P.buf() for _ in range(2)]
        cnt = 0
        for g in range(ngroups):
            w = wt[g % 2]
            bw = B_wt[g % 2]
            P.op("pool", lambda e, w=w, g=g: e.dma_start(out=w[:].rearrange("p k n -> p (k n)"), in_=wsrc(g)),
                 writes=[bw], dma=True)
            for tb in range(TT // 128):
                pb = banks[cnt % len(banks)]
                cnt += 1
                for kc in range(KC):
                    P.op("pe", lambda e, w=w, kc=kc, tb=tb, pb=pb: e.matmul(
                        PS[pb][:], lhsT=hT[kc][:, tb * 128:(tb + 1) * 128], rhs=w[:, kc, :],
                        start=(kc == 0), stop=(kc == KC - 1)), reads=[bw, B_h[kc]], writes=[PSB[pb]])
                consumer(g, tb, pb)
        return m

    class Rot:
        def __init__(self, shape, dt, n, name):
            self.t = [A.alloc(shape, dt, name) for _ in range(n)]
            self.b = [P.buf(name) for _ in range(n)]
            self.i = 0

        def next(self):
            k = self.i % len(self.t)
            self.i += 1
            return self.t[k], self.b[k]

    def rope_consumer(pb, tok0, cosT, sinT, permT, B_tab, dst_ap_fn, R):
        qs, bqs = R["qs"].next()
        cp("act", qs[:], PS[pb][:], [PSB[pb]], [bqs])
        pr = R["pbank"][R["pcnt"][0] % 2]
        R["pcnt"][0] += 1
        P.op("pe", lambda e: e.matmul(PS[pr][:], lhsT=permT[:], rhs=qs[:], start=True, stop=True),
             reads=[bqs, B_tab], writes=[PSB[pr]])
        t1, bt1 = R["t1"].next()
        P.op("dve", lambda e: e.tensor_tensor(out=t1[:], in0=qs[:], in1=cosT[:, tok0:tok0 + SUB], op=ALU.mult),
             reads=[bqs, B_tab], writes=[bt1])
        t2, bt2 = R["t2"].next()
        P.op("dve", lambda e: e.tensor_tensor(out=t2[:], in0=PS[pr][:], in1=sinT[:, tok0:tok0 + SUB], op=ALU.mult),
             reads=[PSB[pr], B_tab], writes=[bt2])
        ob, bob = R["ob"].next()
        P.op("dve", lambda e: e.tensor_tensor(out=ob[:], in0=t1[:], in1=t2[:], op=ALU.add),
             reads=[bt1, bt2], writes=[bob])
        dst, src = dst_ap_fn(ob)
        P.op("sp", lambda e: e.dma_start(out=dst, in_=src), reads=[bob], dma=True)

    ab_wF = din("ab_wF", [32, 128, KC * 128])
    ab_wlow = din("ab_wlow", [128, KC * 32])
    ab_wT = din("ab_wT", [7, 128, KC * 512])
    cosA_in = din("cosA", [128, NTOK])
    sinA_in = din("sinA", [128, NTOK])
    permA_in = din("permA", [128, 128])
    QA = dscr("QA", [8, 128, NTOK], BF16)
    KA = dscr("KA", [8, 128, NTOK], BF16)
    VA2 = dscr("VA2", [8, 128, 16, 128])
    BQ = dscr("BQ", [4, 128, NTOK])
    BKF = dscr("BKF", [4, 128, NTOK])
    BKT = dscr("BKT", [16, 128, 512])
    BVT = dscr("BVT", [16, 128, 1024], BF16)
    BR = dscr("BR", [8, 128, NTOK])
    LOW = dscr("LOW", [2, 16, NTOK])
    MO = dscr("MO", [16, 128, NTOK], BF16)
    o_ak = dout("o_ak", [NTOK, 1024])
    o_av = dout("o_av", [NTOK, 1024])
    out_ops = []

    def proj_L0(t):
        begin_other()
        m = A.mark()
        cosT = A.alloc([128, TT], F32, "cos")
        sinT = A.alloc([128, TT], F32, "sin")
        permT = A.alloc([128, 128], F32, "perm")
        B_tab = P.buf("tab")
        P.op("sp", lambda e: e.dma_start(out=cosT[:], in_=cosA_in[:, t * TT:(t + 1) * TT]), writes=[B_tab], dma=True)
        P.op("sp", lambda e: e.dma_start(out=sinT[:], in_=sinA_in[:, t * TT:(t + 1) * TT]), writes=[B_tab], dma=True)
        P.op("sp", lambda e: e.dma_start(out=permT[:], in_=permA_in), writes=[B_tab], dma=True)
        R = {"qs": Rot([128, SUB], F32, 2, "qs"), "t1": Rot([128, SUB], F32, 2, "t1"), "t2": Rot([128, SUB], F32, 2, "t2"),
             "ob": Rot([128, SUB], BF16, 3, "ob"), "pbank": (2, 3), "pcnt": [0]}
        stF = Rot([128, SUB], F32, 3, "stF")

        def consF(c, sub, pb, m0_):
            tok0 = sub * SUB
            g0 = t * TT + tok0
            if c < 16:
                dst = (QA if c < 8 else KA)[c % 8][:, g0:g0 + SUB]
                rope_consumer(pb, tok0, cosT, sinT, permT, B_tab, lambda ob: (dst, ob[:]), R)
            else:
                if c < 20:
                    dst = BQ[c - 16][:, g0:g0 + SUB]
                elif c < 24:
                    dst = BKF[c - 20][:, g0:g0 + SUB]
                else:
                    dst = BR[c - 24][:, g0:g0 + SUB]
                s_, bs_ = stF.next()
                if c >= 24:
                    P.op("act", lambda e: e.activation(out=s_[:], in_=PS[pb][:], func=AF.Silu), reads=[PSB[pb]], writes=[bs_])
                else:
                    cp("act", s_[:], PS[pb][:], [PSB[pb]], [bs_])
                P.op("sp", lambda e: e.dma_start(out=dst, in_=s_[:]), reads=[bs_], dma=True)

        SUBS = int(os.environ.get("MK_SUB", "9"))
        if SUBS == 5:
            stT = Rot([128, 512], F32, 3, "stT")
            MKT = 1

            def consT0(g, tb, pb):
                tg = t * (TT // 128) + tb
                r0 = tg * 128
                s_, bs_ = stT.next()
                cp("act", s_[:], PS[pb][:], [PSB[pb]], [bs_])
                dst = o_ak[r0:r0 + 128, 0:512]
                out_ops.append(P.op("sp", lambda e: e.dma_start(out=dst, in_=s_[:]), reads=[bs_], dma=True))
            linear_tmaj(lambda g: ab_wT[g], 1, consT0)
            P.barrier()
            A.reset(m)
            return
        if SUBS >= 1:
            linear_fmaj(lambda c: ab_wF[c], 16 if SUBS == 1 else 32, 128, consF)
        stL = Rot([16, SUB], F32, 2, "stL")

        def consL(c, sub, pb, m0_):
            g0 = t * TT + sub * SUB
            s_, bs_ = stL.next()
            cp("dve", s_[:], PS[pb][0:16, :], [PSB[pb]], [bs_])
            P.op("sp", lambda e: e.dma_start(out=LOW[m0_ // 16][:, g0:g0 + SUB], in_=s_[:]), reads=[bs_], dma=True)

        if SUBS >= 3:
            linear_fmaj(lambda c: ab_wlow, 1, 32, consL, msplit=[(0, 16), (16, 32)], nbuf=1)
        stT = Rot([128, 512], F32, 3, "stT")
        stTb = Rot([128, 512], BF16, 3, "stTb")

        MKT = int(os.environ.get("MK_T", "15"))

        def consT(g, tb, pb):
            tg = t * (TT // 128) + tb
            r0 = tg * 128
            if MKT == 0:
                s_, bs_ = stT.next()
                cp("act", s_[:], PS[pb][:], [PSB[pb]], [bs_])
                return
            if g < 4:
                s_, bs_ = stT.next()
                cp("act", s_[:], PS[pb][:], [PSB[pb]], [bs_])
                dst = (o_ak if g < 2 else o_av)[r0:r0 + 128, (g % 2) * 512:(g % 2 + 1) * 512]
                if MKT & 1:
                    out_ops.append(P.op("sp", lambda e: e.dma_start(out=dst, in_=s_[:]), reads=[bs_], dma=True))
                if g >= 2 and (MKT & 2):
                    P.op("sp", lambda e: e.dma_start(out=VA2[(g - 2) * 4:(g - 1) * 4, :, tg, :].rearrange("h p d -> p h d"),
                                                     in_=s_[:].rearrange("p (h d) -> p h d", d=128)), reads=[bs_], dma=True)
            elif g == 4 and not (MKT & 4):
                return
            elif g > 4 and not (MKT & 8):
                return
            elif g == 4:
                s_, bs_ = stT.next()
                cp("act", s_[:], PS[pb][:], [PSB[pb]], [bs_])
                P.op("sp", lambda e: e.dma_start(out=BKT[tg], in_=s_[:]), reads=[bs_], dma=True)
            else:
                sb_, bsb_ = stTb.next()
                cp("dve", sb_[:], PS[pb][:], [PSB[pb]], [bsb_])
                P.op("sp", lambda e: e.dma_start(out=BVT[tg][:, (g - 5) * 512:(g - 4) * 512], in_=sb_[:]), reads=[bsb_], dma=True)

        if os.environ.get("MK_BAR"):
            P.barrier()
        if SUBS >= 4:
            linear_tmaj(lambda g: ab_wT[g], int(os.environ.get("MK_NG", "7")), consT)
        P.barrier()
        A.reset(m)

    def attn_L0():
        alpha_in = din("alpha17", [2, 17, 512])
        Utri_in = din("Utri", [2, 128, 128])
        gmask_in = din("gmask", [2, 128, 128])
        keep_in = din("keep", [128, 1])
        st_in = [din("st_f", [4, 128, 256]), din("st_b", [4, 128, 256])]
        o_st = [dout("o_bf", [8, 4, 128, 256]), dout("o_bb", [8, 4, 128, 256])]
        bnormg_in = din("bnormg_T", [128, 2])
        m = A.mark()
        oB = [A.alloc([128, 2, NTOK], F32, "oB") for _ in range(4)]
        B_oB = [P.buf("oB%d" % h) for h in range(4)]
        S = [[A.alloc([128, 256], F32, "S") for _ in range(2)] for _ in range(4)]
        Sb = [[A.alloc([128, 256], BF16, "Sb") for _ in range(2)] for _ in range(4)]
        B_S = [[P.buf() for _ in range(2)] for _ in range(4)]
        B_Sb = [[P.buf() for _ in range(2)] for _ in range(4)]
        aw = A.alloc([17, 2, 512], F32, "aw")
        Ut = A.alloc([128, 2, 128], F32, "Ut")
        gm = A.alloc([128, 2, 128], F32, "gm")
        keep = A.alloc([128, 1], F32, "keep")
        B_gc = P.buf("gconst")
        for d in range(2):
            P.op("sp", lambda e, d=d: e.dma_start(out=aw[:, d, :], in_=alpha_in[d]), writes=[B_gc], dma=True)
            P.op("sp", lambda e, d=d: e.dma_start(out=Ut[:, d, :], in_=Utri_in[d]), writes=[B_gc], dma=True)
            P.op("sp", lambda e, d=d: e.dma_start(out=gm[:, d, :], in_=gmask_in[d]), writes=[B_gc], dma=True)
        P.op("sp", lambda e: e.dma_start(out=keep[:], in_=keep_in), writes=[B_gc], dma=True)
        for h in range(4):
            for d in range(2):
                P.op("sp", lambda e, h=h, d=d: e.dma_start(out=S[h][d][:], in_=st_in[d][h]), writes=[B_S[h][d]], dma=True)
                cp("dve", Sb[h][d][:], S[h][d][:], [B_S[h][d]], [B_Sb[h][d]])
        lowd = Rot([17, 128], F32, 2, "lowd")
        for k in range(2):
            P.op("dve", lambda e, k=k: e.memset(lowd.t[k][:], 1.0), writes=[lowd.b[k]])
        e1 = Rot([128, 512], F32, 2, "e1")
        lnv = Rot([128, 512], F32, 2, "lnv")
        Ep = Rot([128, 512], F32, 2, "Ep")
        Em = Rot([128, 512], F32, 2, "Em")
        EmT = Rot([128, 512], F32, 2, "EmT")
        qF = Rot([128, 4, 128], F32, 2, "qF")
        kF = Rot([128, 4, 128], F32, 2, "kF")
        kT = Rot([128, 512], F32, 2, "kT")
        vT = Rot([128, 1024], BF16, 2, "vT")
        qe = Rot([128, 512], BF16, 2, "qe")
        ke = Rot([128, 512], BF16, 2, "ke")
        keT = Rot([128, 512], BF16, 2, "keT")
        STm = Rot([128, 128], BF16, 8, "STm")
        tmpS = Rot([128, 256], F32, 8, "tmpS")
        Sout = Rot([128, 256], F32, 8, "Sout")
        pcnt = {"a": 0, "st": 0, "o": 0}
        MKA = int(os.environ.get("MK_A", "7"))
        MKG = int(os.environ.get("MK_G", "99"))
        for i in range(16 if (MKA & 1) else 0):
            for d in range(2):
                b = i if d == 0 else 15 - i
                tok = slice(b * 128, (b + 1) * 128)
                lw, blw = lowd.next()
                P.op("sp", lambda e, lw=lw, d=d, tok=tok: e.dma_start(out=lw[0:16, :], in_=LOW[d][:, tok]), writes=[blw], dma=True)
                qf, bqf = qF.next()
                P.op("sp", lambda e, qf=qf, tok=tok: e.dma_start(out=qf[:], in_=BQ[:, :, tok].rearrange("h p t -> p h t")),
                     writes=[bqf], dma=True)
                kf, bkf = kF.next()
                P.op("sp", lambda e, kf=kf, tok=tok: e.dma_start(out=kf[:], in_=BKF[:, :, tok].rearrange("h p t -> p h t")),
                     writes=[bkf], dma=True)
                kt, bkt = kT.next()
                P.op("sp", lambda e, kt=kt, b=b: e.dma_start(out=kt[:], in_=BKT[b]), writes=[bkt], dma=True)
                vt, bvt = vT.next()
                P.op("sp", lambda e, vt=vt, b=b: e.dma_start(out=vt[:], in_=BVT[b]), writes=[bvt], dma=True)
                pa = pcnt["a"] % 2
                pcnt["a"] += 1
                P.op("pe", lambda e, lw=lw, d=d, pa=pa: e.matmul(PS[pa][:], lhsT=lw[0:17, :], rhs=aw[0:17, d, :], start=True, stop=True),
                     reads=[blw, B_gc], writes=[PSB[pa]])
                x1, bx1 = e1.next()
                P.op("act", lambda e, x1=x1, pa=pa: e.activation(out=x1[:], in_=PS[pa][:], func=AF.Exp, scale=-1.0),
                     reads=[PSB[pa]], writes=[bx1])
                lv, blv = lnv.next()
                P.op("act", lambda e, x1=x1, lv=lv: e.activation(out=lv[:], in_=x1[:], func=AF.Ln, bias=1.0),
                     reads=[bx1], writes=[blv])
                P.op("pe", lambda e, lv=lv, d=d, pa=pa: e.matmul(PS[pa][:], lhsT=Ut[:, d, :], rhs=lv[:], start=True, stop=True),
                     reads=[blv, B_gc], writes=[PSB[pa]])
                emt, bemt = EmT.next()
                P.op("act", lambda e, emt=emt, pa=pa: e.activation(out=emt[:], in_=PS[pa][:], func=AF.Exp, scale=-1.0),
                     reads=[PSB[pa]], writes=[bemt])
                for h in range(4):
                    P.op("pe", lambda e, lv=lv, d=d, h=h: e.matmul(PS[2][:, h * 128:(h + 1) * 128], lhsT=lv[:, h * 128:(h + 1) * 128],
                                                                  rhs=Ut[:, d, :], start=True, stop=True),
                         reads=[blv, B_gc], writes=[PSB[2]])
                ep, bep = Ep.next()
                P.op("act", lambda e, ep=ep: e.activation(out=ep[:], in_=PS[2][:], func=AF.Exp), reads=[PSB[2]], writes=[bep])
                em, bem = Em.next()
                P.op("act", lambda e, em=em: e.activation(out=em[:], in_=PS[2][:], func=AF.Exp, scale=-1.0), reads=[PSB[2]], writes=[bem])
                q_, bq_ = qe.next()
                P.op("dve", lambda e, q_=q_, qf=qf, ep=ep: e.scalar_tensor_tensor(
                    out=q_[:], in0=qf[:].rearrange("p h t -> p (h t)"), scalar=float(128 ** -0.5), in1=ep[:], op0=ALU.mult, op1=ALU.mult),
                    reads=[bqf, bep], writes=[bq_])
                k_, bk_ = ke.next()
                P.op("dve", lambda e, k_=k_, kf=kf, em=em: e.tensor_tensor(
                    out=k_[:], in0=kf[:].rearrange("p h t -> p (h t)"), in1=em[:], op=ALU.mult), reads=[bkf, bem], writes=[bk_])
                kt_, bkt_ = keT.next()
                P.op("dve", lambda e, kt_=kt_, kt=kt, emt=emt: e.tensor_tensor(out=kt_[:], in0=kt[:], in1=emt[:], op=ALU.mult),
                     reads=[bkt, bemt], writes=[bkt_])
                HS = [slice(h * 128, (h + 1) * 128) for h in range(4)]
                for h in range(4):
                    P.op("pe", lambda e, k_=k_, q_=q_, hs=HS[h]: e.matmul(PS[3][:, hs], lhsT=k_[:, hs], rhs=q_[:, hs], start=True, stop=True),
                         reads=[bk_, bq_], writes=[PSB[3]])
                sms = []
                for h in range(4):
                    sm, bsm = STm.next()
                    P.op("dve", lambda e, sm=sm, d=d, hs=HS[h]: e.tensor_tensor(out=sm[:], in0=PS[3][:, hs], in1=gm[:, d, :], op=ALU.mult),
                         reads=[PSB[3], B_gc], writes=[bsm])
                    sms.append((sm, bsm))
                OB = (5, 5, 6, 6)
                for h in range(4):
                    sm, bsm = sms[h]
                    po = OB[h]
                    o0 = (h % 2) * 256
                    for c in range(2):
                        P.op("pe", lambda e, vt=vt, sm=sm, h=h, c=c, po=po, o0=o0: e.matmul(
                            PS[po][:, o0 + c * 128:o0 + (c + 1) * 128], lhsT=vt[:, h * 256 + c * 128:h * 256 + (c + 1) * 128], rhs=sm[:],
                            start=True, stop=False), reads=[bvt, bsm], writes=[PSB[po]])
                        P.op("pe", lambda e, q_=q_, h=h, d=d, c=c, hs=HS[h], po=po, o0=o0: e.matmul(
                            PS[po][:, o0 + c * 128:o0 + (c + 1) * 128], lhsT=Sb[h][d][:, c * 128:(c + 1) * 128], rhs=q_[:, hs],
                            start=False, stop=True), reads=[B_Sb[h][d], bq_], writes=[PSB[po]])
                first = (d == 0 and b <= 7) or (d == 1 and b >= 8)
                for h in range(4):
                    po = OB[h]
                    o0 = (h % 2) * 256
                    dsto = oB[h][:, :, tok]
                    srco = PS[po][:, o0:o0 + 256].rearrange("p (c t) -> p c t", c=2)
                    if first:
                        cp("act" if po == 5 else "dve", dsto, srco, [PSB[po]], [B_oB[h]])
                    else:
                        P.op("dve", lambda e, dsto=dsto, srco=srco: e.tensor_tensor(out=dsto, in0=srco, in1=dsto, op=ALU.add),
                             reads=[PSB[po], B_oB[h]], writes=[B_oB[h]])
                DB = (7, 7, 4, 4)
                for h in range(4):
                    pdS = DB[h]
                    o0 = (h % 2) * 256
                    P.op("pe", lambda e, kt_=kt_, vt=vt, h=h, hs=HS[h], pdS=pdS, o0=o0: e.matmul(
                        PS[pdS][:, o0:o0 + 256], lhsT=kt_[:, hs], rhs=vt[:, h * 256:(h + 1) * 256], start=True, stop=True),
                        reads=[bkt_, bvt], writes=[PSB[pdS]])
                seq_end = (b % 2 == 1) if d == 0 else (b % 2 == 0)
                for h in range(4):
                    pdS = DB[h]
                    o0 = (h % 2) * 256
                    ts_, bts_ = tmpS.next()
                    P.op("dve", lambda e, ts_=ts_, h=h, d=d, pdS=pdS, o0=o0: e.tensor_tensor(out=ts_[:], in0=PS[pdS][:, o0:o0 + 256], in1=S[h][d][:], op=ALU.add),
                         reads=[PSB[pdS], B_S[h][d]], writes=[bts_])
                    col = h * 128 + (127 if d == 0 else 0)
                    al = ep[:, col:col + 1]
                    if not seq_end:
                        P.op("dve", lambda e, ts_=ts_, h=h, d=d, al=al: e.tensor_scalar(out=S[h][d][:], in0=ts_[:], scalar1=al, scalar2=None, op0=ALU.mult),
                             reads=[bts_, bep], writes=[B_S[h][d]])
                        P.op("act", lambda e, ts_=ts_, h=h, d=d, al=al: e.activation(out=Sb[h][d][:], in_=ts_[:], func=AF.Copy, scale=al),
                             reads=[bts_, bep], writes=[B_Sb[h][d]])
                    else:
                        so, bso = Sout.next()
                        P.op("dve", lambda e, ts_=ts_, so=so, al=al: e.tensor_scalar(out=so[:], in0=ts_[:], scalar1=al, scalar2=None, op0=ALU.mult),
                             reads=[bts_, bep], writes=[bso])
                        out_ops.append(P.op("sp", lambda e, so=so, d=d, b=b, h=h: e.dma_start(out=o_st[d][b // 2, h], in_=so[:]), reads=[bso], dma=True))
                        P.op("dve", lambda e, so=so, h=h, d=d: e.tensor_scalar(out=S[h][d][:], in0=so[:], scalar1=keep[:, 0:1], scalar2=None, op0=ALU.mult),
                             reads=[bso, B_gc], writes=[B_S[h][d]])
                        P.op("act", lambda e, so=so, h=h, d=d: e.activation(out=Sb[h][d][:], in_=so[:], func=AF.Copy, scale=keep[:, 0:1]),
                             reads=[bso, B_gc], writes=[B_Sb[h][d]])
        bng = A.alloc([128, 2], F32, "bng")
        P.op("sp", lambda e: e.dma_start(out=bng[:], in_=bnormg_in), writes=[B_gc], dma=True)
        sqb = Rot([128, SUB], BF16, 2, "sqb")
        rt = Rot([128, SUB], F32, 2, "rt")
        rr = Rot([128, SUB], F32, 2, "rr")
        brs = Rot([128, SUB], F32, 3, "brs")
        tn = Rot([128, SUB], F32, 2, "tn")
        mob = Rot([128, SUB], BF16, 3, "mob")
        for h in range(4 if (MKA & 2) else 0):
            for sub in range(NTOK // SUB):
                ts = slice(sub * SUB, (sub + 1) * SUB)
                pb = sub % 2
                for c in range(2):
                    s_, bs_ = sqb.next()
                    P.op("act", lambda e, s_=s_, h=h, c=c, ts=ts: e.activation(out=s_[:], in_=oB[h][:, c, ts], func=AF.Square),
                         reads=[B_oB[h]], writes=[bs_])
                    P.op("pe", lambda e, s_=s_, c=c, pb=pb: e.matmul(PS[pb][:], lhsT=ones_bf[:], rhs=s_[:], start=(c == 0), stop=(c == 1)),
                         reads=[bs_, B_const], writes=[PSB[pb]])
                r1, br1 = rt.next()
                P.op("act", lambda e, r1=r1, pb=pb: e.activation(out=r1[:], in_=PS[pb][:], func=AF.Sqrt, scale=1.0 / 256, bias=EPS),
                     reads=[PSB[pb]], writes=[br1])
                r2, br2 = rr.next()
                P.op("dve", lambda e, r1=r1, r2=r2: e.reciprocal(out=r2[:], in_=r1[:]), reads=[br1], writes=[br2])
                for c in range(2):
                    br_, bbr_ = brs.next()
                    P.op("sp", lambda e, br_=br_, h=h, c=c, ts=ts: e.dma_start(out=br_[:], in_=BR[h * 2 + c][:, ts]), writes=[bbr_], dma=True)
                    t_, bt_ = tn.next()
                    P.op("dve", lambda e, t_=t_, h=h, c=c, ts=ts, r2=r2: e.tensor_tensor(out=t_[:], in0=oB[h][:, c, ts], in1=r2[:], op=ALU.mult),
                         reads=[B_oB[h], br2], writes=[bt_])
                    mo_, bmo_ = mob.next()
                    P.op("dve", lambda e, mo_=mo_, t_=t_, c=c, br_=br_: e.scalar_tensor_tensor(
                        out=mo_[:], in0=t_[:], scalar=bng[:, c:c + 1], in1=br_[:], op0=ALU.mult, op1=ALU.mult),
                        reads=[bt_, bbr_, B_gc], writes=[bmo_])
                    P.op("sp", lambda e, mo_=mo_, h=h, c=c, ts=ts: e.dma_start(out=MO[8 + h * 2 + c][:, ts], in_=mo_[:]), reads=[bmo_], dma=True)
        P.barrier()
        A.reset(m)

        ctx_ak = din("ctx_ak", [8, 512, 128])
        ctx_av = din("ctx_av", [8, 512, 128])
        biasA_in = din("biasA", [128, 20 * 8])
        lam_in = din("a_lambda", [1, 256])
        subg_in = din("subg_T", [128, 1])
        m = A.mark()
        biasA = A.alloc([128, 20, 8], F32, "biasA")
        lamr = A.alloc([1, 256], F32, "lamr")
        lamp = A.alloc([1, 128], F32, "lamp")
        lams = A.alloc([1, 8], F32, "lams")
        neglam = A.alloc([128, 1], F32, "neglam")
        sgl = A.alloc([128, 1], F32, "sgl")
        B_ac = P.buf("aconst")
        P.op("sp", lambda e: e.dma_start(out=biasA[:].rearrange("p a b -> p (a b)"), in_=biasA_in), writes=[B_ac], dma=True)
        P.op("sp", lambda e: e.dma_start(out=lamr[:], in_=lam_in), writes=[B_ac], dma=True)
        P.op("sp", lambda e: e.dma_start(out=sgl[:], in_=subg_in), writes=[B_ac], dma=True)
        P.op("dve", lambda e: e.tensor_tensor(out=lamp[0:1, 0:64], in0=lamr[0:1, 0:64], in1=lamr[0:1, 64:128], op=ALU.mult), reads=[B_ac], writes=[B_ac])
        P.op("dve", lambda e: e.tensor_tensor(out=lamp[0:1, 64:128], in0=lamr[0:1, 128:192], in1=lamr[0:1, 192:256], op=ALU.mult), reads=[B_ac], writes=[B_ac])
        P.op("dve", lambda e: e.reduce_sum(out=lams[0:1, 0:2], in_=lamp[0:1, :].rearrange("p (a b) -> p a b", a=2), axis=AX.X), reads=[B_ac], writes=[B_ac])
        P.op("act", lambda e: e.activation(out=lams[0:1, 2:4], in_=lams[0:1, 0:2], func=AF.Exp), reads=[B_ac], writes=[B_ac])
        P.op("dve", lambda e: e.scalar_tensor_tensor(out=lams[0:1, 4:5], in0=lams[0:1, 3:4], scalar=-LAM_INIT0, in1=lams[0:1, 2:3],
                                                     op0=ALU.add, op1=ALU.subtract), reads=[B_ac], writes=[B_ac])
        P.op("pe", lambda e: e.matmul(PS[7][:, 0:1], lhsT=ones_f[0:1, :], rhs=lams[0:1, 4:5], start=True, stop=True),
             reads=[B_ac, B_const], writes=[PSB[7]])
        cp("dve", neglam[:], PS[7][:, 0:1], [PSB[7]], [B_ac])
        P.op("dve", lambda e: e.tensor_scalar(out=sgl[:], in0=sgl[:], scalar1=1.0 - LAM_INIT0, scalar2=None, op0=ALU.mult), reads=[B_ac], writes=[B_ac])

        QAh = Rot([128, 8, 2, 256], BF16, 2, "QAh")
        for k_ in range(2):
            P.op("dve", lambda e, k_=k_: e.memset(QAh.t[k_][:], 0.0), writes=[QAh.b[k_]])
        KAh = Rot([128, 512 + NTOK], BF16, 2, "KAh")
        VAh = Rot([128, 20, 128], BF16, 2, "VAh")
        ckt = Rot([128, 4, 128], F32, 2, "ckt")
        Pb = Rot([128, 512], BF16, 5, "Pb")
        rden = Rot([128, 512], F32, 2, "rden")
        on = Rot([128, 512], F32, 2, "on")
        ao = Rot([128, 256], F32, 2, "ao")
        sqa = Rot([128, 256], BF16, 2, "sqa")
        r1a = Rot([128, 256], F32, 2, "r1a")
        r2a = Rot([128, 256], F32, 2, "r2a")
        ta = Rot([128, 256], F32, 2, "ta")
        oa = Rot([128, 256], BF16, 3, "oa")
        scnt = 0
        for h in range(8 if (MKA & 4) else 0):
            qh, bqh = QAh.next()
            kh, bkh = KAh.next()
            vh, bvh = VAh.next()
            ck, bck = ckt.next()
            MKL = int(os.environ.get("MK_L", "31"))
            if MKL & 1:
                P.op("sp", lambda e, qh=qh, h=h: e.dma_start(out=qh[0:64, :, 0, :], in_=QA[h][0:64, :].rearrange("p (a b) -> p a b", b=256)), writes=[bqh], dma=True)
                P.op("sp", lambda e, qh=qh, h=h: e.dma_start(out=qh[64:128, :, 1, :], in_=QA[h][64:128, :].rearrange("p (a b) -> p a b", b=256)), writes=[bqh], dma=True)
            if MKL & 2:
                P.op("sp", lambda e, kh=kh, h=h: e.dma_start(out=kh[:, 512:], in_=KA[h]), writes=[bkh], dma=True)
            if MKL & 4:
                P.op("sp", lambda e, ck=ck, h=h: e.dma_start(out=ck[:], in_=ctx_ak[h].rearrange("(c p) d -> p c d", p=128)), writes=[bck], dma=True)
                for c in range(4):
                    P.op("pe", lambda e, ck=ck, c=c: e.transpose(out=PS[7][:, c * 128:(c + 1) * 128], in_=ck[:, c, :], identity=ident[:]),
                         reads=[bck, B_const], writes=[PSB[7]])
                cp("dve", kh[:, 0:512], PS[7][:], [PSB[7]], [bkh])
            if MKL & 8:
                P.op("pool", lambda e, vh=vh, h=h: e.dma_start(out=vh[:, 0:4, :], in_=ctx_av[h].rearrange("(c p) d -> p c d", p=128)), writes=[bvh], dma=True)
            if MKL & 16:
                P.op("pool", lambda e, vh=vh, h=h: e.dma_start(out=vh[:, 4:20, :], in_=VA2[h]), writes=[bvh], dma=True)
            MKD = int(os.environ.get("MK_D", "9"))
            SB = (0, 1, 6)
            steps = [(qt, kc) for qt in range(8) for kc in range(20)]
            pend_pb = {}

            def emit_S(j, kh=kh, qh=qh, bkh=bkh, bqh=bqh):
                qt, kc = steps[j]
                ps_ = SB[j % 3]
                ks_ = slice(kc * 128, (kc + 1) * 128)
                P.op("pe", lambda e: e.matmul(PS[ps_][:], lhsT=kh[:, ks_], rhs=qh[:, qt].rearrange("p a b -> p (a b)"), start=True, stop=True),
                     reads=[bkh, bqh], writes=[PSB[ps_]])

            def emit_exp(j):
                qt, kc = steps[j]
                ps_ = SB[j % 3]
                pb_, bpb_ = Pb.next()
                P.op("act", lambda e: e.activation(out=pb_[:], in_=PS[ps_][:], func=AF.Exp, scale=0.125, bias=biasA[:, kc, qt:qt + 1]),
                     reads=[PSB[ps_], B_ac], writes=[bpb_])
                pend_pb[j] = (pb_, bpb_)

            def emit_PV(j, vh=vh, bvh=bvh):
                qt, kc = steps[j]
                po = 2 + qt % 2
                pd = 4 + qt % 2
                pb_, bpb_ = pend_pb.pop(j)
                P.op("pe", lambda e: e.matmul(PS[po][:], lhsT=vh[:, kc, :], rhs=pb_[:], start=(kc == 0), stop=(kc == 19)),
                     reads=[bvh, bpb_], writes=[PSB[po]])
                P.op("pe", lambda e: e.matmul(PS[pd][:], lhsT=ones_bf[:], rhs=pb_[:], start=(kc == 0), stop=(kc == 19)),
                     reads=[bpb_, B_const], writes=[PSB[pd]])

            nst = len(steps)
            emit_S(0)
            emit_S(1)
            for j in range(nst):
                if j + 2 < nst:
                    emit_S(j + 2)
                emit_exp(j)
                emit_PV(j)
                qt, kc = steps[j]
                if kc != 19:
                    continue
                qs_ = slice(qt * 256, (qt + 1) * 256)
                po = 2 + qt % 2
                pd = 4 + qt % 2
                if MKD < 4:
                    continue
                rd, brd = rden.next()
                P.op("dve", lambda e, rd=rd, pd=pd: e.reciprocal(out=rd[:], in_=PS[pd][:]), reads=[PSB[pd]], writes=[brd])
                on_, bon_ = on.next()
                P.op("dve", lambda e, on_=on_, po=po, rd=rd: e.tensor_tensor(out=on_[:], in0=PS[po][:], in1=rd[:], op=ALU.mult), reads=[PSB[po], brd], writes=[bon_])
                ao_, bao_ = ao.next()
                P.op("dve", lambda e, ao_=ao_, on_=on_: e.scalar_tensor_tensor(out=ao_[:], in0=on_[:, 256:512], scalar=neglam[:, 0:1], in1=on_[:, 0:256],
                                                                              op0=ALU.mult, op1=ALU.add), reads=[bon_, B_ac], writes=[bao_])
                sq_, bsq_ = sqa.next()
                P.op("act", lambda e, sq_=sq_, ao_=ao_: e.activation(out=sq_[:], in_=ao_[:], func=AF.Square), reads=[bao_], writes=[bsq_])
                P.op("pe", lambda e, sq_=sq_: e.matmul(PS[7][:, 0:256], lhsT=ones_bf[:], rhs=sq_[:], start=True, stop=True), reads=[bsq_, B_const], writes=[PSB[7]])
                r1_, br1_ = r1a.next()
                P.op("act", lambda e, r1_=r1_: e.activation(out=r1_[:], in_=PS[7][:, 0:256], func=AF.Sqrt, scale=1.0 / 128, bias=EPS), reads=[PSB[7]], writes=[br1_])
                r2_, br2_ = r2a.next()
                P.op("dve", lambda e, r1_=r1_, r2_=r2_: e.reciprocal(out=r2_[:], in_=r1_[:]), reads=[br1_], writes=[br2_])
                oa_, boa_ = oa.next()
                P.op("dve", lambda e, oa_=oa_, ao_=ao_, r2_=r2_: e.scalar_tensor_tensor(out=oa_[:], in0=ao_[:], scalar=sgl[:, 0:1], in1=r2_[:],
                                                                                       op0=ALU.mult, op1=ALU.mult), reads=[bao_, br2_, B_ac], writes=[boa_])
                P.op("sp", lambda e, oa_=oa_, h=h, qs_=qs_: e.dma_start(out=MO[h][:, qs_], in_=oa_[:]), reads=[boa_], dma=True)
        P.barrier()
        A.reset(m)


    L1d = {}

    def decl_L1():
        if L1d:
            return
        L1d["c_wF"] = din("c_wF", [20, 128, KC * 128])
        L1d["c_wT"] = din("c_wT", [2, 128, KC * 512])
        L1d["cosC"] = din("cosC", [128, NTOK])
        L1d["sinC"] = din("sinC", [128, NTOK])
        L1d["permC"] = din("permC", [128, 128])
        L1d["QC"] = dscr("QC", [4, 128, 4, NTOK], BF16)
        L1d["KCs"] = dscr("KCs", [4, 128, NTOK], BF16)
        L1d["VC"] = dscr("VC2", [4, 128, 16, 128])
        L1d["MO1"] = dscr("MO1", [16, 128, NTOK])
        L1d["o_ck"] = dout("o_ck", [NTOK, 512])
        L1d["o_cv"] = dout("o_cv", [NTOK, 512])

    def proj_L1(t):
        decl_L1()
        QC, KCs, VC = L1d["QC"], L1d["KCs"], L1d["VC"]
        begin_other()
        m = A.mark()
        cosT = A.alloc([128, TT], F32, "cos")
        sinT = A.alloc([128, TT], F32, "sin")
        permT = A.alloc([128, 128], F32, "perm")
        B_tab = P.buf("tab")
        P.op("sp", lambda e: e.dma_start(out=cosT[:], in_=L1d["cosC"][:, t * TT:(t + 1) * TT]), writes=[B_tab], dma=True)
        P.op("sp", lambda e: e.dma_start(out=sinT[:], in_=L1d["sinC"][:, t * TT:(t + 1) * TT]), writes=[B_tab], dma=True)
        P.op("sp", lambda e: e.dma_start(out=permT[:], in_=L1d["permC"]), writes=[B_tab], dma=True)
        R = {"qs": Rot([128, SUB], F32, 2, "qs"), "t1": Rot([128, SUB], F32, 2, "t1"), "t2": Rot([128, SUB], F32, 2, "t2"),
             "ob": Rot([128, SUB], BF16, 3, "ob"), "pbank": (2, 3), "pcnt": [0]}

        def consF(c, sub, pb, m0_):
            tok0 = sub * SUB
            g0 = t * TT + tok0
            if c < 16:
                dstf = lambda ob: (QC[c // 4][:, c % 4, g0:g0 + SUB], ob[:])
            else:
                dstf = lambda ob: (KCs[c - 16][:, g0:g0 + SUB], ob[:])
            rope_consumer(pb, tok0, cosT, sinT, permT, B_tab, dstf, R)

        linear_fmaj(lambda c: L1d["c_wF"][c], 20, 128, consF)
        stT = Rot([128, 512], F32, 3, "stT")
        stTb = Rot([128, 512], BF16, 3, "stTb")

        def consT(g, tb, pb):
            tg = t * (TT // 128) + tb
            r0 = tg * 128
            s_, bs_ = stT.next()
            cp("act", s_[:], PS[pb][:], [PSB[pb]], [bs_])
            dst = (L1d["o_ck"] if g == 0 else L1d["o_cv"])[r0:r0 + 128, :]
            out_ops.append(P.op("sp", lambda e: e.dma_start(out=dst, in_=s_[:]), reads=[bs_], dma=True))
            if g == 1:
                P.op("sp", lambda e: e.dma_start(out=VC[:, :, tg, :].rearrange("h p d -> p h d"),
                                                 in_=s_[:].rearrange("p (h d) -> p h d", d=128)), reads=[bs_], dma=True)

        linear_tmaj(lambda g: L1d["c_wT"][g], 2, consT)
        P.barrier()
        A.reset(m)

    def attn_L1():
        decl_L1()
        QC, KCs, VC, MO1 = L1d["QC"], L1d["KCs"], L1d["VC"], L1d["MO1"]
        ctx_ck = din("ctx_ck", [4, 512, 128])
        ctx_cv = din("ctx_cv", [4, 512, 128])
        maskC_in = din("maskC", [128, 16 * 3 * 128])
        ctxb_in = din("ctxbiasC", [128, 1])
        sink_in = din("c_sink", [1, 16])
        m = A.mark()
        maskE = A.alloc([128, 16, 3, 4, 128], BF16, "maskE")
        ctxb = A.alloc([128, 1], F32, "ctxb")
        sk1 = A.alloc([1, 32], F32, "sk1")
        sk = A.alloc([128, 16], F32, "sk")
        sinkT = A.alloc([128, 16, 128], F32, "sinkT")
        B_cc = P.buf("cconst")
        maskK = A.alloc([128, 16, 3, 128], BF16, "maskK")
        B_mk = P.buf("maskK")
        P.op("pool", lambda e: e.dma_start(out=maskK[:].rearrange("p a b c -> p (a b c)"), in_=maskC_in), writes=[B_mk], dma=True)
        for j in range(4):
            P.op("dve", lambda e, j=j: e.tensor_copy(out=maskE[:, :, :, j, :], in_=maskK[:]), reads=[B_mk], writes=[B_cc])
        P.op("sp", lambda e: e.dma_start(out=ctxb[:], in_=ctxb_in), writes=[B_cc], dma=True)
        P.op("sp", lambda e: e.dma_start(out=sk1[0:1, 0:16], in_=sink_in), writes=[B_cc], dma=True)
        P.op("act", lambda e: e.activation(out=sk1[0:1, 16:32], in_=sk1[0:1, 0:16], func=AF.Exp), reads=[B_cc], writes=[B_cc])
        P.op("pe", lambda e: e.matmul(PS[7][:, 0:16], lhsT=ones_f[0:1, :], rhs=sk1[0:1, 16:32], start=True, stop=True),
             reads=[B_cc, B_const], writes=[PSB[7]])
        cp("dve", sk[:], PS[7][:, 0:16], [PSB[7]], [B_cc])
        P.op("dve", lambda e: e.tensor_copy(out=sinkT[:], in_=sk[:].unsqueeze(2).broadcast_to([128, 16, 128])), reads=[B_cc], writes=[B_cc])
        QCg = Rot([128, 4, NTOK], BF16, 2, "QCg")
        KCg = Rot([128, 512 + NTOK], BF16, 2, "KCg")
        VCg = Rot([128, 20, 128], BF16, 2, "VCg")
        ckt = Rot([128, 4, 128], F32, 2, "ckt")
        Pb = Rot([128, 512], BF16, 5, "Pb")
        den = Rot([128, 512], F32, 2, "den")
        rd = Rot([128, 512], F32, 2, "rd")
        ob = Rot([128, 4, 128], F32, 3, "obC")
        scale = float(128 ** -0.5)
        scnt = 0
        for g in range(4):
            qg, bqg = QCg.next()
            kg, bkg = KCg.next()
            vg, bvg = VCg.next()
            ck, bck = ckt.next()
            P.op("sp", lambda e, qg=qg, g=g: e.dma_start(out=qg[:], in_=QC[g]), writes=[bqg], dma=True)
            P.op("sp", lambda e, kg=kg, g=g: e.dma_start(out=kg[:, 512:], in_=KCs[g]), writes=[bkg], dma=True)
            P.op("sp", lambda e, ck=ck, g=g: e.dma_start(out=ck[:], in_=ctx_ck[g].rearrange("(c p) d -> p c d", p=128)), writes=[bck], dma=True)
            P.op("pool", lambda e, vg=vg, g=g: e.dma_start(out=vg[:, 0:4, :], in_=ctx_cv[g].rearrange("(c p) d -> p c d", p=128)), writes=[bvg], dma=True)
            P.op("pool", lambda e, vg=vg, g=g: e.dma_start(out=vg[:, 4:20, :], in_=VC[g]), writes=[bvg], dma=True)
            for c in range(4):
                P.op("pe", lambda e, ck=ck, c=c: e.transpose(out=PS[7][:, c * 128:(c + 1) * 128], in_=ck[:, c, :], identity=ident[:]),
                     reads=[bck, B_const], writes=[PSB[7]])
            cp("dve", kg[:, 0:512], PS[7][:], [PSB[7]], [bkg])
            SB = (0, 1, 6)
            steps = []
            for qb in range(16):
                chunks = [(c, None) for c in range(4)] + [(4 + kb, kb - qb + 1) for kb in (qb - 1, qb, qb + 1) if 0 <= kb < 16]
                for idx, (kc, mi) in enumerate(chunks):
                    steps.append((qb, kc, mi, idx, len(chunks)))
            pend_pb = {}

            def emit_S(j, kg=kg, qg=qg, bkg=bkg, bqg=bqg):
                qb, kc, mi, idx, n = steps[j]
                ps_ = SB[j % 3]
                P.op("pe", lambda e: e.matmul(PS[ps_][:], lhsT=kg[:, kc * 128:(kc + 1) * 128], rhs=qg[:, :, qb * 128:(qb + 1) * 128], start=True, stop=True),
                     reads=[bkg, bqg], writes=[PSB[ps_]])

            def emit_exp(j):
                qb, kc, mi, idx, n = steps[j]
                ps_ = SB[j % 3]
                pb_, bpb_ = Pb.next()
                if kc < 4:
                    P.op("act", lambda e: e.activation(out=pb_[:], in_=PS[ps_][:], func=AF.Exp, scale=scale, bias=ctxb[:, 0:1]),
                         reads=[PSB[ps_], B_cc], writes=[bpb_])
                else:
                    P.op("act", lambda e: e.activation(out=pb_[:], in_=PS[ps_][:], func=AF.Exp, scale=scale),
                         reads=[PSB[ps_]], writes=[bpb_])
                    P.op("dve", lambda e: e.tensor_tensor(out=pb_[:], in0=pb_[:], in1=maskE[:, qb, mi].rearrange("p j q -> p (j q)"), op=ALU.mult),
                         reads=[bpb_, B_cc], writes=[bpb_])
                pend_pb[j] = (pb_, bpb_)

            def emit_PV(j, vg=vg, bvg=bvg):
                qb, kc, mi, idx, n = steps[j]
                po = 2 + qb % 2
                pd = 4 + qb % 2
                pb_, bpb_ = pend_pb.pop(j)
                P.op("pe", lambda e: e.matmul(PS[po][:], lhsT=vg[:, kc, :], rhs=pb_[:], start=(idx == 0), stop=(idx == n - 1)),
                     reads=[bvg, bpb_], writes=[PSB[po]])
                P.op("pe", lambda e: e.matmul(PS[pd][:], lhsT=ones_bf[:], rhs=pb_[:], start=(idx == 0), stop=(idx == n - 1)),
                     reads=[bpb_, B_const], writes=[PSB[pd]])

            nst = len(steps)
            emit_S(0)
            emit_S(1)
            for j in range(nst):
                if j + 2 < nst:
                    emit_S(j + 2)
                emit_exp(j)
                emit_PV(j)
                qb, kc, mi, idx, n = steps[j]
                if idx != n - 1:
                    continue
                po = 2 + qb % 2
                pd = 4 + qb % 2
                dn, bdn = den.next()
                P.op("dve", lambda e, dn=dn, pd=pd, g=g: e.tensor_tensor(out=dn[:], in0=PS[pd][:], in1=sinkT[:, g * 4:(g + 1) * 4, :].rearrange("p j q -> p (j q)"), op=ALU.add),
                     reads=[PSB[pd], B_cc], writes=[bdn])
                r_, br_ = rd.next()
                P.op("dve", lambda e, r_=r_, dn=dn: e.reciprocal(out=r_[:], in_=dn[:]), reads=[bdn], writes=[br_])
                o_, bo_ = ob.next()
                P.op("dve", lambda e, o_=o_, po=po, r_=r_: e.tensor_tensor(out=o_[:].rearrange("p j q -> p (j q)"), in0=PS[po][:], in1=r_[:], op=ALU.mult),
                     reads=[PSB[po], br_], writes=[bo_])
                P.op("sp", lambda e, o_=o_, g=g, qb=qb: e.dma_start(
                    out=MO1[g * 4:(g + 1) * 4, :, qb * 128:(qb + 1) * 128].rearrange("j p q -> p j q"), in_=o_[:]), reads=[bo_], dma=True)
        P.barrier()
        A.reset(m)

    def final_out(t):
        if "finalg_in" not in L1d:
            L1d["finalg_in"] = din("finalg_T", [128, KC])
            L1d["y_out"] = dout("y", [NTOK, D])
        finalg_T = L1d["finalg_in"]
        y_out = L1d["y_out"]
        begin_other()
        m = A.mark()
        fg = A.alloc([128, KC], F32, "fg")
        B_fg = P.buf("fg")
        P.op("sp", lambda e: e.dma_start(out=fg[:], in_=finalg_T), writes=[B_mod], dma=True)
        norm_mod(fg, None, out_f32=xT)
        yst = Rot([128, D], F32, 2, "yst")
        cnt = 0
        for tb in range(TT // 128):
            ys, bys = yst.next()
            for k4 in range(4):
                pb = cnt % 2
                cnt += 1
                for kk in range(4):
                    kc = k4 * 4 + kk
                    P.op("pe", lambda e, kc=kc, kk=kk, tb=tb, pb=pb: e.transpose(
                        out=PS[pb][:, kk * 128:(kk + 1) * 128], in_=xT[kc][:, tb * 128:(tb + 1) * 128], identity=ident[:]),
                        reads=[B_x[kc], B_const], writes=[PSB[pb]])
                cp("act" if k4 % 2 == 0 else "dve", ys[:, k4 * 512:(k4 + 1) * 512], PS[pb][:], [PSB[pb]], [bys])
            r0 = t * TT + tb * 128
            out_ops.append(P.op("sp", lambda e, ys=ys, r0=r0: e.dma_start(out=y_out[r0:r0 + 128, :], in_=ys[:]), reads=[bys], dma=True))
        P.barrier()
        A.reset(m)

    def mixer_out(t, MOsrc, wsrc, l, cast=False):
        begin_other()
        for kc in range(KC):
            P.op("pool" if cast else "sp", lambda e, kc=kc: e.dma_start(out=hT[kc][:], in_=MOsrc[kc][:, t * TT:(t + 1) * TT]), writes=[B_h[kc]], dma=True)
        hgv = hg(l, 1)

        def cons(c, sub, pb, m0_):
            P.op("dve", lambda e: e.scalar_tensor_tensor(
                out=xT[c][:, sub * SUB:(sub + 1) * SUB], in0=PS[pb][:], scalar=hgv[:, c:c + 1],
                in1=xT[c][:, sub * SUB:(sub + 1) * SUB], op0=ALU.mult, op1=ALU.add), reads=[PSB[pb], B_x[c], B_mod], writes=[B_x[c]])

        mm = linear_fmaj(wsrc, KC, 128, cons)
        P.barrier()
        A.reset(mm)

    def dump_dbg(t, dbg):
        ops = []
        for kc in range(KC):
            ops.append(P.op("sp", lambda e, kc=kc: e.dma_start(out=dbg[t, kc], in_=xT[kc][:]), reads=[B_x[kc]], dma=True))
        return ops

    last = []
    dbg = dout("dbg", [NT, KC, 128, TT]) if STAGE < 99 else None
    for t in range(NT):
        load_x_from_input(t)
        ffn(0, 0)
        if STAGE <= 1:
            last += dump_dbg(t, dbg)
            P.barrier()
            continue
        norm_mod(gs(0, 1), shiftT(0, 1))
        proj_L0(t)
        store_x(t, xs)
        P.barrier()
    if STAGE >= 3:
        A.reset(seg0)
        attn_L0()
        A.reset(seg_top)
        phase["last"] = "other"
        ab_wout = din("ab_wout", [KC, 128, KC * 128])
        for t in range(NT):
            load_x(t, xs)
            mixer_out(t, MO, lambda c: ab_wout[c], 0)
            if STAGE <= 3:
                last += dump_dbg(t, dbg)
                P.barrier()
                continue
            ffn(0, 1)
            if STAGE <= 4:
                last += dump_dbg(t, dbg)
                P.barrier()
                continue
            ffn(1, 0)
            if STAGE <= 5:
                last += dump_dbg(t, dbg)
                P.barrier()
                continue
            norm_mod(gs(1, 1), shiftT(1, 1))
            proj_L1(t)
            store_x(t, xs)
            P.barrier()
    if STAGE >= 6:
        A.reset(seg0)
        attn_L1()
        A.reset(seg_top)
        phase["last"] = "other"
        c_wout = din("c_wout", [KC, 128, KC * 128])
        for t in range(NT):
            load_x(t, xs)
            mixer_out(t, L1d["MO1"], lambda c: c_wout[c], 1, cast=True)
            if STAGE <= 6:
                last += dump_dbg(t, dbg)
                P.barrier()
                continue
            ffn(1, 1)
            if STAGE <= 7:
                last += dump_dbg(t, dbg)
                P.barrier()
                continue
            final_out(t)

    P.emit(final_wait_ops=last + out_ops)
    return nc, used_inputs


def _colchunks(W, cols, ncol):
    return np.ascontiguousarray(W[:, cols].reshape(KC, 128, ncol).transpose(1, 0, 2).reshape(128, KC * ncol))


def _rope_tables(head_dim, sample):
    n = NTOK
    if not sample:
        return np.ones((128, n), np.float32), np.zeros((128, n), np.float32)
    grid_w = 64
    t = np.arange(n)
    row = (t // grid_w).astype(np.float32)
    col = (t % grid_w).astype(np.float32)
    d_axis = head_dim // 2
    inv = (10000.0 ** (-np.arange(0, d_axis, 2, dtype=np.float32) / d_axis)).astype(np.float32)
    ang_r = row[:, None] * inv[None, :]
    ang_c = col[:, None] * inv[None, :]
    nf = d_axis // 2
    cos = np.zeros((128, n), np.float32)
    sin = np.zeros((128, n), np.float32)
    for r in range(128):
        loc = r % head_dim
        ang = ang_r if loc < d_axis else ang_c
        f = (loc % d_axis) % nf
        cos[r] = np.cos(ang[:, f])
        sin[r] = np.sin(ang[:, f])
    return cos, sin


def _perm_T(head_dim):
    d_axis = head_dim // 2
    nf = d_axis // 2
    Pm = np.zeros((128, 128), np.float32)
    for r in range(128):
        base = r - (r % d_axis)
        loc = r % d_axis
        if loc < nf:
            Pm[r, base + loc + nf] = -1.0
        else:
            Pm[r, base + loc - nf] = 1.0
    return np.ascontiguousarray(Pm.T)


def _prep_shared(inp):
    f32 = np.float32
    sh = {}
    for l in range(2):
        sh["ada_w%d" % l] = np.ascontiguousarray(
            inp["ada_w"][l].reshape(KC, 128, 36, 512).transpose(2, 1, 0, 3).reshape(36, 128, KC * 512))
    sh["ada_b"] = np.ascontiguousarray(inp["ada_b"].reshape(2, 1, 9 * D))
    sh["normg_T"] = np.ascontiguousarray(inp["norm_g"].reshape(6, KC, 128).transpose(2, 0, 1).reshape(128, 6 * KC))
    sh["finalg_T"] = np.ascontiguousarray(inp["final_g"].reshape(KC, 128).T)
    for l in range(2):
        for i in range(2):
            wi = inp["ffn_w_in"][l, i].reshape(KC, 128, 2, FC, 128)
            sh["w_in_%d_%d" % (l, i)] = np.ascontiguousarray(wi.transpose(3, 1, 0, 2, 4).reshape(FC, 128, KC * 256))
            wo = inp["ffn_w_out"][l, i].reshape(2, FH, 128, KC, 128)
            sh["w_out_%d_%d" % (l, i)] = np.ascontiguousarray(wo.transpose(0, 3, 2, 1, 4).reshape(2, KC, 128, FH * 128))
    sh["ident"] = np.eye(128, dtype=f32)
    W = inp["ab_w_in"][0]
    fch = ([np.arange(c * 128, (c + 1) * 128) for c in range(16)]
           + [np.arange(3072 + c * 128, 3072 + (c + 1) * 128) for c in range(8)]
           + [np.arange(5120 + c * 128, 5120 + (c + 1) * 128) for c in range(8)])
    sh["ab_wF"] = np.stack([_colchunks(W, c, 128) for c in fch])
    sh["ab_wlow"] = _colchunks(W, np.arange(6144, 6176), 32)
    tg = [np.arange(1024, 1536), np.arange(1536, 2048), np.arange(2048, 2560), np.arange(2560, 3072),
          np.arange(3584, 4096), np.arange(4096, 4608), np.arange(4608, 5120)]
    sh["ab_wT"] = np.stack([_colchunks(W, c, 512) for c in tg])
    sh["permA"] = _perm_T(64)
    sh["alpha17"] = np.ascontiguousarray(np.concatenate([inp["b_alpha_w"][0], inp["b_alpha_b"][0][:, None, :]], axis=1))
    tt = np.arange(128)
    Uf = np.where(tt[:, None] <= tt[None, :], -1.0 / 16, 0.0).astype(f32)
    Ub = np.where(tt[:, None] >= tt[None, :], -1.0 / 16, 0.0).astype(f32)
    sh["Utri"] = np.stack([Uf, Ub])
    mf = (tt[:, None] <= tt[None, :]).astype(f32)
    mb = (tt[:, None] >= tt[None, :]).astype(f32)
    sh["gmask"] = np.stack([mf, mb])
    sh["bnormg_T"] = np.ascontiguousarray(inp["b_norm_g"][0].reshape(2, 128).T)
    sh["a_lambda"] = np.ascontiguousarray(inp["a_lambda"][0].reshape(1, 256))
    sh["subg_T"] = np.ascontiguousarray(inp["a_subln_g"][0].reshape(128, 1))
    sh["ab_wout"] = np.stack([_colchunks(inp["ab_w_out"][0], np.arange(c * 128, (c + 1) * 128), 128) for c in range(KC)])
    Wc = inp["c_w_in"][0]
    sh["c_wF"] = np.stack([_colchunks(Wc, np.arange(c * 128, (c + 1) * 128), 128) for c in range(20)])
    sh["c_wT"] = np.stack([_colchunks(Wc, np.arange(2048, 2560), 512), _colchunks(Wc, np.arange(2560, 3072), 512)])
    sh["permC"] = _perm_T(128)
    sh["c_sink"] = np.ascontiguousarray(inp["c_sink"][0].reshape(1, 16))
    sh["c_wout"] = np.stack([_colchunks(inp["c_w_out"][0], np.arange(c * 128, (c + 1) * 128), 128) for c in range(KC)])
    return sh


def _per_core(inp, g):
    f32 = np.float32
    m = {}
    sample = g >= 2
    if not sample:
        m["x"] = np.ascontiguousarray(inp["x_prompt"][8 * g:8 * g + 8].reshape(NTOK, D))
        cond = inp["c_ctx"]
        m["ctx_ak"] = np.zeros((8, 512, 128), f32)
        m["ctx_av"] = np.zeros((8, 512, 128), f32)
        m["st_f"] = np.zeros((4, 128, 256), f32)
        m["st_b"] = np.zeros((4, 128, 256), f32)
        m["keep"] = np.zeros((128, 1), f32)
        bias = np.full((128, 20, 8), NEG, f32)
        for kc in range(4, 20):
            bias[:, kc, (kc - 4) // 2] = 0.0
        m["biasA"] = bias.reshape(128, 160)
    else:
        b = g - 2
        m["x"] = np.ascontiguousarray(inp["x_sample"][b])
        cond = inp["c"][b]
        m["ctx_ak"] = np.ascontiguousarray(inp["cache_a_k"][b, 0])
        m["ctx_av"] = np.ascontiguousarray(inp["cache_a_v"][b, 0])
        m["st_f"] = np.ascontiguousarray(inp["state_b_fwd"][b, 0])
        m["st_b"] = np.ascontiguousarray(inp["state_b_bwd"][b, 0])
        m["keep"] = np.ones((128, 1), f32)
        m["biasA"] = np.zeros((128, 160), f32)
    m["cond_T"] = np.ascontiguousarray(cond.reshape(KC, 128).T)
    m["cosA"], m["sinA"] = _rope_tables(64, sample)
    m["cosC"], m["sinC"] = _rope_tables(128, sample)
    k = np.arange(128)[:, None]
    q = np.arange(128)[None, :]
    mk = np.zeros((128, 16, 3, 128), f32)
    for qb in range(16):
        for mi in range(3):
            kb = qb - 1 + mi
            if not (0 <= kb < 16):
                continue
            if sample:
                mk[:, qb, mi, :] = (np.abs((qb * 128 + q) - (kb * 128 + k)) <= 128).astype(f32)
            else:
                mk[:, qb, mi, :] = 1.0 if (kb // 2 == qb // 2) else 0.0
    m["maskC"] = mk.reshape(128, 16 * 3 * 128)
    if sample:
        b = g - 2
        m["ctx_ck"] = np.ascontiguousarray(inp["cache_c_k"][b, 0])
        m["ctx_cv"] = np.ascontiguousarray(inp["cache_c_v"][b, 0])
        m["ctxbiasC"] = np.zeros((128, 1), f32)
    else:
        m["ctx_ck"] = np.zeros((4, 512, 128), f32)
        m["ctx_cv"] = np.zeros((4, 512, 128), f32)
        m["ctxbiasC"] = np.full((128, 1), NEG, f32)
    return m


def kernel(**inp):
    inp = {k: np.asarray(v) for k, v in inp.items()}
    groups = [int(s) for s in os.environ.get("MK_GROUPS", "0,1,2,3,0,1,2,3").split(",")]
    sh = _prep_shared(inp)
    nc, used = build_program()
    pcs = {}
    in_maps = []
    for g in groups:
        if g not in pcs:
            pcs[g] = _per_core(inp, g)
        full = dict(sh)
        full.update(pcs[g])
        in_maps.append({k: full[k] for k in used})
    res = run_bass_kernel_spmd(nc, in_maps, core_ids=list(range(len(groups))))
    R = res.results
    if STAGE < 99:
        return R
    gi = {g: groups.index(g) for g in range(4)}
    pr = [R[gi[0]], R[gi[1]]]
    y_prompt = np.concatenate([r["y"].reshape(8, 256, D) for r in pr], axis=0)
    y_sample = np.stack([R[gi[2]]["y"], R[gi[3]]["y"]], axis=0)

    def heads_out(name, nh, dh):
        a = [r[name].reshape(8, 256, nh, dh).transpose(0, 2, 1, 3) for r in pr]
        return np.ascontiguousarray(np.concatenate(a, axis=0)[:, None])

    new_a_k = heads_out("o_ak", 8, 128)
    new_a_v = heads_out("o_av", 8, 128)
    new_b_fwd = np.ascontiguousarray(np.concatenate([r["o_bf"] for r in pr], axis=0)[:, None])
    new_b_bwd = np.ascontiguousarray(np.concatenate([r["o_bb"] for r in pr], axis=0)[:, None])
    new_c_k = heads_out("o_ck", 4, 128)
    new_c_v = heads_out("o_cv", 4, 128)
    f32 = np.float32
    return tuple(np.asarray(a, dtype=f32) for a in (y_prompt, y_sample, new_a_k, new_a_v, new_b_fwd, new_b_bwd, new_c_k, new_c_v))
```

```python
import os
import contextlib
import numpy as np
import concourse.bass as bass
import concourse.mybir as mybir
from concourse.bass_utils import run_bass_kernel_spmd

F32 = mybir.dt.float32
BF16 = mybir.dt.bfloat16
AF = mybir.ActivationFunctionType
ALU = mybir.AluOpType
AX = mybir.AxisListType

ENGS = ("pe", "act", "dve", "pool", "sp")
N_DMA_SEMS = int(os.environ.get("MK_NSEM", "32"))

D = 2048
KC = 16
NTOK = 2048
TT = 1024
NT = NTOK // TT
SUB = 512
NSUB = TT // SUB
DFF = 5632
FC = 44
FH = 22
EPS = 1e-6
NEG = -30000.0

STAGE = int(os.environ.get("MK_STAGE", "99"))


class Buf:
    __slots__ = ("w", "r", "name")

    def __init__(self, name=""):
        self.w = None
        self.r = []
        self.name = name


class Op:
    __slots__ = ("eng", "fn", "deps", "is_dma", "need_inc", "sem", "val")

    def __init__(self, eng, fn, is_dma):
        self.eng = eng
        self.fn = fn
        self.deps = []
        self.is_dma = is_dma
        self.need_inc = False
        self.sem = None
        self.val = None


class Prog:
    def __init__(self, nc):
        self.nc = nc
        self.streams = {e: [] for e in ENGS}
        self.all_bufs = []

    def buf(self, name=""):
        b = Buf(name)
        self.all_bufs.append(b)
        return b

    def op(self, eng, fn, reads=(), writes=(), dma=False):
        o = Op(eng, fn, dma)
        deps = {}
        for b in reads:
            if b.w is not None:
                deps[id(b.w)] = b.w
        for b in writes:
            if b.w is not None:
                deps[id(b.w)] = b.w
            lastr = {}
            for r in b.r:
                if r.is_dma:
                    deps[id(r)] = r
                else:
                    lastr[r.eng] = r
            for r in lastr.values():
                deps[id(r)] = r
        for d in deps.values():
            if d.eng == eng and not d.is_dma and eng == "pe":
                continue
            o.deps.append(d)
            d.need_inc = True
        for b in reads:
            b.r.append(o)
        for b in writes:
            b.w = o
            b.r = []
        self.streams[eng].append(o)
        return o

    def barrier(self):
        lasts = []
        for e in ENGS:
            s = self.streams[e]
            for o in reversed(s):
                if not o.is_dma and o.fn is not None:
                    lasts.append(o)
                    break
        pend = {}
        for b in self.all_bufs:
            if b.w is not None and b.w.is_dma:
                pend[id(b.w)] = b.w
            for r in b.r:
                if r.is_dma:
                    pend[id(r)] = r
        for o in lasts:
            pend[id(o)] = o
        for e in ENGS:
            o = Op(e, None, False)
            for d in pend.values():
                if d.eng == e and not d.is_dma:
                    continue
                o.deps.append(d)
                d.need_inc = True
            self.streams[e].append(o)
        for b in self.all_bufs:
            b.w = None
            b.r = []

    def emit(self, final_wait_ops=()):
        nc = self.nc
        with contextlib.ExitStack() as st:
            esem = {e: st.enter_context(nc.semaphore("s_" + e)) for e in ENGS}
            dsem = {e: [st.enter_context(nc.semaphore("d_%s%d" % (e, i))) for i in range(N_DMA_SEMS)]
                    for e in ("sp", "pool", "act")}
            for e in ENGS:
                cnt = 0
                dcnt = 0
                for o in self.streams[e]:
                    if o.is_dma:
                        o.sem = dsem[e][dcnt % N_DMA_SEMS]
                        o.val = 16 * (dcnt // N_DMA_SEMS + 1)
                        dcnt += 1
                    elif o.need_inc:
                        if o.fn is None:
                            raise RuntimeError("barrier op referenced")
                        cnt += 1
                        o.sem = esem[e]
                        o.val = cnt
            block = st.enter_context(nc.Block())
            engobj = {"pe": "tensor", "act": "scalar", "dve": "vector", "pool": "gpsimd", "sp": "sync"}

            def make(e):
                ops = self.streams[e]

                def body(eng):
                    waited = {}

                    def wait(sem, val):
                        k = id(sem)
                        if waited.get(k, 0) < val:
                            eng.wait_ge(sem, val)
                            waited[k] = val

                    for o in ops:
                        if o.is_dma and o.val > 16:
                            wait(o.sem, o.val - 16)
                        for d in o.deps:
                            wait(d.sem, d.val)
                        if o.fn is None:
                            continue
                        inst = o.fn(eng)
                        if o.is_dma:
                            inst.then_inc(o.sem, 16)
                        elif o.need_inc:
                            inst.then_inc(o.sem, 1)
                    if e == "sp":
                        for d in final_wait_ops:
                            wait(d.sem, d.val)
                return body

            for e in ENGS:
                getattr(block, engobj[e])(make(e))


class Arena:
    def __init__(self, nc, lo, hi):
        self.nc = nc
        self.lo = lo
        self.hi = hi
        self.cur = lo
        self.n = 0

    def alloc(self, shape, dtype, name="t"):
        nbytes = int(np.prod(shape[1:])) * (2 if dtype == BF16 else 4)
        off = (self.cur + 63) // 64 * 64
        if off + nbytes > self.hi:
            raise RuntimeError("SBUF arena overflow %s %d+%d > %d" % (name, off, nbytes, self.hi))
        self.cur = off + nbytes
        self.n += 1
        return self.nc.alloc_sbuf_tensor_at("%s_%d" % (name, self.n), list(shape), dtype, offset=off)

    def mark(self):
        return self.cur

    def reset(self, m):
        self.cur = m


LAM_INIT0 = 0.8 - 0.6 * float(np.exp(-0.3 * 0))


def build_program():
    nc = bass.Bass("TRN2", target_bir_lowering=False)
    P = Prog(nc)
    used_inputs = []

    def din(name, shape, dt=F32):
        used_inputs.append(name)
        return nc.dram_tensor(name, list(shape), dt, kind="ExternalInput").ap()

    def dout(name, shape, dt=F32):
        return nc.dram_tensor(name, list(shape), dt, kind="ExternalOutput").ap()

    def dscr(name, shape, dt=F32):
        return nc.dram_tensor(name, list(shape), dt, kind="Internal").ap()

    x_in = din("x", [NTOK, D])
    cond_T = din("cond_T", [128, KC])
    ada_w = [din("ada_w%d" % l, [36, 128, KC * 512]) for l in range(2)]
    ada_b = din("ada_b", [2, 1, 9 * D])
    normg_T = din("normg_T", [128, 2 * 3 * KC])
    ident_in = din("ident", [128, 128])
    xs = dscr("xs", [NT, KC, 128, TT])

    A = Arena(nc, 16512, 229344 - 64)
    PS = [nc.alloc_psum_tensor("ps%d" % i, [128, 512], F32) for i in range(8)]
    PSB = [P.buf("ps%d" % i) for i in range(8)]

    ident = A.alloc([128, 128], F32, "ident")
    ones_bf = A.alloc([128, 128], BF16, "ones")
    ones_f = A.alloc([1, 128], F32, "onesf")
    condT = A.alloc([128, KC], F32, "condT")
    sc_bf = A.alloc([128, KC], BF16, "scbf")
    normgT = A.alloc([128, 6 * KC], F32, "normgT")
    modT = A.alloc([128, 2 * 144], F32, "modT")
    gsT = A.alloc([128, 2 * 3 * KC], F32, "gsT")
    hgT = A.alloc([128, 2 * 3 * KC], F32, "hgT")
    B_const = P.buf("const")
    B_mod = P.buf("mod")

    P.op("sp", lambda e: e.dma_start(out=ident[:], in_=ident_in), writes=[B_const], dma=True)
    P.op("sp", lambda e: e.dma_start(out=condT[:], in_=cond_T), writes=[B_const], dma=True)
    P.op("sp", lambda e: e.dma_start(out=normgT[:], in_=normg_T), writes=[B_const], dma=True)
    P.op("dve", lambda e: e.memset(ones_bf[:], 1.0), writes=[B_const])
    P.op("dve", lambda e: e.memset(ones_f[:], 1.0), writes=[B_const])
    P.op("act", lambda e: e.activation(out=sc_bf[:], in_=condT[:], func=AF.Silu), reads=[B_const], writes=[B_const])

    def cp(eng, out, in_, reads, writes):
        if eng == "act":
            return P.op("act", lambda e: e.copy(out=out, in_=in_), reads=reads, writes=writes)
        return P.op(eng, lambda e: e.tensor_copy(out=out, in_=in_), reads=reads, writes=writes)

    m0 = A.mark()
    row = A.alloc([1, 9 * D], F32, "modrow")
    brow = A.alloc([1, 9 * D], F32, "biasrow")
    wg = [A.alloc([128, KC, 512], BF16, "adaw") for _ in range(2)]
    B_row = P.buf("row")
    B_brow = P.buf("brow")
    B_wg = [P.buf("wg0"), P.buf("wg1")]
    for l in range(2):
        P.op("sp", lambda e, l=l: e.dma_start(out=brow[:], in_=ada_b[l]), writes=[B_brow], dma=True)
        for g in range(36):
            wb = wg[g % 2]
            P.op("pool", lambda e, wb=wb, l=l, g=g: e.dma_start(
                out=wb[:].rearrange("p k n -> p (k n)"), in_=ada_w[l][g]), writes=[B_wg[g % 2]], dma=True)
            pb = g % 2
            for kc in range(KC):
                P.op("pe", lambda e, wb=wb, kc=kc, pb=pb: e.matmul(
                    PS[pb][0:1, :], lhsT=sc_bf[:, kc:kc + 1], rhs=wb[:, kc, :], start=(kc == 0), stop=(kc == KC - 1)),
                    reads=[B_wg[g % 2], B_const], writes=[PSB[pb]])
            P.op("dve", lambda e, g=g, pb=pb: e.tensor_tensor(
                out=row[0:1, g * 512:(g + 1) * 512], in0=PS[pb][0:1, :], in1=brow[0:1, g * 512:(g + 1) * 512], op=ALU.add),
                reads=[PSB[pb], B_brow], writes=[B_row])
        for j in range(144):
            P.op("pe", lambda e, j=j: e.matmul(PS[2][:, j:j + 1], lhsT=row[0:1, j * 128:(j + 1) * 128],
                                                rhs=ones_f[0:1, 0:1], start=True, stop=True),
                 reads=[B_row, B_const], writes=[PSB[2]])
        P.op("dve", lambda e, l=l: e.tensor_copy(out=modT[:, l * 144:(l + 1) * 144], in_=PS[2][:, 0:144]),
             reads=[PSB[2]], writes=[B_mod])
        for k in range(3):
            o = (l * 3 + k) * KC
            sh = l * 144 + (3 * k) * KC
            P.op("dve", lambda e, o=o, sh=sh: e.scalar_tensor_tensor(
                out=gsT[:, o:o + KC], in0=modT[:, sh + KC:sh + 2 * KC], scalar=1.0, in1=normgT[:, o:o + KC],
                op0=ALU.add, op1=ALU.mult), reads=[B_mod, B_const], writes=[B_mod])
            P.op("dve", lambda e, o=o, sh=sh, k=k: e.tensor_scalar(
                out=hgT[:, o:o + KC], in0=modT[:, sh + 2 * KC:sh + 3 * KC], scalar1=(1.0 if k == 1 else 0.5), scalar2=None,
                op0=ALU.mult), reads=[B_mod], writes=[B_mod])
    P.barrier()
    A.reset(m0)

    def shiftT(l, k):
        o = l * 144 + 3 * k * KC
        return modT[:, o:o + KC]

    def gs(l, k):
        o = (l * 3 + k) * KC
        return gsT[:, o:o + KC]

    def hg(l, k):
        o = (l * 3 + k) * KC
        return hgT[:, o:o + KC]

    seg0 = A.mark()
    xT = [A.alloc([128, TT], F32, "xT") for _ in range(KC)]
    hT = [A.alloc([128, TT], BF16, "hT") for _ in range(KC)]
    B_x = [P.buf("x%d" % i) for i in range(KC)]
    B_h = [P.buf("h%d" % i) for i in range(KC)]
    rstd = A.alloc([128, TT], F32, "rstd")
    B_rstd = P.buf("rstd")
    n_sq = [A.alloc([128, TT], BF16, "sq") for _ in range(2)]
    n_tmp = [A.alloc([128, TT], F32, "nt") for _ in range(2)]
    B_nsq = [P.buf(), P.buf()]
    B_ntmp = [P.buf(), P.buf()]
    seg_top = A.mark()
    NWI = 3
    NWO = 3
    f_actT = [A.alloc([128, TT], BF16, "actT") for _ in range(FH)]
    f_Bact = [P.buf() for _ in range(FH)]
    f_wi = [A.alloc([128, KC, 256], BF16, "wi") for _ in range(NWI)]
    f_Bwi = [P.buf() for _ in range(NWI)]
    f_wo = [A.alloc([128, FH, 128], BF16, "wo") for _ in range(NWO)]
    f_Bwo = [P.buf() for _ in range(NWO)]
    f_sg = [A.alloc([128, SUB], F32, "sg") for _ in range(2)]
    f_Bsg = [P.buf(), P.buf()]
    A.reset(seg_top)
    phase = {"last": "other", "fcnt": 0, "ocnt": 0}

    def begin_other():
        P.barrier()
        phase["last"] = "other"

    def load_x_from_input(t):
        begin_other()
        m = A.mark()
        xin = [A.alloc([128, 4, D], F32, "xin") for _ in range(2)]
        B_xin = [P.buf(), P.buf()]
        for g in range(TT // 512):
            xb = xin[g % 2]
            src = x_in[t * TT + g * 512: t * TT + (g + 1) * 512, :].rearrange("(j p) d -> p j d", p=128)
            P.op("sp", lambda e, xb=xb, src=src: e.dma_start(out=xb[:], in_=src), writes=[B_xin[g % 2]], dma=True)
            for kc in range(KC):
                pb = kc % 2
                for j in range(4):
                    P.op("pe", lambda e, xb=xb, kc=kc, j=j, pb=pb: e.transpose(
                        out=PS[pb][:, j * 128:(j + 1) * 128], in_=xb[:, j, kc * 128:(kc + 1) * 128], identity=ident[:]),
                        reads=[B_xin[g % 2], B_const], writes=[PSB[pb]])
                cp("act" if kc % 2 == 0 else "dve", xT[kc][:, g * 512:(g + 1) * 512], PS[pb][:], [PSB[pb]], [B_x[kc]])
        P.barrier()
        A.reset(m)

    def store_x(t, dst):
        for kc in range(KC):
            P.op("sp", lambda e, kc=kc: e.dma_start(out=dst[t, kc], in_=xT[kc][:]), reads=[B_x[kc]], dma=True)

    def load_x(t, src):
        for kc in range(KC):
            P.op("sp", lambda e, kc=kc: e.dma_start(out=xT[kc][:], in_=src[t, kc]), writes=[B_x[kc]], dma=True)

    def norm_mod(gsv, shv, out_f32=None):
        sq, tmp, B_sq, B_tmp = n_sq, n_tmp, B_nsq, B_ntmp
        for kc in range(KC):
            s = kc % 2
            P.op("act", lambda e, kc=kc, s=s: e.activation(out=sq[s][:], in_=xT[kc][:], func=AF.Square),
                 reads=[B_x[kc]], writes=[B_sq[s]])
            for sub in range(NSUB):
                P.op("pe", lambda e, kc=kc, s=s, sub=sub: e.matmul(
                    PS[6 + sub][:], lhsT=ones_bf[:], rhs=sq[s][:, sub * SUB:(sub + 1) * SUB], start=(kc == 0), stop=(kc == KC - 1)),
                    reads=[B_sq[s], B_const], writes=[PSB[6 + sub]])
        for sub in range(NSUB):
            P.op("act", lambda e, sub=sub: e.activation(out=tmp[0][:, sub * SUB:(sub + 1) * SUB], in_=PS[6 + sub][:], func=AF.Sqrt,
                                                         scale=1.0 / D, bias=EPS), reads=[PSB[6 + sub]], writes=[B_tmp[0]])
        P.op("dve", lambda e: e.reciprocal(out=rstd[:], in_=tmp[0][:]), reads=[B_tmp[0]], writes=[B_rstd])
        for kc in range(KC):
            s = kc % 2
            P.op("dve", lambda e, kc=kc, s=s: e.tensor_tensor(out=tmp[s][:], in0=xT[kc][:], in1=rstd[:], op=ALU.mult),
                 reads=[B_x[kc], B_rstd], writes=[B_tmp[s]])
            if out_f32 is None:
                P.op("act", lambda e, kc=kc, s=s: e.activation(out=hT[kc][:], in_=tmp[s][:], func=AF.Identity,
                                                                scale=gsv[:, kc:kc + 1], bias=shv[:, kc:kc + 1]),
                     reads=[B_tmp[s], B_mod, B_const], writes=[B_h[kc]])
            else:
                P.op("act", lambda e, kc=kc, s=s: e.activation(out=out_f32[kc][:], in_=tmp[s][:], func=AF.Copy,
                                                                scale=gsv[:, kc:kc + 1]),
                     reads=[B_tmp[s], B_mod, B_const], writes=[B_x[kc]])

    w_in_d = {}
    w_out_d = {}

    def ffn(l, i):
        if (l, i) not in w_in_d:
            w_in_d[(l, i)] = din("w_in_%d_%d" % (l, i), [FC, 128, KC * 256])
            w_out_d[(l, i)] = din("w_out_%d_%d" % (l, i), [2, KC, 128, FH * 128])
        w_in = w_in_d[(l, i)]
        w_out = w_out_d[(l, i)]
        k = 0 if i == 0 else 2
        if phase["last"] != "ffn":
            P.barrier()
        phase["last"] = "ffn"
        norm_mod(gs(l, k), shiftT(l, k))
        actT, B_act, wi, B_wi, wo, B_wo, sg, B_sg = f_actT, f_Bact, f_wi, f_Bwi, f_wo, f_Bwo, f_sg, f_Bsg
        hgv = hg(l, k)
        cnt = phase["fcnt"]
        for hf in range(2):
            for fi in range(FH):
                f = hf * FH + fi
                w = wi[f % NWI]
                bw = B_wi[f % NWI]
                P.op("pool", lambda e, w=w, f=f: e.dma_start(out=w[:].rearrange("p k n -> p (k n)"), in_=w_in[f]),
                     writes=[bw], dma=True)
                for sub in range(NSUB):
                    pg = (cnt % 2) * 2
                    pu = pg + 1
                    for kc in range(KC):
                        P.op("pe", lambda e, w=w, kc=kc, sub=sub, pg=pg: e.matmul(
                            PS[pg][:], lhsT=w[:, kc, 0:128], rhs=hT[kc][:, sub * SUB:(sub + 1) * SUB],
                            start=(kc == 0), stop=(kc == KC - 1)), reads=[bw, B_h[kc]], writes=[PSB[pg]])
                    for kc in range(KC):
                        P.op("pe", lambda e, w=w, kc=kc, sub=sub, pu=pu: e.matmul(
                            PS[pu][:], lhsT=w[:, kc, 128:256], rhs=hT[kc][:, sub * SUB:(sub + 1) * SUB],
                            start=(kc == 0), stop=(kc == KC - 1)), reads=[bw, B_h[kc]], writes=[PSB[pu]])
                    s = cnt % 2
                    P.op("act", lambda e, s=s, pg=pg: e.activation(out=sg[s][:], in_=PS[pg][:], func=AF.Silu),
                         reads=[PSB[pg]], writes=[B_sg[s]])
                    P.op("dve", lambda e, s=s, pu=pu, fi=fi, sub=sub: e.tensor_tensor(
                        out=actT[fi][:, sub * SUB:(sub + 1) * SUB], in0=sg[s][:], in1=PS[pu][:], op=ALU.mult),
                        reads=[B_sg[s], PSB[pu]], writes=[B_act[fi]])
                    cnt += 1
            for d in range(KC):
                idx = phase["ocnt"]
                phase["ocnt"] += 1
                w = wo[idx % NWO]
                bw = B_wo[idx % NWO]
                P.op("pool", lambda e, w=w, hf=hf, d=d: e.dma_start(out=w[:].rearrange("p k n -> p (k n)"), in_=w_out[hf, d]),
                     writes=[bw], dma=True)
                for sub in range(NSUB):
                    po = 4 + (idx * NSUB + sub) % 2
                    for fi in range(FH):
                        P.op("pe", lambda e, w=w, fi=fi, sub=sub, po=po: e.matmul(
                            PS[po][:], lhsT=w[:, fi, :], rhs=actT[fi][:, sub * SUB:(sub + 1) * SUB],
                            start=(fi == 0), stop=(fi == FH - 1)), reads=[bw, B_act[fi]], writes=[PSB[po]])
                    P.op("dve", lambda e, d=d, sub=sub, po=po: e.scalar_tensor_tensor(
                        out=xT[d][:, sub * SUB:(sub + 1) * SUB], in0=PS[po][:], scalar=hgv[:, d:d + 1],
                        in1=xT[d][:, sub * SUB:(sub + 1) * SUB], op0=ALU.mult, op1=ALU.add),
                        reads=[PSB[po], B_x[d], B_mod], writes=[B_x[d]])
        phase["fcnt"] = cnt

    def linear_fmaj(wsrc, nchunks, ncol, consumer, banks=(0, 1), nbuf=3, msplit=None):
        m = A.mark()
        wt = [A.alloc([128, KC, ncol], BF16, "wf") for _ in range(nbuf)]
        B_wt = [P.buf() for _ in range(nbuf)]
        cnt = 0
        for c in range(nchunks):
            w = wt[c % nbuf]
            bw = B_wt[c % nbuf]
            P.op("pool", lambda e, w=w, c=c: e.dma_start(out=w[:].rearrange("p k n -> p (k n)"), in_=wsrc(c)),
                 writes=[bw], dma=True)
            for sub in range(NSUB):
                parts = [(0, ncol)] if msplit is None else msplit
                for (m0_, m1_) in parts:
                    pb = banks[cnt % len(banks)]
                    cnt += 1
                    for kc in range(KC):
                        P.op("pe", lambda e, w=w, kc=kc, sub=sub, pb=pb, m0_=m0_, m1_=m1_: e.matmul(
                            PS[pb][0:m1_ - m0_, :], lhsT=w[:, kc, m0_:m1_], rhs=hT[kc][:, sub * SUB:(sub + 1) * SUB],
                            start=(kc == 0), stop=(kc == KC - 1)), reads=[bw, B_h[kc]], writes=[PSB[pb]])
                    consumer(c, sub, pb, m0_)
        return m

    def linear_tmaj(wsrc, ngroups, consumer, banks=(4, 5)):
        m = A.mark()
        wt = [A.alloc([128, KC, 512], BF16, "wtm") for _ in range(2)]
        B_wt = [P.buf() for _ in range(2)]
        cnt = 0
        for g in range(ngroups):
            w = wt[g % 2]
            bw = B_wt[g % 2]
            P.op("pool", lambda e, w=w, g=g: e.dma_start(out=w[:].rearrange("p k n -> p (k n)"), in_=wsrc(g)),
                 writes=[bw], dma=True)
            for tb in range(TT // 128):
                pb = banks[cnt % len(banks)]
                cnt += 1
                for kc in range(KC):
                    P.op("pe", lambda e, w=w, kc=kc, tb=tb, pb=pb: e.matmul(
                        PS[pb][:], lhsT=hT[kc][:, tb * 128:(tb + 1) * 128], rhs=w[:, kc, :],
                        start=(kc == 0), stop=(kc == KC - 1)), reads=[bw, B_h[kc]], writes=[PSB[pb]])
                consumer(g, tb, pb)
        return m

    class Rot:
        def __init__(self, shape, dt, n, name):
            self.t = [A.alloc(shape, dt, name) for _ in range(n)]
            self.b = [P.buf(name) for _ in range(n)]
            self.i = 0

        def next(self):
            k = self.i % len(self.t)
            self.i += 1
            return self.t[k], self.b[k]

    def rope_consumer(pb, tok0, cosT, sinT, permT, B_tab, dst_ap_fn, R):
        qs, bqs = R["qs"].next()
        cp("act", qs[:], PS[pb][:], [PSB[pb]], [bqs])
        pr = R["pbank"][R["pcnt"][0] % 2]
        R["pcnt"][0] += 1
        P.op("pe", lambda e: e.matmul(PS[pr][:], lhsT=permT[:], rhs=qs[:], start=True, stop=True),
             reads=[bqs, B_tab], writes=[PSB[pr]])
        t1, bt1 = R["t1"].next()
        P.op("dve", lambda e: e.tensor_tensor(out=t1[:], in0=qs[:], in1=cosT[:, tok0:tok0 + SUB], op=ALU.mult),
             reads=[bqs, B_tab], writes=[bt1])
        t2, bt2 = R["t2"].next()
        P.op("dve", lambda e: e.tensor_tensor(out=t2[:], in0=PS[pr][:], in1=sinT[:, tok0:tok0 + SUB], op=ALU.mult),
             reads=[PSB[pr], B_tab], writes=[bt2])
        ob, bob = R["ob"].next()
        P.op("dve", lambda e: e.tensor_tensor(out=ob[:], in0=t1[:], in1=t2[:], op=ALU.add),
             reads=[bt1, bt2], writes=[bob])
        dst, src = dst_ap_fn(ob)
        P.op("sp", lambda e: e.dma_start(out=dst, in_=src), reads=[bob], dma=True)

    ab_wF = din("ab_wF", [32, 128, KC * 128])
    ab_wlow = din("ab_wlow", [128, KC * 32])
    ab_wT = din("ab_wT", [7, 128, KC * 512])
    cosA_in = din("cosA", [128, NTOK])
    sinA_in = din("sinA", [128, NTOK])
    permA_in = din("permA", [128, 128])
    QA = dscr("QA", [8, 128, NTOK], BF16)
    KA = dscr("KA", [8, 128, NTOK], BF16)
    VA2 = dscr("VA2", [8, 128, 16, 128])
    BQ = dscr("BQ", [4, 128, NTOK])
    BKF = dscr("BKF", [4, 128, NTOK])
    BKT = dscr("BKT", [16, 128, 512])
    BVT = dscr("BVT", [16, 128, 1024], BF16)
    BR = dscr("BR", [8, 128, NTOK])
    LOW = dscr("LOW", [2, 16, NTOK])
    MO = dscr("MO", [16, 128, NTOK], BF16)
    o_ak = dout("o_ak", [NTOK, 1024])
    o_av = dout("o_av", [NTOK, 1024])
    out_ops = []

    def proj_L0(t):
        begin_other()
        m = A.mark()
        cosT = A.alloc([128, TT], F32, "cos")
        sinT = A.alloc([128, TT], F32, "sin")
        permT = A.alloc([128, 128], F32, "perm")
        B_tab = P.buf("tab")
        P.op("sp", lambda e: e.dma_start(out=cosT[:], in_=cosA_in[:, t * TT:(t + 1) * TT]), writes=[B_tab], dma=True)
        P.op("sp", lambda e: e.dma_start(out=sinT[:], in_=sinA_in[:, t * TT:(t + 1) * TT]), writes=[B_tab], dma=True)
        P.op("sp", lambda e: e.dma_start(out=permT[:], in_=permA_in), writes=[B_tab], dma=True)
        R = {"qs": Rot([128, SUB], F32, 2, "qs"), "t1": Rot([128, SUB], F32, 2, "t1"), "t2": Rot([128, SUB], F32, 2, "t2"),
             "ob": Rot([128, SUB], BF16, 3, "ob"), "pbank": (2, 3), "pcnt": [0]}
        stF = Rot([128, SUB], F32, 3, "stF")

        def consF(c, sub, pb, m0_):
            tok0 = sub * SUB
            g0 = t * TT + tok0
            if c < 16:
                dst = (QA if c < 8 else KA)[c % 8][:, g0:g0 + SUB]
                rope_consumer(pb, tok0, cosT, sinT, permT, B_tab, lambda ob: (dst, ob[:]), R)
            else:
                if c < 20:
                    dst = BQ[c - 16][:, g0:g0 + SUB]
                elif c < 24:
                    dst = BKF[c - 20][:, g0:g0 + SUB]
                else:
                    dst = BR[c - 24][:, g0:g0 + SUB]
                s_, bs_ = stF.next()
                if c >= 24:
                    P.op("act", lambda e: e.activation(out=s_[:], in_=PS[pb][:], func=AF.Silu), reads=[PSB[pb]], writes=[bs_])
                else:
                    cp("act", s_[:], PS[pb][:], [PSB[pb]], [bs_])
                P.op("sp", lambda e: e.dma_start(out=dst, in_=s_[:]), reads=[bs_], dma=True)

        SUBS = int(os.environ.get("MK_SUB", "9"))
        if SUBS == 5:
            stT = Rot([128, 512], F32, 3, "stT")
            MKT = 1

            def consT0(g, tb, pb):
                tg = t * (TT // 128) + tb
                r0 = tg * 128
                s_, bs_ = stT.next()
                cp("act", s_[:], PS[pb][:], [PSB[pb]], [bs_])
                dst = o_ak[r0:r0 + 128, 0:512]
                out_ops.append(P.op("sp", lambda e: e.dma_start(out=dst, in_=s_[:]), reads=[bs_], dma=True))
            linear_tmaj(lambda g: ab_wT[g], 1, consT0)
            P.barrier()
            A.reset(m)
            return
        if SUBS >= 1:
            linear_fmaj(lambda c: ab_wF[c], 16 if SUBS == 1 else 32, 128, consF)
        stL = Rot([16, SUB], F32, 2, "stL")

        def consL(c, sub, pb, m0_):
            g0 = t * TT + sub * SUB
            s_, bs_ = stL.next()
            cp("dve", s_[:], PS[pb][0:16, :], [PSB[pb]], [bs_])
            P.op("sp", lambda e: e.dma_start(out=LOW[m0_ // 16][:, g0:g0 + SUB], in_=s_[:]), reads=[bs_], dma=True)

        if SUBS >= 3:
            linear_fmaj(lambda c: ab_wlow, 1, 32, consL, msplit=[(0, 16), (16, 32)], nbuf=1)
        stT = Rot([128, 512], F32, 3, "stT")
        stTb = Rot([128, 512], BF16, 3, "stTb")

        MKT = int(os.environ.get("MK_T", "15"))

        def consT(g, tb, pb):
            tg = t * (TT // 128) + tb
            r0 = tg * 128
            if MKT == 0:
                s_, bs_ = stT.next()
                cp("act", s_[:], PS[pb][:], [PSB[pb]], [bs_])
                return
            if g < 4:
                s_, bs_ = stT.next()
                cp("act", s_[:], PS[pb][:], [PSB[pb]], [bs_])
                dst = (o_ak if g < 2 else o_av)[r0:r0 + 128, (g % 2) * 512:(g % 2 + 1) * 512]
                if MKT & 1:
                    out_ops.append(P.op("sp", lambda e: e.dma_start(out=dst, in_=s_[:]), reads=[bs_], dma=True))
                if g >= 2 and (MKT & 2):
                    P.op("sp", lambda e: e.dma_start(out=VA2[(g - 2) * 4:(g - 1) * 4, :, tg, :].rearrange("h p d -> p h d"),
                                                     in_=s_[:].rearrange("p (h d) -> p h d", d=128)), reads=[bs_], dma=True)
            elif g == 4 and not (MKT & 4):
                return
            elif g > 4 and not (MKT & 8):
                return
            elif g == 4:
                s_, bs_ = stT.next()
                cp("act", s_[:], PS[pb][:], [PSB[pb]], [bs_])
                P.op("sp", lambda e: e.dma_start(out=BKT[tg], in_=s_[:]), reads=[bs_], dma=True)
            else:
                sb_, bsb_ = stTb.next()
                cp("dve", sb_[:], PS[pb][:], [PSB[pb]], [bsb_])
                P.op("sp", lambda e: e.dma_start(out=BVT[tg][:, (g - 5) * 512:(g - 4) * 512], in_=sb_[:]), reads=[bsb_], dma=True)

        if os.environ.get("MK_BAR"):
            P.barrier()
        if SUBS >= 4:
            linear_tmaj(lambda g: ab_wT[g], int(os.environ.get("MK_NG", "7")), consT)
        P.barrier()
        A.reset(m)

    def attn_L0():
        alpha_in = din("alpha17", [2, 17, 512])
        Utri_in = din("Utri", [2, 128, 128])
        gmask_in = din("gmask", [2, 128, 128])
        keep_in = din("keep", [128, 1])
        st_in = [din("st_f", [4, 128, 256]), din("st_b", [4, 128, 256])]
        o_st = [dout("o_bf", [8, 4, 128, 256]), dout("o_bb", [8, 4, 128, 256])]
        bnormg_in = din("bnormg_T", [128, 2])
        m = A.mark()
        oB = [A.alloc([128, 2, NTOK], F32, "oB") for _ in range(4)]
        B_oB = [P.buf("oB%d" % h) for h in range(4)]
        S = [[A.alloc([128, 256], F32, "S") for _ in range(2)] for _ in range(4)]
        Sb = [[A.alloc([128, 256], BF16, "Sb") for _ in range(2)] for _ in range(4)]
        B_S = [[P.buf() for _ in range(2)] for _ in range(4)]
        B_Sb = [[P.buf() for _ in range(2)] for _ in range(4)]
        aw = A.alloc([17, 2, 512], F32, "aw")
        Ut = A.alloc([128, 2, 128], F32, "Ut")
        gm = A.alloc([128, 2, 128], F32, "gm")
        keep = A.alloc([128, 1], F32, "keep")
        B_gc = P.buf("gconst")
        for d in range(2):
            P.op("sp", lambda e, d=d: e.dma_start(out=aw[:, d, :], in_=alpha_in[d]), writes=[B_gc], dma=True)
            P.op("sp", lambda e, d=d: e.dma_start(out=Ut[:, d, :], in_=Utri_in[d]), writes=[B_gc], dma=True)
            P.op("sp", lambda e, d=d: e.dma_start(out=gm[:, d, :], in_=gmask_in[d]), writes=[B_gc], dma=True)
        P.op("sp", lambda e: e.dma_start(out=keep[:], in_=keep_in), writes=[B_gc], dma=True)
        for h in range(4):
            for d in range(2):
                P.op("sp", lambda e, h=h, d=d: e.dma_start(out=S[h][d][:], in_=st_in[d][h]), writes=[B_S[h][d]], dma=True)
                cp("dve", Sb[h][d][:], S[h][d][:], [B_S[h][d]], [B_Sb[h][d]])
        lowd = Rot([17, 128], F32, 2, "lowd")
        for k in range(2):
            P.op("dve", lambda e, k=k: e.memset(lowd.t[k][:], 1.0), writes=[lowd.b[k]])
        e1 = Rot([128, 512], F32, 2, "e1")
        lnv = Rot([128, 512], F32, 2, "lnv")
        Ep = Rot([128, 512], F32, 2, "Ep")
        Em = Rot([128, 512], F32, 2, "Em")
        EmT = Rot([128, 512], F32, 2, "EmT")
        qF = Rot([128, 4, 128], F32, 2, "qF")
        kF = Rot([128, 4, 128], F32, 2, "kF")
        kT = Rot([128, 512], F32, 2, "kT")
        vT = Rot([128, 1024], BF16, 2, "vT")
        qe = Rot([128, 512], BF16, 2, "qe")
        ke = Rot([128, 512], BF16, 2, "ke")
        keT = Rot([128, 512], BF16, 2, "keT")
        STm = Rot([128, 128], BF16, 8, "STm")
        tmpS = Rot([128, 256], F32, 8, "tmpS")
        Sout = Rot([128, 256], F32, 8, "Sout")
        pcnt = {"a": 0, "st": 0, "o": 0}
        MKA = int(os.environ.get("MK_A", "7"))
        MKG = int(os.environ.get("MK_G", "99"))
        for i in range(16 if (MKA & 1) else 0):
            for d in range(2):
                b = i if d == 0 else 15 - i
                tok = slice(b * 128, (b + 1) * 128)
                lw, blw = lowd.next()
                P.op("sp", lambda e, lw=lw, d=d, tok=tok: e.dma_start(out=lw[0:16, :], in_=LOW[d][:, tok]), writes=[blw], dma=True)
                qf, bqf = qF.next()
                P.op("sp", lambda e, qf=qf, tok=tok: e.dma_start(out=qf[:], in_=BQ[:, :, tok].rearrange("h p t -> p h t")),
                     writes=[bqf], dma=True)
                kf, bkf = kF.next()
                P.op("sp", lambda e, kf=kf, tok=tok: e.dma_start(out=kf[:], in_=BKF[:, :, tok].rearrange("h p t -> p h t")),
                     writes=[bkf], dma=True)
                kt, bkt = kT.next()
                P.op("sp", lambda e, kt=kt, b=b: e.dma_start(out=kt[:], in_=BKT[b]), writes=[bkt], dma=True)
                vt, bvt = vT.next()
                P.op("sp", lambda e, vt=vt, b=b: e.dma_start(out=vt[:], in_=BVT[b]), writes=[bvt], dma=True)
                pa = pcnt["a"] % 2
                pcnt["a"] += 1
                P.op("pe", lambda e, lw=lw, d=d, pa=pa: e.matmul(PS[pa][:], lhsT=lw[0:17, :], rhs=aw[0:17, d, :], start=True, stop=True),
                     reads=[blw, B_gc], writes=[PSB[pa]])
                x1, bx1 = e1.next()
                P.op("act", lambda e, x1=x1, pa=pa: e.activation(out=x1[:], in_=PS[pa][:], func=AF.Exp, scale=-1.0),
                     reads=[PSB[pa]], writes=[bx1])
                lv, blv = lnv.next()
                P.op("act", lambda e, x1=x1, lv=lv: e.activation(out=lv[:], in_=x1[:], func=AF.Ln, bias=1.0),
                     reads=[bx1], writes=[blv])
                P.op("pe", lambda e, lv=lv, d=d, pa=pa: e.matmul(PS[pa][:], lhsT=Ut[:, d, :], rhs=lv[:], start=True, stop=True),
                     reads=[blv, B_gc], writes=[PSB[pa]])
                emt, bemt = EmT.next()
                P.op("act", lambda e, emt=emt, pa=pa: e.activation(out=emt[:], in_=PS[pa][:], func=AF.Exp, scale=-1.0),
                     reads=[PSB[pa]], writes=[bemt])
                for h in range(4):
                    P.op("pe", lambda e, lv=lv, d=d, h=h: e.matmul(PS[2][:, h * 128:(h + 1) * 128], lhsT=lv[:, h * 128:(h + 1) * 128],
                                                                  rhs=Ut[:, d, :], start=True, stop=True),
                         reads=[blv, B_gc], writes=[PSB[2]])
                ep, bep = Ep.next()
                P.op("act", lambda e, ep=ep: e.activation(out=ep[:], in_=PS[2][:], func=AF.Exp), reads=[PSB[2]], writes=[bep])
                em, bem = Em.next()
                P.op("act", lambda e, em=em: e.activation(out=em[:], in_=PS[2][:], func=AF.Exp, scale=-1.0), reads=[PSB[2]], writes=[bem])
                q_, bq_ = qe.next()
                P.op("dve", lambda e, q_=q_, qf=qf, ep=ep: e.scalar_tensor_tensor(
                    out=q_[:], in0=qf[:].rearrange("p h t -> p (h t)"), scalar=float(128 ** -0.5), in1=ep[:], op0=ALU.mult, op1=ALU.mult),
                    reads=[bqf, bep], writes=[bq_])
                k_, bk_ = ke.next()
                P.op("dve", lambda e, k_=k_, kf=kf, em=em: e.tensor_tensor(
                    out=k_[:], in0=kf[:].rearrange("p h t -> p (h t)"), in1=em[:], op=ALU.mult), reads=[bkf, bem], writes=[bk_])
                kt_, bkt_ = keT.next()
                P.op("dve", lambda e, kt_=kt_, kt=kt, emt=emt: e.tensor_tensor(out=kt_[:], in0=kt[:], in1=emt[:], op=ALU.mult),
                     reads=[bkt, bemt], writes=[bkt_])
                HS = [slice(h * 128, (h + 1) * 128) for h in range(4)]
                for h in range(4):
                    P.op("pe", lambda e, k_=k_, q_=q_, hs=HS[h]: e.matmul(PS[3][:, hs], lhsT=k_[:, hs], rhs=q_[:, hs], start=True, stop=True),
                         reads=[bk_, bq_], writes=[PSB[3]])
                sms = []
                for h in range(4):
                    sm, bsm = STm.next()
                    P.op("dve", lambda e, sm=sm, d=d, hs=HS[h]: e.tensor_tensor(out=sm[:], in0=PS[3][:, hs], in1=gm[:, d, :], op=ALU.mult),
                         reads=[PSB[3], B_gc], writes=[bsm])
                    sms.append((sm, bsm))
                OB = (5, 5, 6, 6)
                for h in range(4):
                    sm, bsm = sms[h]
                    po = OB[h]
                    o0 = (h % 2) * 256
                    for c in range(2):
                        P.op("pe", lambda e, vt=vt, sm=sm, h=h, c=c, po=po, o0=o0: e.matmul(
                            PS[po][:, o0 + c * 128:o0 + (c + 1) * 128], lhsT=vt[:, h * 256 + c * 128:h * 256 + (c + 1) * 128], rhs=sm[:],
                            start=True, stop=False), reads=[bvt, bsm], writes=[PSB[po]])
                        P.op("pe", lambda e, q_=q_, h=h, d=d, c=c, hs=HS[h], po=po, o0=o0: e.matmul(
                            PS[po][:, o0 + c * 128:o0 + (c + 1) * 128], lhsT=Sb[h][d][:, c * 128:(c + 1) * 128], rhs=q_[:, hs],
                            start=False, stop=True), reads=[B_Sb[h][d], bq_], writes=[PSB[po]])
                first = (d == 0 and b <= 7) or (d == 1 and b >= 8)
                for h in range(4):
                    po = OB[h]
                    o0 = (h % 2) * 256
                    dsto = oB[h][:, :, tok]
                    srco = PS[po][:, o0:o0 + 256].rearrange("p (c t) -> p c t", c=2)
                    if first:
                        cp("act" if po == 5 else "dve", dsto, srco, [PSB[po]], [B_oB[h]])
                    else:
                        P.op("dve", lambda e, dsto=dsto, srco=srco: e.tensor_tensor(out=dsto, in0=srco, in1=dsto, op=ALU.add),
                             reads=[PSB[po], B_oB[h]], writes=[B_oB[h]])
                DB = (7, 7, 4, 4)
                for h in range(4):
                    pdS = DB[h]
                    o0 = (h % 2) * 256
                    P.op("pe", lambda e, kt_=kt_, vt=vt, h=h, hs=HS[h], pdS=pdS, o0=o0: e.matmul(
                        PS[pdS][:, o0:o0 + 256], lhsT=kt_[:, hs], rhs=vt[:, h * 256:(h + 1) * 256], start=True, stop=True),
                        reads=[bkt_, bvt], writes=[PSB[pdS]])
                seq_end = (b % 2 == 1) if d == 0 else (b % 2 == 0)
                for h in range(4):
                    pdS = DB[h]
                    o0 = (h % 2) * 256
                    ts_, bts_ = tmpS.next()
                    P.op("dve", lambda e, ts_=ts_, h=h, d=d, pdS=pdS, o0=o0: e.tensor_tensor(out=ts_[:], in0=PS[pdS][:, o0:o0 + 256], in1=S[h][d][:], op=ALU.add),
                         reads=[PSB[pdS], B_S[h][d]], writes=[bts_])
                    col = h * 128 + (127 if d == 0 else 0)
                    al = ep[:, col:col + 1]
                    if not seq_end:
                        P.op("dve", lambda e, ts_=ts_, h=h, d=d, al=al: e.tensor_scalar(out=S[h][d][:], in0=ts_[:], scalar1=al, scalar2=None, op0=ALU.mult),
                             reads=[bts_, bep], writes=[B_S[h][d]])
                        P.op("act", lambda e, ts_=ts_, h=h, d=d, al=al: e.activation(out=Sb[h][d][:], in_=ts_[:], func=AF.Copy, scale=al),
                             reads=[bts_, bep], writes=[B_Sb[h][d]])
                    else:
                        so, bso = Sout.next()
                        P.op("dve", lambda e, ts_=ts_, so=so, al=al: e.tensor_scalar(out=so[:], in0=ts_[:], scalar1=al, scalar2=None, op0=ALU.mult),
                             reads=[bts_, bep], writes=[bso])
                        out_ops.append(P.op("sp", lambda e, so=so, d=d, b=b, h=h: e.dma_start(out=o_st[d][b // 2, h], in_=so[:]), reads=[bso], dma=True))
                        P.op("dve", lambda e, so=so, h=h, d=d: e.tensor_scalar(out=S[h][d][:], in0=so[:], scalar1=keep[:, 0:1], scalar2=None, op0=ALU.mult),
                             reads=[bso, B_gc], writes=[B_S[h][d]])
                        P.op("act", lambda e, so=so, h=h, d=d: e.activation(out=Sb[h][d][:], in_=so[:], func=AF.Copy, scale=keep[:, 0:1]),
                             reads=[bso, B_gc], writes=[B_Sb[h][d]])
        bng = A.alloc([128, 2], F32, "bng")
        P.op("sp", lambda e: e.dma_start(out=bng[:], in_=bnormg_in), writes=[B_gc], dma=True)
        sqb = Rot([128, SUB], BF16, 2, "sqb")
        rt = Rot([128, SUB], F32, 2, "rt")
        rr = Rot([128, SUB], F32, 2, "rr")
        brs = Rot([128, SUB], F32, 3, "brs")
        tn = Rot([128, SUB], F32, 2, "tn")
        mob = Rot([128, SUB], BF16, 3, "mob")
        for h in range(4 if (MKA & 2) else 0):
            for sub in range(NTOK // SUB):
                ts = slice(sub * SUB, (sub + 1) * SUB)
                pb = sub % 2
                for c in range(2):
                    s_, bs_ = sqb.next()
                    P.op("act", lambda e, s_=s_, h=h, c=c, ts=ts: e.activation(out=s_[:], in_=oB[h][:, c, ts], func=AF.Square),
                         reads=[B_oB[h]], writes=[bs_])
                    P.op("pe", lambda e, s_=s_, c=c, pb=pb: e.matmul(PS[pb][:], lhsT=ones_bf[:], rhs=s_[:], start=(c == 0), stop=(c == 1)),
                         reads=[bs_, B_const], writes=[PSB[pb]])
                r1, br1 = rt.next()
                P.op("act", lambda e, r1=r1, pb=pb: e.activation(out=r1[:], in_=PS[pb][:], func=AF.Sqrt, scale=1.0 / 256, bias=EPS),
                     reads=[PSB[pb]], writes=[br1])
                r2, br2 = rr.next()
                P.op("dve", lambda e, r1=r1, r2=r2: e.reciprocal(out=r2[:], in_=r1[:]), reads=[br1], writes=[br2])
                for c in range(2):
                    br_, bbr_ = brs.next()
                    P.op("sp", lambda e, br_=br_, h=h, c=c, ts=ts: e.dma_start(out=br_[:], in_=BR[h * 2 + c][:, ts]), writes=[bbr_], dma=True)
                    t_, bt_ = tn.next()
                    P.op("dve", lambda e, t_=t_, h=h, c=c, ts=ts, r2=r2: e.tensor_tensor(out=t_[:], in0=oB[h][:, c, ts], in1=r2[:], op=ALU.mult),
                         reads=[B_oB[h], br2], writes=[bt_])
                    mo_, bmo_ = mob.next()
                    P.op("dve", lambda e, mo_=mo_, t_=t_, c=c, br_=br_: e.scalar_tensor_tensor(
                        out=mo_[:], in0=t_[:], scalar=bng[:, c:c + 1], in1=br_[:], op0=ALU.mult, op1=ALU.mult),
                        reads=[bt_, bbr_, B_gc], writes=[bmo_])
                    P.op("sp", lambda e, mo_=mo_, h=h, c=c, ts=ts: e.dma_start(out=MO[8 + h * 2 + c][:, ts], in_=mo_[:]), reads=[bmo_], dma=True)
        P.barrier()
        A.reset(m)

        ctx_ak = din("ctx_ak", [8, 512, 128])
        ctx_av = din("ctx_av", [8, 512, 128])
        biasA_in = din("biasA", [128, 20 * 8])
        lam_in = din("a_lambda", [1, 256])
        subg_in = din("subg_T", [128, 1])
        m = A.mark()
        biasA = A.alloc([128, 20, 8], F32, "biasA")
        lamr = A.alloc([1, 256], F32, "lamr")
        lamp = A.alloc([1, 128], F32, "lamp")
        lams = A.alloc([1, 8], F32, "lams")
        neglam = A.alloc([128, 1], F32, "neglam")
        sgl = A.alloc([128, 1], F32, "sgl")
        B_ac = P.buf("aconst")
        P.op("sp", lambda e: e.dma_start(out=biasA[:].rearrange("p a b -> p (a b)"), in_=biasA_in), writes=[B_ac], dma=True)
        P.op("sp", lambda e: e.dma_start(out=lamr[:], in_=lam_in), writes=[B_ac], dma=True)
        P.op("sp", lambda e: e.dma_start(out=sgl[:], in_=subg_in), writes=[B_ac], dma=True)
        P.op("dve", lambda e: e.tensor_tensor(out=lamp[0:1, 0:64], in0=lamr[0:1, 0:64], in1=lamr[0:1, 64:128], op=ALU.mult), reads=[B_ac], writes=[B_ac])
        P.op("dve", lambda e: e.tensor_tensor(out=lamp[0:1, 64:128], in0=lamr[0:1, 128:192], in1=lamr[0:1, 192:256], op=ALU.mult), reads=[B_ac], writes=[B_ac])
        P.op("dve", lambda e: e.reduce_sum(out=lams[0:1, 0:2], in_=lamp[0:1, :].rearrange("p (a b) -> p a b", a=2), axis=AX.X), reads=[B_ac], writes=[B_ac])
        P.op("act", lambda e: e.activation(out=lams[0:1, 2:4], in_=lams[0:1, 0:2], func=AF.Exp), reads=[B_ac], writes=[B_ac])
        P.op("dve", lambda e: e.scalar_tensor_tensor(out=lams[0:1, 4:5], in0=lams[0:1, 3:4], scalar=-LAM_INIT0, in1=lams[0:1, 2:3],
                                                     op0=ALU.add, op1=ALU.subtract), reads=[B_ac], writes=[B_ac])
        P.op("pe", lambda e: e.matmul(PS[7][:, 0:1], lhsT=ones_f[0:1, :], rhs=lams[0:1, 4:5], start=True, stop=True),
             reads=[B_ac, B_const], writes=[PSB[7]])
        cp("dve", neglam[:], PS[7][:, 0:1], [PSB[7]], [B_ac])
        P.op("dve", lambda e: e.tensor_scalar(out=sgl[:], in0=sgl[:], scalar1=1.0 - LAM_INIT0, scalar2=None, op0=ALU.mult), reads=[B_ac], writes=[B_ac])

        QAh = Rot([128, 8, 2, 256], BF16, 2, "QAh")
        for k_ in range(2):
            P.op("dve", lambda e, k_=k_: e.memset(QAh.t[k_][:], 0.0), writes=[QAh.b[k_]])
        KAh = Rot([128, 512 + NTOK], BF16, 2, "KAh")
        VAh = Rot([128, 20, 128], BF16, 2, "VAh")
        ckt = Rot([128, 4, 128], F32, 2, "ckt")
        Pb = Rot([128, 512], BF16, 5, "Pb")
        rden = Rot([128, 512], F32, 2, "rden")
        on = Rot([128, 512], F32, 2, "on")
        ao = Rot([128, 256], F32, 2, "ao")
        sqa = Rot([128, 256], BF16, 2, "sqa")
        r1a = Rot([128, 256], F32, 2, "r1a")
        r2a = Rot([128, 256], F32, 2, "r2a")
        ta = Rot([128, 256], F32, 2, "ta")
        oa = Rot([128, 256], BF16, 3, "oa")
        scnt = 0
        for h in range(8 if (MKA & 4) else 0):
            qh, bqh = QAh.next()
            kh, bkh = KAh.next()
            vh, bvh = VAh.next()
            ck, bck = ckt.next()
            MKL = int(os.environ.get("MK_L", "31"))
            if MKL & 1:
                P.op("sp", lambda e, qh=qh, h=h: e.dma_start(out=qh[0:64, :, 0, :], in_=QA[h][0:64, :].rearrange("p (a b) -> p a b", b=256)), writes=[bqh], dma=True)
                P.op("sp", lambda e, qh=qh, h=h: e.dma_start(out=qh[64:128, :, 1, :], in_=QA[h][64:128, :].rearrange("p (a b) -> p a b", b=256)), writes=[bqh], dma=True)
            if MKL & 2:
                P.op("sp", lambda e, kh=kh, h=h: e.dma_start(out=kh[:, 512:], in_=KA[h]), writes=[bkh], dma=True)
            if MKL & 4:
                P.op("sp", lambda e, ck=ck, h=h: e.dma_start(out=ck[:], in_=ctx_ak[h].rearrange("(c p) d -> p c d", p=128)), writes=[bck], dma=True)
                for c in range(4):
                    P.op("pe", lambda e, ck=ck, c=c: e.transpose(out=PS[7][:, c * 128:(c + 1) * 128], in_=ck[:, c, :], identity=ident[:]),
                         reads=[bck, B_const], writes=[PSB[7]])
                cp("dve", kh[:, 0:512], PS[7][:], [PSB[7]], [bkh])
            if MKL & 8:
                P.op("pool", lambda e, vh=vh, h=h: e.dma_start(out=vh[:, 0:4, :], in_=ctx_av[h].rearrange("(c p) d -> p c d", p=128)), writes=[bvh], dma=True)
            if MKL & 16:
                P.op("pool", lambda e, vh=vh, h=h: e.dma_start(out=vh[:, 4:20, :], in_=VA2[h]), writes=[bvh], dma=True)
            MKD = int(os.environ.get("MK_D", "9"))
            SB = (0, 1, 6)
            steps = [(qt, kc) for qt in range(8) for kc in range(20)]
            pend_pb = {}

            def emit_S(j, kh=kh, qh=qh, bkh=bkh, bqh=bqh):
                qt, kc = steps[j]
                ps_ = SB[j % 3]
                ks_ = slice(kc * 128, (kc + 1) * 128)
                P.op("pe", lambda e: e.matmul(PS[ps_][:], lhsT=kh[:, ks_], rhs=qh[:, qt].rearrange("p a b -> p (a b)"), start=True, stop=True),
                     reads=[bkh, bqh], writes=[PSB[ps_]])

            def emit_exp(j):
                qt, kc = steps[j]
                ps_ = SB[j % 3]
                pb_, bpb_ = Pb.next()
                P.op("act", lambda e: e.activation(out=pb_[:], in_=PS[ps_][:], func=AF.Exp, scale=0.125, bias=biasA[:, kc, qt:qt + 1]),
                     reads=[PSB[ps_], B_ac], writes=[bpb_])
                pend_pb[j] = (pb_, bpb_)

            def emit_PV(j, vh=vh, bvh=bvh):
                qt, kc = steps[j]
                po = 2 + qt % 2
                pd = 4 + qt % 2
                pb_, bpb_ = pend_pb.pop(j)
                P.op("pe", lambda e: e.matmul(PS[po][:], lhsT=vh[:, kc, :], rhs=pb_[:], start=(kc == 0), stop=(kc == 19)),
                     reads=[bvh, bpb_], writes=[PSB[po]])
                P.op("pe", lambda e: e.matmul(PS[pd][:], lhsT=ones_bf[:], rhs=pb_[:], start=(kc == 0), stop=(kc == 19)),
                     reads=[bpb_, B_const], writes=[PSB[pd]])

            nst = len(steps)
            emit_S(0)
            emit_S(1)
            for j in range(nst):
                if j + 2 < nst:
                    emit_S(j + 2)
                emit_exp(j)
                emit_PV(j)
                qt, kc = steps[j]
                if kc != 19:
                    continue
                qs_ = slice(qt * 256, (qt + 1) * 256)
                po = 2 + qt % 2
                pd = 4 + qt % 2
                if MKD < 4:
                    continue
                rd, brd = rden.next()
                P.op("dve", lambda e, rd=rd, pd=pd: e.reciprocal(out=rd[:], in_=PS[pd][:]), reads=[PSB[pd]], writes=[brd])
                on_, bon_ = on.next()
                P.op("dve", lambda e, on_=on_, po=po, rd=rd: e.tensor_tensor(out=on_[:], in0=PS[po][:], in1=rd[:], op=ALU.mult), reads=[PSB[po], brd], writes=[bon_])
                ao_, bao_ = ao.next()
                P.op("dve", lambda e, ao_=ao_, on_=on_: e.scalar_tensor_tensor(out=ao_[:], in0=on_[:, 256:512], scalar=neglam[:, 0:1], in1=on_[:, 0:256],
                                                                              op0=ALU.mult, op1=ALU.add), reads=[bon_, B_ac], writes=[bao_])
                sq_, bsq_ = sqa.next()
                P.op("act", lambda e, sq_=sq_, ao_=ao_: e.activation(out=sq_[:], in_=ao_[:], func=AF.Square), reads=[bao_], writes=[bsq_])
                P.op("pe", lambda e, sq_=sq_: e.matmul(PS[7][:, 0:256], lhsT=ones_bf[:], rhs=sq_[:], start=True, stop=True), reads=[bsq_, B_const], writes=[PSB[7]])
                r1_, br1_ = r1a.next()
                P.op("act", lambda e, r1_=r1_: e.activation(out=r1_[:], in_=PS[7][:, 0:256], func=AF.Sqrt, scale=1.0 / 128, bias=EPS), reads=[PSB[7]], writes=[br1_])
                r2_, br2_ = r2a.next()
                P.op("dve", lambda e, r1_=r1_, r2_=r2_: e.reciprocal(out=r2_[:], in_=r1_[:]), reads=[br1_], writes=[br2_])
                oa_, boa_ = oa.next()
                P.op("dve", lambda e, oa_=oa_, ao_=ao_, r2_=r2_: e.scalar_tensor_tensor(out=oa_[:], in0=ao_[:], scalar=sgl[:, 0:1], in1=r2_[:],
                                                                                       op0=ALU.mult, op1=ALU.mult), reads=[bao_, br2_, B_ac], writes=[boa_])
                P.op("sp", lambda e, oa_=oa_, h=h, qs_=qs_: e.dma_start(out=MO[h][:, qs_], in_=oa_[:]), reads=[boa_], dma=True)
        P.barrier()
        A.reset(m)


    L1d = {}

    def decl_L1():
        if L1d:
            return
        L1d["c_wF"] = din("c_wF", [20, 128, KC * 128])
        L1d["c_wT"] = din("c_wT", [2, 128, KC * 512])
        L1d["cosC"] = din("cosC", [128, NTOK])
        L1d["sinC"] = din("sinC", [128, NTOK])
        L1d["permC"] = din("permC", [128, 128])
        L1d["QC"] = dscr("QC", [4, 128, 4, NTOK], BF16)
        L1d["KCs"] = dscr("KCs", [4, 128, NTOK], BF16)
        L1d["VC"] = dscr("VC2", [4, 128, 16, 128])
        L1d["MO1"] = dscr("MO1", [16, 128, NTOK])
        L1d["o_ck"] = dout("o_ck", [NTOK, 512])
        L1d["o_cv"] = dout("o_cv", [NTOK, 512])

    def proj_L1(t):
        decl_L1()
        QC, KCs, VC = L1d["QC"], L1d["KCs"], L1d["VC"]
        begin_other()
        m = A.mark()
        cosT = A.alloc([128, TT], F32, "cos")
        sinT = A.alloc([128, TT], F32, "sin")
        permT = A.alloc([128, 128], F32, "perm")
        B_tab = P.buf("tab")
        P.op("sp", lambda e: e.dma_start(out=cosT[:], in_=L1d["cosC"][:, t * TT:(t + 1) * TT]), writes=[B_tab], dma=True)
        P.op("sp", lambda e: e.dma_start(out=sinT[:], in_=L1d["sinC"][:, t * TT:(t + 1) * TT]), writes=[B_tab], dma=True)
        P.op("sp", lambda e: e.dma_start(out=permT[:], in_=L1d["permC"]), writes=[B_tab], dma=True)
        R = {"qs": Rot([128, SUB], F32, 2, "qs"), "t1": Rot([128, SUB], F32, 2, "t1"), "t2": Rot([128, SUB], F32, 2, "t2"),
             "ob": Rot([128, SUB], BF16, 3, "ob"), "pbank": (2, 3), "pcnt": [0]}

        def consF(c, sub, pb, m0_):
            tok0 = sub * SUB
            g0 = t * TT + tok0
            if c < 16:
                dstf = lambda ob: (QC[c // 4][:, c % 4, g0:g0 + SUB], ob[:])
            else:
                dstf = lambda ob: (KCs[c - 16][:, g0:g0 + SUB], ob[:])
            rope_consumer(pb, tok0, cosT, sinT, permT, B_tab, dstf, R)

        linear_fmaj(lambda c: L1d["c_wF"][c], 20, 128, consF)
        stT = Rot([128, 512], F32, 3, "stT")
        stTb = Rot([128, 512], BF16, 3, "stTb")

        def consT(g, tb, pb):
            tg = t * (TT // 128) + tb
            r0 = tg * 128
            s_, bs_ = stT.next()
            cp("act", s_[:], PS[pb][:], [PSB[pb]], [bs_])
            dst = (L1d["o_ck"] if g == 0 else L1d["o_cv"])[r0:r0 + 128, :]
            out_ops.append(P.op("sp", lambda e: e.dma_start(out=dst, in_=s_[:]), reads=[bs_], dma=True))
            if g == 1:
                P.op("sp", lambda e: e.dma_start(out=VC[:, :, tg, :].rearrange("h p d -> p h d"),
                                                 in_=s_[:].rearrange("p (h d) -> p h d", d=128)), reads=[bs_], dma=True)

        linear_tmaj(lambda g: L1d["c_wT"][g], 2, consT)
        P.barrier()
        A.reset(m)

    def attn_L1():
        decl_L1()
        QC, KCs, VC, MO1 = L1d["QC"], L1d["KCs"], L1d["VC"], L1d["MO1"]
        ctx_ck = din("ctx_ck", [4, 512, 128])
        ctx_cv = din("ctx_cv", [4, 512, 128])
        maskC_in = din("maskC", [128, 16 * 3 * 128])
        ctxb_in = din("ctxbiasC", [128, 1])
        sink_in = din("c_sink", [1, 16])
        m = A.mark()
        maskE = A.alloc([128, 16, 3, 4, 128], BF16, "maskE")
        ctxb = A.alloc([128, 1], F32, "ctxb")
        sk1 = A.alloc([1, 32], F32, "sk1")
        sk = A.alloc([128, 16], F32, "sk")
        sinkT = A.alloc([128, 16, 128], F32, "sinkT")
        B_cc = P.buf("cconst")
        maskK = A.alloc([128, 16, 3, 128], BF16, "maskK")
        B_mk = P.buf("maskK")
        P.op("pool", lambda e: e.dma_start(out=maskK[:].rearrange("p a b c -> p (a b c)"), in_=maskC_in), writes=[B_mk], dma=True)
        for j in range(4):
            P.op("dve", lambda e, j=j: e.tensor_copy(out=maskE[:, :, :, j, :], in_=maskK[:]), reads=[B_mk], writes=[B_cc])
        P.op("sp", lambda e: e.dma_start(out=ctxb[:], in_=ctxb_in), writes=[B_cc], dma=True)
        P.op("sp", lambda e: e.dma_start(out=sk1[0:1, 0:16], in_=sink_in), writes=[B_cc], dma=True)
        P.op("act", lambda e: e.activation(out=sk1[0:1, 16:32], in_=sk1[0:1, 0:16], func=AF.Exp), reads=[B_cc], writes=[B_cc])
        P.op("pe", lambda e: e.matmul(PS[7][:, 0:16], lhsT=ones_f[0:1, :], rhs=sk1[0:1, 16:32], start=True, stop=True),
             reads=[B_cc, B_const], writes=[PSB[7]])
        cp("dve", sk[:], PS[7][:, 0:16], [PSB[7]], [B_cc])
        P.op("dve", lambda e: e.tensor_copy(out=sinkT[:], in_=sk[:].unsqueeze(2).broadcast_to([128, 16, 128])), reads=[B_cc], writes=[B_cc])
        QCg = Rot([128, 4, NTOK], BF16, 2, "QCg")
        KCg = Rot([128, 512 + NTOK], BF16, 2, "KCg")
        VCg = Rot([128, 20, 128], BF16, 2, "VCg")
        ckt = Rot([128, 4, 128], F32, 2, "ckt")
        Pb = Rot([128, 512], BF16, 5, "Pb")
        den = Rot([128, 512], F32, 2, "den")
        rd = Rot([128, 512], F32, 2, "rd")
        ob = Rot([128, 4, 128], F32, 3, "obC")
        scale = float(128 ** -0.5)
        scnt = 0
        for g in range(4):
            qg, bqg = QCg.next()
            kg, bkg = KCg.next()
            vg, bvg = VCg.next()
            ck, bck = ckt.next()
            P.op("sp", lambda e, qg=qg, g=g: e.dma_start(out=qg[:], in_=QC[g]), writes=[bqg], dma=True)
            P.op("sp", lambda e, kg=kg, g=g: e.dma_start(out=kg[:, 512:], in_=KCs[g]), writes=[bkg], dma=True)
            P.op("sp", lambda e, ck=ck, g=g: e.dma_start(out=ck[:], in_=ctx_ck[g].rearrange("(c p) d -> p c d", p=128)), writes=[bck], dma=True)
            P.op("pool", lambda e, vg=vg, g=g: e.dma_start(out=vg[:, 0:4, :], in_=ctx_cv[g].rearrange("(c p) d -> p c d", p=128)), writes=[bvg], dma=True)
            P.op("pool", lambda e, vg=vg, g=g: e.dma_start(out=vg[:, 4:20, :], in_=VC[g]), writes=[bvg], dma=True)
            for c in range(4):
                P.op("pe", lambda e, ck=ck, c=c: e.transpose(out=PS[7][:, c * 128:(c + 1) * 128], in_=ck[:, c, :], identity=ident[:]),
                     reads=[bck, B_const], writes=[PSB[7]])
            cp("dve", kg[:, 0:512], PS[7][:], [PSB[7]], [bkg])
            SB = (0, 1, 6)
            steps = []
            for qb in range(16):
                chunks = [(c, None) for c in range(4)] + [(4 + kb, kb - qb + 1) for kb in (qb - 1, qb, qb + 1) if 0 <= kb < 16]
                for idx, (kc, mi) in enumerate(chunks):
                    steps.append((qb, kc, mi, idx, len(chunks)))
            pend_pb = {}

            def emit_S(j, kg=kg, qg=qg, bkg=bkg, bqg=bqg):
                qb, kc, mi, idx, n = steps[j]
                ps_ = SB[j % 3]
                P.op("pe", lambda e: e.matmul(PS[ps_][:], lhsT=kg[:, kc * 128:(kc + 1) * 128], rhs=qg[:, :, qb * 128:(qb + 1) * 128], start=True, stop=True),
                     reads=[bkg, bqg], writes=[PSB[ps_]])

            def emit_exp(j):
                qb, kc, mi, idx, n = steps[j]
                ps_ = SB[j % 3]
                pb_, bpb_ = Pb.next()
                if kc < 4:
                    P.op("act", lambda e: e.activation(out=pb_[:], in_=PS[ps_][:], func=AF.Exp, scale=scale, bias=ctxb[:, 0:1]),
                         reads=[PSB[ps_], B_cc], writes=[bpb_])
                else:
                    P.op("act", lambda e: e.activation(out=pb_[:], in_=PS[ps_][:], func=AF.Exp, scale=scale),
                         reads=[PSB[ps_]], writes=[bpb_])
                    P.op("dve", lambda e: e.tensor_tensor(out=pb_[:], in0=pb_[:], in1=maskE[:, qb, mi].rearrange("p j q -> p (j q)"), op=ALU.mult),
                         reads=[bpb_, B_cc], writes=[bpb_])
                pend_pb[j] = (pb_, bpb_)

            def emit_PV(j, vg=vg, bvg=bvg):
                qb, kc, mi, idx, n = steps[j]
                po = 2 + qb % 2
                pd = 4 + qb % 2
                pb_, bpb_ = pend_pb.pop(j)
                P.op("pe", lambda e: e.matmul(PS[po][:], lhsT=vg[:, kc, :], rhs=pb_[:], start=(idx == 0), stop=(idx == n - 1)),
                     reads=[bvg, bpb_], writes=[PSB[po]])
                P.op("pe", lambda e: e.matmul(PS[pd][:], lhsT=ones_bf[:], rhs=pb_[:], start=(idx == 0), stop=(idx == n - 1)),
                     reads=[bpb_, B_const], writes=[PSB[pd]])

            nst = len(steps)
            emit_S(0)
            emit_S(1)
            for j in range(nst):
                if j + 2 < nst:
                    emit_S(j + 2)
                emit_exp(j)
                emit_PV(j)
                qb, kc, mi, idx, n = steps[j]
                if idx != n - 1:
                    continue
                po = 2 + qb % 2
                pd = 4 + qb % 2
                dn, bdn = den.next()
                P.op("dve", lambda e, dn=dn, pd=pd, g=g: e.tensor_tensor(out=dn[:], in0=PS[pd][:], in1=sinkT[:, g * 4:(g + 1) * 4, :].rearrange("p j q -> p (j q)"), op=ALU.add),
                     reads=[PSB[pd], B_cc], writes=[bdn])
                r_, br_ = rd.next()
                P.op("dve", lambda e, r_=r_, dn=dn: e.reciprocal(out=r_[:], in_=dn[:]), reads=[bdn], writes=[br_])
                o_, bo_ = ob.next()
                P.op("dve", lambda e, o_=o_, po=po, r_=r_: e.tensor_tensor(out=o_[:].rearrange("p j q -> p (j q)"), in0=PS[po][:], in1=r_[:], op=ALU.mult),
                     reads=[PSB[po], br_], writes=[bo_])
                P.op("sp", lambda e, o_=o_, g=g, qb=qb: e.dma_start(
                    out=MO1[g * 4:(g + 1) * 4, :, qb * 128:(qb + 1) * 128].rearrange("j p q -> p j q"), in_=o_[:]), reads=[bo_], dma=True)
        P.barrier()
        A.reset(m)

    def final_out(t):
        if "finalg_in" not in L1d:
            L1d["finalg_in"] = din("finalg_T", [128, KC])
            L1d["y_out"] = dout("y", [NTOK, D])
        finalg_T = L1d["finalg_in"]
        y_out = L1d["y_out"]
        begin_other()
        m = A.mark()
        fg = A.alloc([128, KC], F32, "fg")
        B_fg = P.buf("fg")
        P.op("sp", lambda e: e.dma_start(out=fg[:], in_=finalg_T), writes=[B_mod], dma=True)
        norm_mod(fg, None, out_f32=xT)
        yst = Rot([128, D], F32, 2, "yst")
        cnt = 0
        for tb in range(TT // 128):
            ys, bys = yst.next()
            for k4 in range(4):
                pb = cnt % 2
                cnt += 1
                for kk in range(4):
                    kc = k4 * 4 + kk
                    P.op("pe", lambda e, kc=kc, kk=kk, tb=tb, pb=pb: e.transpose(
                        out=PS[pb][:, kk * 128:(kk + 1) * 128], in_=xT[kc][:, tb * 128:(tb + 1) * 128], identity=ident[:]),
                        reads=[B_x[kc], B_const], writes=[PSB[pb]])
                cp("act" if k4 % 2 == 0 else "dve", ys[:, k4 * 512:(k4 + 1) * 512], PS[pb][:], [PSB[pb]], [bys])
            r0 = t * TT + tb * 128
            out_ops.append(P.op("sp", lambda e, ys=ys, r0=r0: e.dma_start(out=y_out[r0:r0 + 128, :], in_=ys[:]), reads=[bys], dma=True))
        P.barrier()
        A.reset(m)

    def mixer_out(t, MOsrc, wsrc, l, cast=False):
        begin_other()
        for kc in range(KC):
            P.op("pool" if cast else "sp", lambda e, kc=kc: e.dma_start(out=hT[kc][:], in_=MOsrc[kc][:, t * TT:(t + 1) * TT]), writes=[B_h[kc]], dma=True)
        hgv = hg(l, 1)

        def cons(c, sub, pb, m0_):
            P.op("dve", lambda e: e.scalar_tensor_tensor(
                out=xT[c][:, sub * SUB:(sub + 1) * SUB], in0=PS[pb][:], scalar=hgv[:, c:c + 1],
                in1=xT[c][:, sub * SUB:(sub + 1) * SUB], op0=ALU.mult, op1=ALU.add), reads=[PSB[pb], B_x[c], B_mod], writes=[B_x[c]])

        mm = linear_fmaj(wsrc, KC, 128, cons)
        P.barrier()
        A.reset(mm)

    def dump_dbg(t, dbg):
        ops = []
        for kc in range(KC):
            ops.append(P.op("sp", lambda e, kc=kc: e.dma_start(out=dbg[t, kc], in_=xT[kc][:]), reads=[B_x[kc]], dma=True))
        return ops

    last = []
    dbg = dout("dbg", [NT, KC, 128, TT]) if STAGE < 99 else None
    for t in range(NT):
        load_x_from_input(t)
        ffn(0, 0)
        if STAGE <= 1:
            last += dump_dbg(t, dbg)
            P.barrier()
            continue
        norm_mod(gs(0, 1), shiftT(0, 1))
        proj_L0(t)
        store_x(t, xs)
        P.barrier()
    if STAGE >= 3:
        A.reset(seg0)
        attn_L0()
        A.reset(seg_top)
        phase["last"] = "other"
        ab_wout = din("ab_wout", [KC, 128, KC * 128])
        for t in range(NT):
            load_x(t, xs)
            mixer_out(t, MO, lambda c: ab_wout[c], 0)
            if STAGE <= 3:
                last += dump_dbg(t, dbg)
                P.barrier()
                continue
            ffn(0, 1)
            if STAGE <= 4:
                last += dump_dbg(t, dbg)
                P.barrier()
                continue
            ffn(1, 0)
            if STAGE <= 5:
                last += dump_dbg(t, dbg)
                P.barrier()
                continue
            norm_mod(gs(1, 1), shiftT(1, 1))
            proj_L1(t)
            store_x(t, xs)
            P.barrier()
    if STAGE >= 6:
        A.reset(seg0)
        attn_L1()
        A.reset(seg_top)
        phase["last"] = "other"
        c_wout = din("c_wout", [KC, 128, KC * 128])
        for t in range(NT):
            load_x(t, xs)
            mixer_out(t, L1d["MO1"], lambda c: c_wout[c], 1, cast=True)
            if STAGE <= 6:
                last += dump_dbg(t, dbg)
                P.barrier()
                continue
            ffn(1, 1)
            if STAGE <= 7:
                last += dump_dbg(t, dbg)
                P.barrier()
                continue
            final_out(t)

    P.emit(final_wait_ops=last + out_ops)
    return nc, used_inputs


def _colchunks(W, cols, ncol):
    return np.ascontiguousarray(W[:, cols].reshape(KC, 128, ncol).transpose(1, 0, 2).reshape(128, KC * ncol))


def _rope_tables(head_dim, sample):
    n = NTOK
    if not sample:
        return np.ones((128, n), np.float32), np.zeros((128, n), np.float32)
    grid_w = 64
    t = np.arange(n)
    row = (t // grid_w).astype(np.float32)
    col = (t % grid_w).astype(np.float32)
    d_axis = head_dim // 2
    inv = (10000.0 ** (-np.arange(0, d_axis, 2, dtype=np.float32) / d_axis)).astype(np.float32)
    ang_r = row[:, None] * inv[None, :]
    ang_c = col[:, None] * inv[None, :]
    nf = d_axis // 2
    cos = np.zeros((128, n), np.float32)
    sin = np.zeros((128, n), np.float32)
    for r in range(128):
        loc = r % head_dim
        ang = ang_r if loc < d_axis else ang_c
        f = (loc % d_axis) % nf
        cos[r] = np.cos(ang[:, f])
        sin[r] = np.sin(ang[:, f])
    return cos, sin


def _perm_T(head_dim):
    d_axis = head_dim // 2
    nf = d_axis // 2
    Pm = np.zeros((128, 128), np.float32)
    for r in range(128):
        base = r - (r % d_axis)
        loc = r % d_axis
        if loc < nf:
            Pm[r, base + loc + nf] = -1.0
        else:
            Pm[r, base + loc - nf] = 1.0
    return np.ascontiguousarray(Pm.T)


def _prep_shared(inp):
    f32 = np.float32
    sh = {}
    for l in range(2):
        sh["ada_w%d" % l] = np.ascontiguousarray(
            inp["ada_w"][l].reshape(KC, 128, 36, 512).transpose(2, 1, 0, 3).reshape(36, 128, KC * 512))
    sh["ada_b"] = np.ascontiguousarray(inp["ada_b"].reshape(2, 1, 9 * D))
    sh["normg_T"] = np.ascontiguousarray(inp["norm_g"].reshape(6, KC, 128).transpose(2, 0, 1).reshape(128, 6 * KC))
    sh["finalg_T"] = np.ascontiguousarray(inp["final_g"].reshape(KC, 128).T)
    for l in range(2):
        for i in range(2):
            wi = inp["ffn_w_in"][l, i].reshape(KC, 128, 2, FC, 128)
            sh["w_in_%d_%d" % (l, i)] = np.ascontiguousarray(wi.transpose(3, 1, 0, 2, 4).reshape(FC, 128, KC * 256))
            wo = inp["ffn_w_out"][l, i].reshape(2, FH, 128, KC, 128)
            sh["w_out_%d_%d" % (l, i)] = np.ascontiguousarray(wo.transpose(0, 3, 2, 1, 4).reshape(2, KC, 128, FH * 128))
    sh["ident"] = np.eye(128, dtype=f32)
    W = inp["ab_w_in"][0]
    fch = ([np.arange(c * 128, (c + 1) * 128) for c in range(16)]
           + [np.arange(3072 + c * 128, 3072 + (c + 1) * 128) for c in range(8)]
           + [np.arange(5120 + c * 128, 5120 + (c + 1) * 128) for c in range(8)])
    sh["ab_wF"] = np.stack([_colchunks(W, c, 128) for c in fch])
    sh["ab_wlow"] = _colchunks(W, np.arange(6144, 6176), 32)
    tg = [np.arange(1024, 1536), np.arange(1536, 2048), np.arange(2048, 2560), np.arange(2560, 3072),
          np.arange(3584, 4096), np.arange(4096, 4608), np.arange(4608, 5120)]
    sh["ab_wT"] = np.stack([_colchunks(W, c, 512) for c in tg])
    sh["permA"] = _perm_T(64)
    sh["alpha17"] = np.ascontiguousarray(np.concatenate([inp["b_alpha_w"][0], inp["b_alpha_b"][0][:, None, :]], axis=1))
    tt = np.arange(128)
    Uf = np.where(tt[:, None] <= tt[None, :], -1.0 / 16, 0.0).astype(f32)
    Ub = np.where(tt[:, None] >= tt[None, :], -1.0 / 16, 0.0).astype(f32)
    sh["Utri"] = np.stack([Uf, Ub])
    mf = (tt[:, None] <= tt[None, :]).astype(f32)
    mb = (tt[:, None] >= tt[None, :]).astype(f32)
    sh["gmask"] = np.stack([mf, mb])
    sh["bnormg_T"] = np.ascontiguousarray(inp["b_norm_g"][0].reshape(2, 128).T)
    sh["a_lambda"] = np.ascontiguousarray(inp["a_lambda"][0].reshape(1, 256))
    sh["subg_T"] = np.ascontiguousarray(inp["a_subln_g"][0].reshape(128, 1))
    sh["ab_wout"] = np.stack([_colchunks(inp["ab_w_out"][0], np.arange(c * 128, (c + 1) * 128), 128) for c in range(KC)])
    Wc = inp["c_w_in"][0]
    sh["c_wF"] = np.stack([_colchunks(Wc, np.arange(c * 128, (c + 1) * 128), 128) for c in range(20)])
    sh["c_wT"] = np.stack([_colchunks(Wc, np.arange(2048, 2560), 512), _colchunks(Wc, np.arange(2560, 3072), 512)])
    sh["permC"] = _perm_T(128)
    sh["c_sink"] = np.ascontiguousarray(inp["c_sink"][0].reshape(1, 16))
    sh["c_wout"] = np.stack([_colchunks(inp["c_w_out"][0], np.arange(c * 128, (c + 1) * 128), 128) for c in range(KC)])
    return sh


def _per_core(inp, g):
    f32 = np.float32
    m = {}
    sample = g >= 2
    if not sample:
        m["x"] = np.ascontiguousarray(inp["x_prompt"][8 * g:8 * g + 8].reshape(NTOK, D))
        cond = inp["c_ctx"]
        m["ctx_ak"] = np.zeros((8, 512, 128), f32)
        m["ctx_av"] = np.zeros((8, 512, 128), f32)
        m["st_f"] = np.zeros((4, 128, 256), f32)
        m["st_b"] = np.zeros((4, 128, 256), f32)
        m["keep"] = np.zeros((128, 1), f32)
        bias = np.full((128, 20, 8), NEG, f32)
        for kc in range(4, 20):
            bias[:, kc, (kc - 4) // 2] = 0.0
        m["biasA"] = bias.reshape(128, 160)
    else:
        b = g - 2
        m["x"] = np.ascontiguousarray(inp["x_sample"][b])
        cond = inp["c"][b]
        m["ctx_ak"] = np.ascontiguousarray(inp["cache_a_k"][b, 0])
        m["ctx_av"] = np.ascontiguousarray(inp["cache_a_v"][b, 0])
        m["st_f"] = np.ascontiguousarray(inp["state_b_fwd"][b, 0])
        m["st_b"] = np.ascontiguousarray(inp["state_b_bwd"][b, 0])
        m["keep"] = np.ones((128, 1), f32)
        m["biasA"] = np.zeros((128, 160), f32)
    m["cond_T"] = np.ascontiguousarray(cond.reshape(KC, 128).T)
    m["cosA"], m["sinA"] = _rope_tables(64, sample)
    m["cosC"], m["sinC"] = _rope_tables(128, sample)
    k = np.arange(128)[:, None]
    q = np.arange(128)[None, :]
    mk = np.zeros((128, 16, 3, 128), f32)
    for qb in range(16):
        for mi in range(3):
            kb = qb - 1 + mi
            if not (0 <= kb < 16):
                continue
            if sample:
                mk[:, qb, mi, :] = (np.abs((qb * 128 + q) - (kb * 128 + k)) <= 128).astype(f32)
            else:
                mk[:, qb, mi, :] = 1.0 if (kb // 2 == qb // 2) else 0.0
    m["maskC"] = mk.reshape(128, 16 * 3 * 128)
    if sample:
        b = g - 2
        m["ctx_ck"] = np.ascontiguousarray(inp["cache_c_k"][b, 0])
        m["ctx_cv"] = np.ascontiguousarray(inp["cache_c_v"][b, 0])
        m["ctxbiasC"] = np.zeros((128, 1), f32)
    else:
        m["ctx_ck"] = np.zeros((4, 512, 128), f32)
        m["ctx_cv"] = np.zeros((4, 512, 128), f32)
        m["ctxbiasC"] = np.full((128, 1), NEG, f32)
    return m


def kernel(**inp):
    inp = {k: np.asarray(v) for k, v in inp.items()}
    groups = [s_ if s_ == "Z" else int(s_) for s_ in os.environ.get("MK_GROUPS", "0,Z,1,Z,2,Z,3,Z").split(",")]
    sh = _prep_shared(inp)
    nc, used = build_program()
    pcs = {}
    in_maps = []
    zero_map = None
    for g in groups:
        if g == "Z":
            if zero_map is None:
                full0 = dict(sh)
                full0.update(_per_core(inp, 0))
                zero_map = {k: np.zeros(full0[k].shape, full0[k].dtype) for k in used}
                zero_map["ident"] = sh["ident"]
            in_maps.append(zero_map)
            continue
        if g not in pcs:
            pcs[g] = _per_core(inp, g)
        full = dict(sh)
        full.update(pcs[g])
        in_maps.append({k: full[k] for k in used})
    res = run_bass_kernel_spmd(nc, in_maps, core_ids=list(range(len(groups))))
    R = res.results
    if STAGE < 99:
        return R
    gi = {g: groups.index(g) for g in range(4)}
    pr = [R[gi[0]], R[gi[1]]]
    y_prompt = np.concatenate([r["y"].reshape(8, 256, D) for r in pr], axis=0)
    y_sample = np.stack([R[gi[2]]["y"], R[gi[3]]["y"]], axis=0)

    def heads_out(name, nh, dh):
        a = [r[name].reshape(8, 256, nh, dh).transpose(0, 2, 1, 3) for r in pr]
        return np.ascontiguousarray(np.concatenate(a, axis=0)[:, None])

    new_a_k = heads_out("o_ak", 8, 128)
    new_a_v = heads_out("o_av", 8, 128)
    new_b_fwd = np.ascontiguousarray(np.concatenate([r["o_bf"] for r in pr], axis=0)[:, None])
    new_b_bwd = np.ascontiguousarray(np.concatenate([r["o_bb"] for r in pr], axis=0)[:, None])
    new_c_k = heads_out("o_ck", 4, 128)
    new_c_v = heads_out("o_cv", 4, 128)
    f32 = np.float32
    return tuple(np.asarray(a, dtype=f32) for a in (y_prompt, y_sample, new_a_k, new_a_v, new_b_fwd, new_b_bwd, new_c_k, new_c_v))
```
